# Optimizing a Trainium2 kernel written in Bass

```python
import math
import jax, jax.numpy as jnp
from jax import lax
import numpy as np

D_MODEL = 1024
BATCH = 8
SEQ = 8192
DEPTH = 1

N_MEM = 256
EPS = 1e-6
CONV_WIDTH = D_MODEL // 2
CONV_K = 3
SSM_WIDTH = D_MODEL // 2
SSM_GROUP = 16
SSM_GROUPS = SSM_WIDTH // SSM_GROUP
SSM_STATE = 64
SSM_CHUNK = 128
MEM_HEADS = 4
MEM_HEAD_DIM = 128
MEM_WIDTH = MEM_HEADS * MEM_HEAD_DIM
N_BRANCH = 3
IN_COLS = 3 * CONV_WIDTH + SSM_WIDTH + MEM_WIDTH + N_BRANCH * D_MODEL
FFN_HIDDEN = int(math.ceil(8 * D_MODEL / 3 / 256) * 256)

kernel_name = "hybrid_gated_conv_s5_memxattn_block"


def rms_norm(x, g):
    xf = x.astype(jnp.float32)
    y = xf * lax.rsqrt(jnp.mean(xf * xf, axis=-1, keepdims=True) + EPS)
    return (y * g.astype(jnp.float32)).astype(x.dtype)


def causal_dwconv(v, w):
    s = v.shape[1]
    vp = jnp.pad(v, ((0, 0), (CONV_K - 1, 0), (0, 0)))
    return w[0] * vp[:, 0:s] + w[1] * vp[:, 1:s + 1] + w[2] * vp[:, 2:s + 2]


def _ssm_combine(e1, e2):
    a1, b1 = e1
    a2, b2 = e2
    return a1 * a2, a2 * b1 + b2


def s5_scan(u, A_re, A_im, log_dt, B_re, B_im, C_re, C_im, D_skip):
    b, s, _ = u.shape
    n_chunks = s // SSM_CHUNK
    f32 = jnp.float32
    uf = u.astype(f32).reshape(b, n_chunks, SSM_CHUNK, SSM_GROUPS, SSM_GROUP)
    uf = uf.transpose(1, 0, 2, 3, 4)
    lam = lax.complex(A_re.astype(f32), A_im.astype(f32))
    dt = jnp.exp(log_dt.astype(f32))
    lam_bar = jnp.exp(lam * dt[:, None])
    Bc = lax.complex(B_re.astype(f32), B_im.astype(f32))
    B_bar = ((lam_bar - 1.0) / lam)[..., None] * Bc
    Cc = lax.complex(C_re.astype(f32), C_im.astype(f32))
    Dv = D_skip.astype(f32).reshape(SSM_GROUPS, SSM_GROUP)

    def chunk_step(h_prev, u_c):
        bu = jnp.einsum('gph,blgh->blgp', B_bar, u_c.astype(jnp.complex64))
        a = jnp.broadcast_to(lam_bar, bu.shape)
        a_cum, h_loc = lax.associative_scan(_ssm_combine, (a, bu), axis=1)
        h = h_loc + a_cum * h_prev[:, None]
        y = jnp.einsum('ghp,blgp->blgh', Cc, h).real + Dv * u_c
        return h[:, -1], y

    h0 = jnp.zeros((b, SSM_GROUPS, SSM_STATE), jnp.complex64)
    _, ys = lax.scan(chunk_step, h0, uf)
    return ys.transpose(1, 0, 2, 3, 4).reshape(b, s, SSM_WIDTH).astype(u.dtype)


def mem_cross_attention(q, mem_n, w_k, w_v):
    b, s, _ = q.shape
    qh = q.reshape(b, s, MEM_HEADS, MEM_HEAD_DIM)
    k = (mem_n @ w_k).reshape(b, N_MEM, MEM_HEADS, MEM_HEAD_DIM)
    v = (mem_n @ w_v).reshape(b, N_MEM, MEM_HEADS, MEM_HEAD_DIM)
    scores = jnp.einsum('bshd,bmhd->bhsm', qh.astype(jnp.float32), k.astype(jnp.float32))
    p = jax.nn.softmax(scores * (MEM_HEAD_DIM ** -0.5), axis=-1).astype(q.dtype)
    o = jnp.einsum('bhsm,bmhd->bshd', p, v)
    return o.reshape(b, s, MEM_WIDTH)


def setup_inputs(seed: int = 0) -> dict:
    key = jax.random.key(seed)
    ks = jax.random.split(key, 32)
    f32 = jnp.float32
    L = DEPTH

    def nrm(k, shape, fan_in):
        return jax.random.normal(k, shape, f32) * (fan_in ** -0.5)

    def gain(k, shape):
        return 1.0 + 0.02 * jax.random.normal(k, shape, f32)

    n_idx = jnp.arange(SSM_STATE, dtype=f32)
    A_re = -0.5 * (1.0 + 0.01 * jax.random.normal(ks[4], (L, SSM_GROUPS, SSM_STATE), f32))
    A_im = math.pi * n_idx * (1.0 + 0.01 * jax.random.normal(ks[5], (L, SSM_GROUPS, SSM_STATE), f32))
    log_dt = jax.random.uniform(ks[6], (L, SSM_GROUPS), f32, math.log(1e-3), math.log(1e-1))
    return {
        "x": jax.random.normal(ks[0], (BATCH, SEQ, D_MODEL), f32),
        "mem": jax.random.normal(ks[1], (BATCH, N_MEM, D_MODEL), f32),
        "norm1_g": gain(ks[2], (L, D_MODEL)),
        "w_in": nrm(ks[3], (L, D_MODEL, IN_COLS), D_MODEL),
        "conv_w": nrm(ks[7], (L, CONV_K, CONV_WIDTH), CONV_K),
        "conv_proj": nrm(ks[8], (L, CONV_WIDTH, D_MODEL), CONV_WIDTH),
        "ssm_A_re": A_re,
        "ssm_A_im": A_im,
        "ssm_log_dt": log_dt,
        "ssm_B_re": nrm(ks[9], (L, SSM_GROUPS, SSM_STATE, SSM_GROUP), 2 * SSM_GROUP),
        "ssm_B_im": nrm(ks[10], (L, SSM_GROUPS, SSM_STATE, SSM_GROUP), 2 * SSM_GROUP),
        "ssm_C_re": nrm(ks[11], (L, SSM_GROUPS, SSM_GROUP, SSM_STATE), SSM_STATE),
        "ssm_C_im": nrm(ks[12], (L, SSM_GROUPS, SSM_GROUP, SSM_STATE), SSM_STATE),
        "ssm_D": jax.random.normal(ks[13], (L, SSM_WIDTH), f32),
        "ssm_glu_w": nrm(ks[14], (L, SSM_WIDTH, SSM_WIDTH), SSM_WIDTH),
        "ssm_glu_b": 0.01 * jax.random.normal(ks[15], (L, SSM_WIDTH), f32),
        "ssm_proj": nrm(ks[16], (L, SSM_WIDTH, D_MODEL), SSM_WIDTH),
        "mem_norm_g": gain(ks[17], (L, D_MODEL)),
        "attn_wk": nrm(ks[18], (L, D_MODEL, MEM_WIDTH), D_MODEL),
        "attn_wv": nrm(ks[19], (L, D_MODEL, MEM_WIDTH), D_MODEL),
        "attn_proj": nrm(ks[20], (L, MEM_WIDTH, D_MODEL), MEM_WIDTH),
        "w_o": nrm(ks[21], (L, D_MODEL, D_MODEL), D_MODEL),
        "norm2_g": gain(ks[22], (L, D_MODEL)),
        "ffn_w_gate": nrm(ks[23], (L, D_MODEL, FFN_HIDDEN), D_MODEL),
        "ffn_w_up": nrm(ks[24], (L, D_MODEL, FFN_HIDDEN), D_MODEL),
        "ffn_w_down": nrm(ks[25], (L, FFN_HIDDEN, D_MODEL), FFN_HIDDEN),
        "final_norm_g": gain(ks[26], (D_MODEL,)),
    }


def reference(x, mem, norm1_g, w_in, conv_w, conv_proj, ssm_A_re, ssm_A_im, ssm_log_dt,
              ssm_B_re, ssm_B_im, ssm_C_re, ssm_C_im, ssm_D, ssm_glu_w, ssm_glu_b, ssm_proj,
              mem_norm_g, attn_wk, attn_wv, attn_proj, w_o, norm2_g, ffn_w_gate, ffn_w_up,
              ffn_w_down, final_norm_g):
    split_at = [CONV_WIDTH, 2 * CONV_WIDTH, 3 * CONV_WIDTH,
                3 * CONV_WIDTH + SSM_WIDTH,
                3 * CONV_WIDTH + SSM_WIDTH + MEM_WIDTH,
                3 * CONV_WIDTH + SSM_WIDTH + MEM_WIDTH + D_MODEL,
                3 * CONV_WIDTH + SSM_WIDTH + MEM_WIDTH + 2 * D_MODEL]
    for l in range(DEPTH):
        h = rms_norm(x, norm1_g[l])
        proj = h @ w_in[l]
        b_gate, c_gate, v, u, q, g_c, g_s, g_m = jnp.split(proj, split_at, axis=-1)

        y_conv = (b_gate * causal_dwconv(c_gate * v, conv_w[l])) @ conv_proj[l]

        y_s = jax.nn.gelu(s5_scan(u, ssm_A_re[l], ssm_A_im[l], ssm_log_dt[l], ssm_B_re[l],
                                  ssm_B_im[l], ssm_C_re[l], ssm_C_im[l], ssm_D[l]))
        y_s = y_s * jax.nn.sigmoid(y_s @ ssm_glu_w[l] + ssm_glu_b[l])
        y_ssm = y_s @ ssm_proj[l]

        mem_n = rms_norm(mem, mem_norm_g[l])
        y_mem = mem_cross_attention(q, mem_n, attn_wk[l], attn_wv[l]) @ attn_proj[l]

        merged = (jax.nn.sigmoid(g_c) * y_conv + jax.nn.sigmoid(g_s) * y_ssm
                  + jax.nn.sigmoid(g_m) * y_mem)
        x = x + merged @ w_o[l]

        h2 = rms_norm(x, norm2_g[l])
        x = x + (jax.nn.silu(h2 @ ffn_w_gate[l]) * (h2 @ ffn_w_up[l])) @ ffn_w_down[l]
    return rms_norm(x, final_norm_g)
```

```python
import math
from contextlib import ExitStack

import numpy as np
import concourse.bass as bass
import concourse.mybir as mybir
from concourse.bass_utils import run_bass_kernel_spmd

F32 = mybir.dt.float32
BF16 = mybir.dt.bfloat16
I32 = mybir.dt.int32
AF = mybir.ActivationFunctionType
ALU = mybir.AluOpType
AX = mybir.AxisListType

D = 1024
NB = 512
NCH = NB // 8
NSLOT = 4
SLOT_E = 4096
EPS = 1e-6
TWO_PI = 2.0 * math.pi


class Sem:
    def __init__(self, h, name):
        self.h = h
        self.name = name
        self.count = 0


class Buf:
    def __init__(self, name, excl=False):
        self.name = name
        self.w = None
        self.r = {}
        self.excl = excl


def _flat(xs):
    out = []
    for x in xs:
        if isinstance(x, (list, tuple)):
            out.extend(x)
        else:
            out.append(x)
    return out


class Eng:
    def __init__(self, name, sem, self_sync):
        self.name = name
        self.sem = sem
        self.ops = []
        self.seen = {}
        self.self_sync = self_sync


class Prog:
    def __init__(self, nc, stack):
        self.nc = nc
        self.stack = stack
        self.engs = {}
        self.sems = []
        self.stopped = False
        for name, ss in (("pe", False), ("act", True), ("dve", True), ("pool", True), ("sp", True)):
            self.engs[name] = Eng(name, self.new_sem("e_" + name), ss)

    def new_sem(self, name):
        h = self.stack.enter_context(self.nc.semaphore(name))
        s = Sem(h, name)
        self.sems.append(s)
        return s

    def barrier(self):
        if self.stopped:
            return
        for e in self.engs.values():
            for s in self.sems:
                if s is e.sem or s.count == 0:
                    continue
                if e.seen.get(s, 0) >= s.count:
                    continue
                e.seen[s] = s.count
                e.ops.append((lambda h, s=s, v=s.count: h.wait_ge(s.h, v)))

    def final_wait(self):
        self.stopped = False
        self.barrier()

    def op(self, eng, fn, reads=(), writes=(), dma=None):
        if self.stopped:
            return None
        e = self.engs[eng]
        reads = _flat(reads)
        writes = _flat(writes)
        deps = {}

        def add(ev):
            s, v = ev
            if deps.get(s, 0) < v:
                deps[s] = v
        for b in reads:
            if b.w is not None:
                add(b.w)
            if b.excl:
                for s, v in b.r.items():
                    if s is not e.sem:
                        add((s, v))
        for b in writes:
            if b.w is not None:
                add(b.w)
            for s, v in b.r.items():
                add((s, v))
        for s, v in deps.items():
            if s is e.sem and not e.self_sync:
                continue
            if e.seen.get(s, 0) >= v:
                continue
            e.seen[s] = v
            e.ops.append((lambda h, s=s, v=v: h.wait_ge(s.h, v)))
        if dma is not None:
            dma.count += 16
            ev = (dma, dma.count)
            e.ops.append((lambda h, fn=fn, s=dma: fn(h).then_inc(s.h, 16)))
        else:
            e.sem.count += 1
            ev = (e.sem, e.sem.count)
            e.ops.append((lambda h, fn=fn, s=e.sem: fn(h).then_inc(s.h, 1)))
        for b in writes:
            b.w = ev
            b.r = {}
        for b in reads:
            if b in writes:
                continue
            s, v = ev
            if b.r.get(s, 0) < v:
                b.r[s] = v
        return ev

    def wait_all(self, eng, bufs):
        e = self.engs[eng]
        for b in bufs:
            evs = []
            if b.w is not None:
                evs.append(b.w)
            for s, v in evs:
                if e.seen.get(s, 0) < v:
                    e.seen[s] = v
                    e.ops.append((lambda h, s=s, v=v: h.wait_ge(s.h, v)))

    def mm(self, out, lhsT, rhs, start=True, stop=True, extra_r=()):
        o, l, r = out[0], lhsT[0], rhs[0]
        self.op("pe", lambda h: h.matmul(o, lhsT=l, rhs=r, start=start, stop=stop),
                reads=[lhsT[1], rhs[1]] + list(extra_r), writes=[out[1]])

    def tr(self, out, in_, ident):
        o, i, d = out[0], in_[0], ident[0]
        self.op("pe", lambda h: h.transpose(o, i, d), reads=[in_[1], ident[1]], writes=[out[1]])

    def act(self, out, in_, func, scale=None, bias=None, accum=None):
        o, i = out[0], in_[0]
        kw = {}
        reads = [in_[1]]
        writes = [out[1]]
        if scale is not None:
            if isinstance(scale, tuple):
                kw["scale"] = scale[0]
                reads.append(scale[1])
            else:
                kw["scale"] = float(scale)
        if bias is not None:
            if isinstance(bias, tuple):
                kw["bias"] = bias[0]
                reads.append(bias[1])
            else:
                kw["bias"] = float(bias)
        if accum is not None:
            kw["accum_out"] = accum[0]
            writes.append(accum[1])
        self.op("act", lambda h: h.activation(out=o, in_=i, func=func, **kw), reads=reads, writes=writes)

    def tt(self, eng, out, in0, in1, op):
        o, a, b = out[0], in0[0], in1[0]
        self.op(eng, lambda h: h.tensor_tensor(out=o, in0=a, in1=b, op=op),
                reads=[in0[1], in1[1]], writes=[out[1]])

    def ts(self, eng, out, in0, s1, op0, s2=None, op1=None):
        o, a = out[0], in0[0]
        reads = [in0[1]]
        if isinstance(s1, tuple):
            reads.append(s1[1])
            s1 = s1[0]
        else:
            s1 = float(s1)
        if isinstance(s2, tuple):
            reads.append(s2[1])
            s2 = s2[0]
        elif s2 is not None:
            s2 = float(s2)
        if op1 is None:
            self.op(eng, lambda h: h.tensor_scalar(out=o, in0=a, scalar1=s1, scalar2=None, op0=op0),
                    reads=reads, writes=[out[1]])
        else:
            self.op(eng, lambda h: h.tensor_scalar(out=o, in0=a, scalar1=s1, scalar2=s2, op0=op0, op1=op1),
                    reads=reads, writes=[out[1]])

    def stt(self, out, in0, scalar, in1, op0, op1):
        o, a, b = out[0], in0[0], in1[0]
        reads = [in0[1], in1[1]]
        if isinstance(scalar, tuple):
            reads.append(scalar[1])
            sc = scalar[0]
        else:
            sc = float(scalar)
        self.op("dve", lambda h: h.scalar_tensor_tensor(out=o, in0=a, scalar=sc, in1=b, op0=op0, op1=op1),
                reads=reads, writes=[out[1]])

    def copy(self, eng, out, in_):
        o, i = out[0], in_[0]
        if eng == "act":
            self.op("act", lambda h: h.activation(out=o, in_=i, func=AF.Copy), reads=[in_[1]], writes=[out[1]])
        else:
            self.op(eng, lambda h: h.tensor_copy(out=o, in_=i), reads=[in_[1]], writes=[out[1]])

    def memset(self, eng, out, val):
        o = out[0]
        self.op(eng, lambda h: h.memset(o, val), writes=[out[1]])

    def recip(self, out, in_):
        o, i = out[0], in_[0]
        self.op("dve", lambda h: h.reciprocal(out=o, in_=i), reads=[in_[1]], writes=[out[1]])

    def dma(self, eng, out, in_, sem, slow=False):
        o, i = out[0], in_[0]
        if slow:
            fn = lambda h: h.dma_start(out=o, in_=i, allow_slow_non_contiguous=True)
        else:
            fn = lambda h: h.dma_start(out=o, in_=i)
        self.op(eng, fn, reads=[in_[1]], writes=[out[1]], dma=sem)

    def emit(self):
        nc = self.nc
        with nc.Block() as block:
            @block.sync
            def _(h):
                for f in self.engs["sp"].ops:
                    f(h)

            @block.tensor
            def _(h):
                for f in self.engs["pe"].ops:
                    f(h)

            @block.scalar
            def _(h):
                for f in self.engs["act"].ops:
                    f(h)

            @block.vector
            def _(h):
                for f in self.engs["dve"].ops:
                    f(h)

            @block.gpsimd
            def _(h):
                for f in self.engs["pool"].ops:
                    f(h)


class _Stop(Exception):
    pass


def build(T, debug=(), stop=None):
    NBLK = T // NB
    nc = bass.Bass("TRN2", target_bir_lowering=False)

    def din(name, shape):
        return nc.dram_tensor(name, shape, F32, kind="ExternalInput").ap()

    x = din("x", [T, D])
    mem = din("mem", [256, D])
    norm1_g = din("norm1_g", [D])
    w_in = din("w_in", [D, 5632])
    conv_w = din("conv_w", [3, 512])
    conv_proj = din("conv_proj", [512, D])
    A_re = din("ssm_A_re", [32, 64])
    A_im = din("ssm_A_im", [32, 64])
    log_dt = din("ssm_log_dt", [1, 32])
    B_re = din("ssm_B_re", [32, 64, 16])
    B_im = din("ssm_B_im", [32, 64, 16])
    C_re = din("ssm_C_re", [512, 64])
    C_im = din("ssm_C_im", [512, 64])
    ssm_D = din("ssm_D", [512])
    glu_w = din("ssm_glu_w", [512, 512])
    glu_b = din("ssm_glu_b", [512])
    ssm_proj = din("ssm_proj", [512, D])
    mem_norm_g = din("mem_norm_g", [D])
    attn_wk = din("attn_wk", [D, 512])
    attn_wv = din("attn_wv", [D, 512])
    attn_proj = din("attn_proj", [512, D])
    w_o = din("w_o", [D, D])
    norm2_g = din("norm2_g", [D])
    ffn_g = din("ffn_w_gate", [D, 2816])
    ffn_u = din("ffn_w_up", [D, 2816])
    ffn_d = din("ffn_w_down", [2816, D])
    final_g = din("final_norm_g", [1, D])
    out = nc.dram_tensor("out", [T, D], F32, kind="ExternalOutput").ap()

    NCHUNK = 34
    wscr = nc.dram_tensor("wscr", [NCHUNK, 128, SLOT_E], BF16, kind="Internal").ap()
    dbg_outs = {}

    with ExitStack() as st:
      P = Prog(nc, st)
      try:
        def chk(name):
            if stop == name:
                P.stopped = True

        def sb(name, shape, dt, stack=None):
            return (stack or st).enter_context(nc.sbuf_tensor(name, shape, dt))

        DRAM = Buf("dram_in")
        OUTB = Buf("dram_out")

        def dbg(name, ap, buf, shape):
            if name not in debug:
                return
            o = nc.dram_tensor("dbg_" + name, list(shape), ap.dtype, kind="ExternalOutput").ap()
            dbg_outs[name] = o
            s = P.new_sem("dbg_" + name)
            P.dma("sp", (o, OUTB), (ap, buf), s)

        pbank = []
        for i in range(8):
            t = st.enter_context(nc.psum_tensor("pb%d" % i, [128, 512], F32))
            pbank.append((t, Buf("pb%d" % i, excl=True)))
        bank_rr = [0]

        def bank():
            i = bank_rr[0]
            bank_rr[0] = (i + 1) % 8
            return pbank[i]

        cbuf = Buf("consts")
        csem = P.new_sem("csem")
        identf = sb("identf", [128, 128], F32)
        ident = sb("ident", [128, 128], BF16)
        ones_bf = sb("ones_bf", [128, 128], BF16)
        trix2 = sb("trix2", [128, 128], F32)
        sel2 = sb("sel2", [66, 128], F32)
        ones2 = sb("ones2", [128, 32], F32)
        tmask = sb("tmask", [128, 128], F32)
        g1b = sb("g1b", [128, D], F32)
        g2b = sb("g2b", [128, D], F32)
        convw = sb("convw", [128, 3, 4], F32)
        glub = sb("glub", [128, 4], F32)
        gF = sb("gF", [128, D], F32)
        epscol = sb("epscol", [128, 1], F32)
        ss = sb("ss", [128, 4], F32); ss_b = [Buf("ss%d" % i) for i in range(4)]
        rstd = sb("rstd", [128, 4], F32); rstd_b = [Buf("rstd%d" % i) for i in range(4)]
        CB = lambda ap: (ap, cbuf)

        def asel(ap, pattern, op, base, cm):
            P.op("pool", lambda h: h.affine_select(out=ap, in_=ap, pattern=pattern, compare_op=op, fill=0.0,
                                                   base=base, channel_multiplier=cm), reads=[cbuf], writes=[cbuf])

        P.memset("pool", CB(identf[:]), 0.0)
        P.op("pool", lambda h: h.affine_select(out=identf[:], in_=identf[:], pattern=[[-1, 128]],
                                               compare_op=ALU.not_equal, fill=1.0, base=0, channel_multiplier=1),
             reads=[cbuf], writes=[cbuf])
        P.copy("dve", CB(ident[:]), CB(identf[:]))
        P.memset("dve", CB(ones_bf[:]), 1.0)
        P.memset("dve", CB(epscol[:]), EPS)
        P.memset("pool", CB(trix2[:]), 1.0)
        asel(trix2[:], [[1, 128]], ALU.is_ge, -1, -1)
        P.memset("pool", CB(trix2[0:64, 64:128]), 0.0)
        P.memset("pool", CB(sel2[:]), 1.0)
        for base in (0, 32, 64):
            asel(sel2[base:base + 2, :], [[1, 128]], ALU.is_ge, 0, -64)
            asel(sel2[base:base + 2, :], [[-1, 128]], ALU.is_ge, 63, 64)
        P.memset("pool", CB(ones2[:]), 0.0)
        P.memset("pool", CB(ones2[:, 0:2]), 1.0)
        asel(ones2[:, 0:2], [[-64, 2]], ALU.is_ge, 0, 1)
        asel(ones2[:, 0:2], [[64, 2]], ALU.is_ge, 63, -1)
        trix2_bf = sb("trix2_bf", [128, 128], BF16)
        sel2_bf = sb("sel2_bf", [66, 128], BF16)
        ones2_bf = sb("ones2_bf", [128, 32], BF16)
        P.copy("dve", CB(trix2_bf[:]), CB(trix2[:]))
        P.copy("dve", CB(sel2_bf[:]), CB(sel2[:]))
        P.copy("dve", CB(ones2_bf[:]), CB(ones2[:]))
        P.memset("pool", CB(tmask[:]), 1.0)
        asel(tmask[:].rearrange("p (t o) -> p t o", t=8), [[16, 8], [0, 16]], ALU.is_ge, 15, -1)
        P.dma("sp", CB(g1b[:]), (norm1_g.unsqueeze(0).partition_broadcast(128), DRAM), csem)
        P.dma("sp", CB(g2b[:]), (norm2_g.unsqueeze(0).partition_broadcast(128), DRAM), csem)
        for k in range(3):
            P.dma("sp", CB(convw[:, k, :]), (conv_w[k].rearrange("(ft p) -> p ft", p=128), DRAM), csem, slow=True)
        P.dma("sp", CB(glub[:]), (glu_b.rearrange("(ft p) -> p ft", p=128), DRAM), csem, slow=True)
        P.dma("sp", CB(gF[:]), (final_g.partition_broadcast(128), DRAM), csem)

        chk('consts')
        slots = []
        for i in range(NSLOT):
            t = sb("wslot%d" % i, [128, SLOT_E], BF16)
            slots.append((t, Buf("wslot%d" % i), P.new_sem("wld%d" % i), P.new_sem("wst%d" % i)))
        scr_buf = [Buf("scr%d" % i) for i in range(NCHUNK)]

        def wsrc(w, p=128):
            return w.rearrange("(kt p) n -> p kt n", p=p)

        chunk_parts = {}
        for j in range(11):
            chunk_parts["win%d" % j] = [(lambda s: s[:, 0:4096].rearrange("p (kt n) -> p kt n", kt=8),
                                         wsrc(w_in)[:, :, j * 512:(j + 1) * 512])]
        chunk_parts["cp"] = [(lambda s: s[:, 0:4096].rearrange("p (kt n) -> p kt n", kt=4), wsrc(conv_proj))]
        chunk_parts["sp"] = [(lambda s: s[:, 0:4096].rearrange("p (kt n) -> p kt n", kt=4), wsrc(ssm_proj))]
        chunk_parts["ap"] = [(lambda s: s[:, 0:4096].rearrange("p (kt n) -> p kt n", kt=4), wsrc(attn_proj))]
        chunk_parts["glu"] = [(lambda s: s[:, 0:2048].rearrange("p (kt n) -> p kt n", kt=4), wsrc(glu_w))]
        for hh in range(2):
            chunk_parts["wo%d" % hh] = [(lambda s: s[:, 0:4096].rearrange("p (kt n) -> p kt n", kt=8),
                                         wsrc(w_o)[:, :, hh * 512:(hh + 1) * 512])]
        for j in range(11):
            chunk_parts["ffn%d" % j] = [
                (lambda s: s[:, 0:2048].rearrange("p (kt n) -> p kt n", kt=8), wsrc(ffn_g)[:, :, j * 256:(j + 1) * 256]),
                (lambda s: s[:, 2048:4096].rearrange("p (kt n) -> p kt n", kt=8), wsrc(ffn_u)[:, :, j * 256:(j + 1) * 256]),
            ]
        for r in range(6):
            nk = 4 if r < 5 else 2
            chunk_parts["dn%d" % r] = [(lambda s, nk=nk: s[:, 0:nk * 1024].rearrange("p (kt n) -> p kt n", kt=nk),
                                        wsrc(ffn_d)[:, 4 * r:4 * r + nk, :])]
        chunk_ids = list(chunk_parts.keys())
        chunk_idx = {k: i for i, k in enumerate(chunk_ids)}
        assert len(chunk_ids) == NCHUNK

        block_sched = (["win3", "win4", "win1", "win2", "win0", "glu", "cp", "win5", "win6",
                        "sp", "win7", "win8", "ap", "win9", "win10", "wo0", "wo1"]
                       + ["ffn%d" % j for j in range(11)] + ["dn%d" % r for r in range(6)])
        assert len(block_sched) == NCHUNK
        sched = [(b, c) for b in range(NBLK) for c in block_sched]
        wstate = {"issued": 0, "use": 0}

        def w_issue(i):
            b, c = sched[i]
            t, buf, ld, stsem = slots[i % NSLOT]
            ci = chunk_idx[c]
            if b == 0:
                for dstf, src in chunk_parts[c]:
                    P.dma("pool", (dstf(t), buf), (src, DRAM), ld)
                if NBLK > 1:
                    P.dma("sp", (wscr[ci], scr_buf[ci]), (t[:, :], buf), stsem)
            else:
                P.dma("sp", (t[:, :], buf), (wscr[ci], scr_buf[ci]), ld)

        wdone = [False] * len(sched)
        wcur = {}

        def w_pump(limit):
            while wstate["issued"] < min(len(sched), limit):
                n = wstate["issued"]
                if n >= NSLOT and not wdone[n - NSLOT]:
                    break
                w_issue(n)
                wstate["issued"] += 1

        def w_get(c):
            i = wstate["use"]
            assert sched[i][1] == c, (sched[i], c)
            w_pump(len(sched))
            assert wstate["issued"] > i, ("weight slot plan deadlock", i, c)
            wstate["use"] = i + 1
            wcur[c] = i
            t, buf, _, _ = slots[i % NSLOT]
            return t, buf

        def w_rel(c):
            i = wcur.pop(c)
            wdone[i] = True
            w_pump(len(sched))

        xbuf = []
        for i in range(2):
            t = sb("x_tm%d" % i, [128, 4, D], F32)
            xbuf.append((t, Buf("x_tm%d" % i), P.new_sem("xld%d" % i), P.new_sem("xst%d" % i)))
        hT = sb("hT", [128, 8, NB], BF16); hT_tt = [Buf("hT%d" % i) for i in range(4)]; hT_b = tuple(hT_tt)
        conv_preT = sb("conv_preT", [128, 4, NB], BF16); conv_pre_b = Buf("conv_preT")
        ys2T = sb("ys2T", [128, 4, NB], BF16); ys2_b = Buf("ys2T")
        oT = sb("oT", [128, 4, NB], BF16); oT_b = Buf("oT")
        cv = sb("cv", [128, 4, NB + 2], F32); cv_b = [Buf("cv%d" % i) for i in range(4)]
        tmpA = [(sb("tmpA%d" % i, [128, NB], F32), Buf("tmpA%d" % i)) for i in range(2)]
        LTT = sb("LTT", [128, 32, 256], BF16); LTT_b = Buf("LTT")
        VV = sb("VV", [128, 32, 128], BF16); VV_b = Buf("VV")
        S2re = sb("S2re", [128, 1024], F32); S2im = sb("S2im", [128, 1024], F32)
        U2re = sb("U2re", [128, 1024], F32); U2im = sb("U2im", [128, 1024], F32)
        tab_b = Buf("tables")
        HstA = sb("HstA", [66, 512], F32); HstB = sb("HstB", [2, 512], F32); Hst_b = [Buf("Hst%d" % q) for q in range(4)]
        HbfA = sb("HbfA", [66, 512], BF16); HbfB = sb("HbfB", [2, 512], BF16)
        U64A = sb("U64A", [66, 2, 256], F32); U64B = sb("U64B", [2, 2, 256], F32)
        kT = sb("kT", [128, 4, 256], BF16); kT_b = Buf("kT")
        Vm = sb("Vm", [128, 2, 512], BF16); Vm_b = Buf("Vm")

        def hst_rows(q):
            if q < 3:
                return (HstA[32 * q:32 * q + 2, :], U64A[32 * q:32 * q + 2, :, :], sel2_bf[32 * q:32 * q + 2, :],
                        ident[32 * q:32 * q + 2, 32 * q:32 * q + 32], 32 * q, HbfA[32 * q:32 * q + 2, :])
            return HstB[0:2, :], U64B[0:2, :, :], sel2_bf[0:2, :], ident[0:2, 0:32], 0, HbfB[0:2, :]

        def sin_turns(out, turns, tmp_i, tmp_f):
            P.copy("dve", tmp_i, turns)
            P.copy("dve", tmp_f, tmp_i)
            P.tt("dve", turns, turns, tmp_f, ALU.subtract)
            P.ts("dve", tmp_f, turns, 0.5, ALU.is_gt)
            P.tt("dve", turns, turns, tmp_f, ALU.subtract)
            P.ts("dve", tmp_f, turns, -0.5, ALU.is_lt)
            P.tt("dve", turns, turns, tmp_f, ALU.add)
            P.act(out, turns, AF.Sin, scale=TWO_PI)

        def norm_act(xt, xb, htm, htm_b, tt):
            P.act((htm[:, tt, :], htm_b), (xt[:, tt, :], xb), AF.Square, accum=(ss[:, tt:tt + 1], ss_b[tt]))
            P.act((rstd[:, tt:tt + 1], rstd_b[tt]), (ss[:, tt:tt + 1], ss_b[tt]), AF.Sqrt, scale=1.0 / D, bias=(epscol[:, 0:1], cbuf))

        def norm_dve(xt, xb, gb, htm, htm_b, tt):
            P.recip((rstd[:, tt:tt + 1], rstd_b[tt]), (rstd[:, tt:tt + 1], rstd_b[tt]))
            P.stt((htm[:, tt, :], htm_b), (xt[:, tt, :], xb), (rstd[:, tt:tt + 1], rstd_b[tt]), (gb[:], cbuf), ALU.mult, ALU.mult)

        def norm_stats(xt, xb, gb, htm, htm_b, tt):
            norm_act(xt, xb, htm, htm_b, tt)
            norm_dve(xt, xb, gb, htm, htm_b, tt)

        def norm_transposes(htm, htm_b, dstT, dstT_b, tt, eng=None):
            pt, pb = bank()
            pv = pt[:].bitcast(BF16)
            for kt in range(8):
                P.tr((pv[:, kt * 128:(kt + 1) * 128], pb), (htm[:, tt, kt * 128:(kt + 1) * 128], htm_b), (ident[:], cbuf))
            P.copy(eng or ("act" if tt % 2 == 0 else "dve"), (dstT[:, :, tt * 128:(tt + 1) * 128], dstT_b[tt] if isinstance(dstT_b, (list, tuple)) else dstT_b),
                   (pv[:, 0:1024].rearrange("p (k n) -> p k n", k=8), pb))

        with ExitStack() as pst:
            psem = P.new_sem("psem")
            memt = sb("memt", [128, 2, D], F32, pst); memt_b = Buf("memt")
            memh = sb("memh", [128, 2, D], BF16, pst); memh_b = Buf("memh")
            memT = sb("memT", [128, 8, 256], BF16, pst); memT_b = Buf("memT")
            P.dma("sp", (memt[:], memt_b), (mem.rearrange("(mt p) d -> p mt d", p=128), DRAM), psem)
            gmb = sb("gmb", [128, D], F32, pst)
            P.dma("sp", CB(gmb[:]), (mem_norm_g.unsqueeze(0).partition_broadcast(128), DRAM), psem)
            for tt in range(2):
                norm_stats(memt, memt_b, gmb, memh, memh_b, tt)
            for tt in range(2):
                norm_transposes(memh, memh_b, memT, memT_b, tt)
            wk_t, wk_b = slots[0][0], slots[0][1]
            wv_t, wv_b = slots[1][0], slots[1][1]
            wkv = wk_t[:, 0:4096].rearrange("p (kt n) -> p kt n", kt=8)
            wvv = wv_t[:, 0:4096].rearrange("p (kt n) -> p kt n", kt=8)
            P.dma("pool", (wkv, wk_b), (wsrc(attn_wk), DRAM), slots[0][2])
            P.dma("pool", (wvv, wv_b), (wsrc(attn_wv), DRAM), slots[1][2])
            for hh in range(4):
                pt, pb = bank()
                for kt in range(8):
                    P.mm((pt[:, 0:256], pb), (wkv[:, kt, hh * 128:(hh + 1) * 128], wk_b), (memT[:, kt, :], memT_b),
                         start=(kt == 0), stop=(kt == 7))
                P.copy("act", (kT[:, hh, :], kT_b), (pt[:, 0:256], pb))
            for mt in range(2):
                pt, pb = bank()
                for kt in range(8):
                    P.mm((pt[:, :], pb), (memT[:, kt, mt * 128:(mt + 1) * 128], memT_b), (wvv[:, kt, :], wv_b),
                         start=(kt == 0), stop=(kt == 7))
                P.copy("dve", (Vm[:, mt, :], Vm_b), (pt[:, :], pb))
            P.barrier()

        chk('proA')
        with ExitStack() as pst:
            s5sem = P.new_sem("s5sem")
            pb_ = Buf("s5pro")
            PB = lambda ap: (ap, pb_)
            a_nat = sb("a_nat", [32, 2, 128], F32, pst)
            for half in range(2):
                P.dma("sp", PB(a_nat[:, 0, half * 64:(half + 1) * 64]), (A_re, DRAM), s5sem)
                P.dma("sp", PB(a_nat[:, 1, half * 64:(half + 1) * 64]), (A_im, DRAM), s5sem)
            aT = sb("aT", [128, 2, 32], F32, pst)
            for k in range(2):
                pt, pb = bank()
                P.tr((pt[:, 0:32], pb), PB(a_nat[:, k, :]), (identf[0:32, 0:32], cbuf))
                P.copy("dve", PB(aT[:, k, :]), (pt[:, 0:32], pb))
            dtb = sb("dtb", [128, 32], F32, pst)
            P.dma("sp", PB(dtb[:]), (log_dt.partition_broadcast(128), DRAM), s5sem)
            P.act(PB(dtb[:]), PB(dtb[:]), AF.Exp)
            chk('pb0')
            rho = sb("rho", [128, 32], F32, pst)
            thetat = sb("thetat", [128, 32], F32, pst)
            P.tt("dve", PB(rho[:]), PB(aT[:, 0, :]), PB(dtb[:]), ALU.mult)
            P.tt("dve", PB(thetat[:]), PB(aT[:, 1, :]), PB(dtb[:]), ALU.mult)
            P.ts("dve", PB(thetat[:]), PB(thetat[:]), 1.0 / TWO_PI, ALU.mult)
            chk('pb1')
            sm = [sb("sm%d" % i, [128, 32, 8], F32, pst) for i in range(8)]
            smi = sb("smi", [128, 32, 8], I32, pst)
            e1 = sm[0][:, :, 0]; c1 = sm[1][:, :, 0]; s1_ = sm[2][:, :, 0]; sh = sm[3][:, :, 0]
            t0_ = sm[4][:, :, 0]; t1_ = sm[5][:, :, 0]; t2_ = sm[6][:, :, 0]; t3_ = sm[7][:, :, 0]
            ti0 = smi[:, :, 0]
            P.act(PB(e1), PB(rho[:]), AF.Exp)
            P.copy("dve", PB(t0_), PB(thetat[:]))
            sin_turns(PB(s1_), PB(t0_), PB(ti0), PB(t1_))
            P.ts("dve", PB(t0_), PB(thetat[:]), 0.5, ALU.mult)
            sin_turns(PB(sh), PB(t0_), PB(ti0), PB(t1_))
            P.tt("dve", PB(t0_), PB(sh), PB(sh), ALU.mult)
            P.ts("dve", PB(t0_), PB(t0_), -2.0, ALU.mult)
            P.ts("dve", PB(c1), PB(t0_), 1.0, ALU.add)
            P.ts("dve", PB(t1_), PB(e1), -1.0, ALU.add)
            P.tt("dve", PB(t1_), PB(t1_), PB(c1), ALU.mult)
            P.tt("dve", PB(t1_), PB(t1_), PB(t0_), ALU.add)
            P.tt("dve", PB(t2_), PB(e1), PB(s1_), ALU.mult)
            P.tt("dve", PB(t0_), PB(aT[:, 0, :]), PB(aT[:, 0, :]), ALU.mult)
            P.tt("dve", PB(t3_), PB(aT[:, 1, :]), PB(aT[:, 1, :]), ALU.mult)
            P.tt("dve", PB(t0_), PB(t0_), PB(t3_), ALU.add)
            P.recip(PB(t0_), PB(t0_))
            kap = sb("kap", [128, 2, 32], F32, pst)
            P.tt("dve", PB(t3_), PB(t1_), PB(aT[:, 0, :]), ALU.mult)
            P.tt("dve", PB(c1), PB(t2_), PB(aT[:, 1, :]), ALU.mult)
            P.tt("dve", PB(t3_), PB(t3_), PB(c1), ALU.add)
            P.tt("dve", PB(kap[:, 0, :]), PB(t3_), PB(t0_), ALU.mult)
            P.tt("dve", PB(t3_), PB(t2_), PB(aT[:, 0, :]), ALU.mult)
            P.tt("dve", PB(c1), PB(t1_), PB(aT[:, 1, :]), ALU.mult)
            P.tt("dve", PB(t3_), PB(t3_), PB(c1), ALU.subtract)
            P.tt("dve", PB(kap[:, 1, :]), PB(t3_), PB(t0_), ALU.mult)

            chk('pb2')
            Bb = sb("Bb", [128, 2, 32, 16], F32, pst)
            Cz = sb("Cz", [128, 2, 512], F32, pst)
            pst2 = ExitStack()
            Bz = sb("Bz", [128, 2, 32, 16], F32, pst2)
            for half in range(2):
                for q in range(4):
                    P.dma("sp", PB(Bz[half * 64:(half + 1) * 64, 0, q * 8:(q + 1) * 8, :]),
                          (B_re[q * 8:(q + 1) * 8].rearrange("g p i -> p g i"), DRAM), s5sem, slow=True)
                    P.dma("sp", PB(Bz[half * 64:(half + 1) * 64, 1, q * 8:(q + 1) * 8, :]),
                          (B_im[q * 8:(q + 1) * 8].rearrange("g p i -> p g i"), DRAM), s5sem, slow=True)
            bt0 = sb("bt0", [128, 32, 16], F32, pst2)
            bt1 = sb("bt1", [128, 32, 16], F32, pst2)
            kre_b = kap[:, 0, :].unsqueeze(2).to_broadcast([128, 32, 16])
            kim_b = kap[:, 1, :].unsqueeze(2).to_broadcast([128, 32, 16])
            P.tt("dve", PB(bt0[:]), PB(Bz[:, 0]), PB(kre_b), ALU.mult)
            P.tt("dve", PB(bt1[:]), PB(Bz[:, 1]), PB(kim_b), ALU.mult)
            P.tt("dve", PB(Bb[:, 0]), PB(bt0[:]), PB(bt1[:]), ALU.subtract)
            P.tt("dve", PB(bt0[:]), PB(Bz[:, 1]), PB(kre_b), ALU.mult)
            P.tt("dve", PB(bt1[:]), PB(Bz[:, 0]), PB(kim_b), ALU.mult)
            P.tt("dve", PB(Bb[:, 1]), PB(bt0[:]), PB(bt1[:]), ALU.add)

            chk('pb3')
            c_nat = sb("c_nat", [128, 2, 4, 128], F32, pst2)
            for half in range(2):
                P.dma("sp", PB(c_nat[:, 0, :, half * 64:(half + 1) * 64]), (C_re.rearrange("(t r) p -> r t p", r=128), DRAM), s5sem)
                P.dma("sp", PB(c_nat[:, 1, :, half * 64:(half + 1) * 64]), (C_im.rearrange("(t r) p -> r t p", r=128), DRAM), s5sem)
            for k in range(2):
                pt, pb = bank()
                for tl in range(4):
                    P.tr((pt[:, tl * 128:(tl + 1) * 128], pb), PB(c_nat[:, k, tl, :]), (identf[:], cbuf))
                P.copy("dve", PB(Cz[:, k, :]), (pt[:, :], pb))

            chk('pb4')
            pst2.close()
            phcol = sb("phcol", [128, 1], F32, pst)
            sgcol = sb("sgcol", [128, 1], F32, pst)
            P.memset("dve", PB(phcol[:]), 0.0)
            P.memset("dve", PB(phcol[64:128, :]), -0.25)
            P.memset("dve", PB(sgcol[:]), 1.0)
            P.memset("dve", PB(sgcol[64:128, :]), -1.0)
            evec = sb("evec", [128, 8], F32, pst)
            P.op("pool", lambda h: h.iota(evec[:], pattern=[[1, 8]], base=0, channel_multiplier=0,
                                          allow_small_or_imprecise_dtypes=True), writes=[pb_])
            GQ = 8
            big0 = sb("big0", [128, GQ, 8, 16], F32, pst)
            big1 = sb("big1", [128, GQ, 8, 16], F32, pst)
            Lst = sb("Lst", [128, GQ, 128], F32, pst)
            Rst = sb("Rst", [128, GQ, 128], F32, pst)
            Dcol = sb("Dcol", [128, 32], F32, pst)
            Dcol_b = Buf("Dcol")
            for s_ in range(8):
                P.dma("sp", (Dcol[s_ * 16:(s_ + 1) * 16, :], Dcol_b), (ssm_D.rearrange("(g i) -> i g", i=16), DRAM), s5sem, slow=True)
            evs = sb("evs", [128, 8], F32, pst)

            def build_stack(g0, Zre, Zim, eoff, esign, use_sg):
                P.ts("dve", PB(evs[:]), PB(evec[:]), float(esign), ALU.mult, float(eoff), ALU.add)
                evb = evs[:].unsqueeze(1).to_broadcast([128, GQ, 8])
                thb = thetat[:, g0:g0 + GQ].unsqueeze(2).to_broadcast([128, GQ, 8])
                rhb = rho[:, g0:g0 + GQ].unsqueeze(2).to_broadcast([128, GQ, 8])
                ang = sm[1][:, 0:GQ, :]; mag = sm[2][:, 0:GQ, :]; cosA = sm[3][:, 0:GQ, :]; sinA = sm[4][:, 0:GQ, :]
                tf = sm[5][:, 0:GQ, :]; ang2 = sm[6][:, 0:GQ, :]; ti = smi[:, 0:GQ, :]
                P.tt("dve", PB(ang), PB(thb), PB(evb), ALU.mult)
                P.ts("dve", PB(ang), PB(ang), (phcol[:, 0:1], pb_), ALU.add)
                P.ts("dve", PB(ang2), PB(ang), 0.25, ALU.add)
                sin_turns(PB(sinA), PB(ang), PB(ti), PB(tf))
                sin_turns(PB(cosA), PB(ang2), PB(ti), PB(tf))
                P.tt("dve", PB(mag), PB(rhb), PB(evb), ALU.mult)
                P.act(PB(mag), PB(mag), AF.Exp)
                if use_sg:
                    P.ts("dve", PB(mag), PB(mag), (sgcol[:, 0:1], pb_), ALU.mult)
                P.tt("dve", PB(cosA), PB(cosA), PB(mag), ALU.mult)
                P.tt("dve", PB(sinA), PB(sinA), PB(mag), ALU.mult)
                cb = cosA.unsqueeze(3).to_broadcast([128, GQ, 8, 16])
                sbb = sinA.unsqueeze(3).to_broadcast([128, GQ, 8, 16])
                zr = Zre[:, g0:g0 + GQ, :].unsqueeze(2).to_broadcast([128, GQ, 8, 16])
                zi = Zim[:, g0:g0 + GQ, :].unsqueeze(2).to_broadcast([128, GQ, 8, 16])
                P.tt("dve", PB(big0[:]), PB(zr), PB(cb), ALU.mult)
                P.tt("dve", PB(big1[:]), PB(zi), PB(sbb), ALU.mult)
                P.tt("dve", PB(big0[:]), PB(big0[:]), PB(big1[:]), ALU.subtract)

            chk('pb5')
            Czre = Cz[:, 0, :].rearrange("p (g o) -> p g o", g=32)
            Czim = Cz[:, 1, :].rearrange("p (g o) -> p g o", g=32)
            for g0 in range(0, 32, GQ):
                build_stack(g0, Bb[:, 0], Bb[:, 1], 7.0, -1.0, False)
                chk('pb6')
                P.copy("dve", PB(Lst[:].rearrange("p g (s i) -> p g s i", s=8)), PB(big0[:]))
                build_stack(g0, Czre, Czim, -7.0, 1.0, True)
                P.copy("dve", PB(Rst[:].rearrange("p g (s i) -> p g s i", s=8)), PB(big0[:]))
                build_stack(g0, Czre, Czim, 1.0, 1.0, True)
                P.copy("dve", (VV[:, g0:g0 + GQ, :].rearrange("p g (s i) -> p g s i", s=8), VV_b), PB(big0[:]))
                chk('pb7')
                for gl in range(GQ):
                    g = g0 + gl
                    if gl == 1:
                        chk('pb8')
                    pt, pb = bank()
                    import os
                    DIS = os.environ.get("DIS", "")
                    if "tr" not in DIS:
                        P.tr((pt[:, 0:128], pb), PB(Lst[:, gl, :]), (identf[:], cbuf))
                    if "mm" not in DIS:
                        P.mm((pt[:, 128:256], pb), PB(Lst[:, gl, :]), PB(Rst[:, gl, :]))
                    if "cp" not in DIS:
                        P.copy("act", (LTT[:, g, 0:128], LTT_b), (pt[:, 0:128], pb))
                    tm, tmb = tmpA[g % 2]
                    if "tt" not in DIS:
                        _o, _a, _b = tm[:, 0:128], pt[:, 128:256], tmask[:]
                        P.op("dve", lambda h, _o=_o, _a=_a, _b=_b: h.tensor_tensor(out=_o, in0=_a, in1=_b, op=ALU.mult),
                             reads=[pb, cbuf, LTT_b], writes=[tmb])
                    if "st" not in DIS:
                        P.ts("dve", (tm[:, 128:256], tmb), (identf[:], cbuf), (Dcol[:, g:g + 1], Dcol_b), ALU.mult)
                        P.tt("dve", (LTT[:, g, 128:256], LTT_b), (tm[:, 0:128], tmb), (tm[:, 128:256], tmb), ALU.add)
            P.barrier()

        chk('proB')
        with ExitStack() as pst:
            s5sem2 = P.new_sem("s5sem2")
            pb_ = Buf("s5pro2")
            PB = lambda ap: (ap, pb_)

            def make_tables(npart, arow, dtrow, ccols, outs, ri, rf, rrow, trow, ta, ta2, tmag, ngl):
                n = ngl * 64
                P.act(PB(dtrow), PB(dtrow), AF.Exp)
                dtb2 = dtrow.unsqueeze(2).to_broadcast([npart, ngl, 64])
                P.tt("dve", PB(rrow.rearrange("c (g p) -> c g p", g=ngl)), PB(arow[:, 0, :].rearrange("c (g p) -> c g p", g=ngl)), PB(dtb2), ALU.mult)
                P.tt("dve", PB(trow.rearrange("c (g p) -> c g p", g=ngl)), PB(arow[:, 1, :].rearrange("c (g p) -> c g p", g=ngl)), PB(dtb2), ALU.mult)
                P.ts("dve", PB(trow), PB(trow), 1.0 / TWO_PI, ALU.mult)
                P.copy("dve", PB(ri), PB(trow))
                P.copy("dve", PB(rf), PB(ri))
                P.tt("dve", PB(trow), PB(trow), PB(rf), ALU.subtract)
                for (col, dre, dim_) in outs:
                    P.ts("dve", PB(ta), PB(trow), (col, pb_), ALU.mult)
                    P.ts("dve", PB(ta2), PB(ta), 0.25, ALU.add)
                    P.act(PB(tmag), PB(rrow), AF.Exp, scale=(col, pb_))
                    sin_turns(dim_, PB(ta), PB(ri), PB(rf))
                    sin_turns(dre, PB(ta2), PB(ri), PB(rf))
                    P.tt("dve", dim_, dim_, PB(tmag), ALU.mult)
                    P.tt("dve", dre, dre, PB(tmag), ALU.mult)

            arow2 = sb("arow2", [128, 2, 1024], F32, pst)
            dtrow2 = sb("dtrow2", [128, 16], F32, pst)
            A_re_h = A_re.rearrange("(h g) p -> h (g p)", h=2)
            A_im_h = A_im.rearrange("(h g) p -> h (g p)", h=2)
            ldt_h = log_dt.rearrange("o (h g) -> (o h) g", h=2)
            for h in range(2):
                P.dma("sp", PB(arow2[64 * h:64 * h + 64, 0, :]), (A_re_h[h:h + 1, :].partition_broadcast(64), DRAM), s5sem2)
                P.dma("sp", PB(arow2[64 * h:64 * h + 64, 1, :]), (A_im_h[h:h + 1, :].partition_broadcast(64), DRAM), s5sem2)
                P.dma("sp", PB(dtrow2[64 * h:64 * h + 64, :]), (ldt_h[h:h + 1, :].partition_broadcast(64), DRAM), s5sem2)
            ccol = sb("ccol", [128, 2], F32, pst)
            for h in range(2):
                P.op("pool", lambda hh, h=h: hh.iota(ccol[64 * h:64 * h + 64, 0:1], pattern=[[0, 1]], base=0, channel_multiplier=8,
                                                      allow_small_or_imprecise_dtypes=True), writes=[pb_])
            P.ts("dve", PB(ccol[:, 1:2]), PB(ccol[:, 0:1]), -1.0, ALU.mult, -8.0, ALU.add)
            ri = sb("ri", [128, 1024], I32, pst); rf = sb("rf", [128, 1024], F32, pst)
            rrow = sb("rrow", [128, 1024], F32, pst); trow = sb("trow", [128, 1024], F32, pst)
            ta = sb("ta", [128, 1024], F32, pst); ta2 = sb("ta2", [128, 1024], F32, pst); tmag = sb("tmag", [128, 1024], F32, pst)
            make_tables(128, arow2[:], dtrow2[:], None,
                        [(ccol[:, 0:1], (U2re[:], tab_b), (U2im[:], tab_b)), (ccol[:, 1:2], (S2re[:], tab_b), (S2im[:], tab_b))],
                        ri[:], rf[:], rrow[:], trow[:], ta[:], ta2[:], tmag[:], 16)
            arow64 = sb("arow64", [66, 2, 256], F32, pst)
            dtrow64 = sb("dtrow64", [66, 4], F32, pst)
            arowB = sb("arowB", [2, 2, 256], F32, pst)
            dtrowB = sb("dtrowB", [2, 4], F32, pst)
            P.memset("dve", PB(arow64[:]), 0.0)
            P.memset("dve", PB(dtrow64[:]), 0.0)
            A_re_hq = A_re.rearrange("(h q g) p -> h q (g p)", h=2, q=4)
            A_im_hq = A_im.rearrange("(h q g) p -> h q (g p)", h=2, q=4)
            ldt_hq = log_dt.rearrange("o (h q g) -> (o h) q g", h=2, q=4)
            for q in range(4):
                if q < 3:
                    dst_a, dst_d, base = arow64, dtrow64, 32 * q
                else:
                    dst_a, dst_d, base = arowB, dtrowB, 0
                P.dma("sp", PB(dst_a[base:base + 2, 0, :]), (A_re_hq[:, q, :], DRAM), s5sem2)
                P.dma("sp", PB(dst_a[base:base + 2, 1, :]), (A_im_hq[:, q, :], DRAM), s5sem2)
                P.dma("sp", PB(dst_d[base:base + 2, :]), (ldt_hq[:, q, :], DRAM), s5sem2)
            c64 = sb("c64", [66, 1], F32, pst)
            P.memset("dve", PB(c64[:]), 512.0)
            make_tables(66, arow64[:], dtrow64[:], None,
                        [(c64[:, 0:1], (U64A[:, 0, :], tab_b), (U64A[:, 1, :], tab_b))],
                        ri[0:66, 0:256], rf[0:66, 0:256], rrow[0:66, 0:256], trow[0:66, 0:256], ta[0:66, 0:256], ta2[0:66, 0:256], tmag[0:66, 0:256], 4)
            make_tables(2, arowB[:], dtrowB[:], None,
                        [(c64[0:2, 0:1], (U64B[:, 0, :], tab_b), (U64B[:, 1, :], tab_b))],
                        ri[0:2, 256:512], rf[0:2, 256:512], rrow[0:2, 256:512], trow[0:2, 256:512], ta[0:2, 256:512], ta2[0:2, 256:512], tmag[0:2, 256:512], 4)
            P.memset("dve", (HstA[:], Hst_b[0]), 0.0)
            P.memset("dve", (HstB[:], Hst_b[3]), 0.0)
            P.memset("dve", (HbfA[:], Hst_b[0]), 0.0)
            P.memset("dve", (HbfB[:], Hst_b[3]), 0.0)
            for ft in range(4):
                P.memset("pool", (cv[:, ft, 0:2], cv_b[ft]), 0.0)
            P.barrier()

        chk('proC')
        dbg("LTT", LTT[:], LTT_b, [128, 32, 256])
        dbg("VV", VV[:], VV_b, [128, 32, 128])
        dbg("U2re", U2re[:], tab_b, [128, 1024])
        dbg("U2im", U2im[:], tab_b, [128, 1024])
        dbg("S2re", S2re[:], tab_b, [128, 1024])
        dbg("S2im", S2im[:], tab_b, [128, 1024])
        dbg("U64A", U64A[:], tab_b, [66, 2, 256])
        dbg("U64B", U64B[:], tab_b, [2, 2, 256])
        dbg("kT", kT[:], kT_b, [128, 4, 256])
        dbg("Vm", Vm[:], Vm_b, [128, 2, 512])

        ARENA_B = 43008
        arena = sb("arena", [128, ARENA_B // 2], BF16)
        arena_bufs = []

        def aview(off, nbytes, dt, name):
            assert off + nbytes <= ARENA_B
            v = arena[:, off // 2:(off + nbytes) // 2]
            if dt == F32:
                v = v.bitcast(F32)
            b = Buf(name)
            b.rng = (off, off + nbytes)
            arena_bufs.append(b)
            return v, b

        def phase(bufs):
            for nb in bufs:
                ev = {}
                for ab in arena_bufs:
                    if ab is not nb and (ab.rng[1] <= nb.rng[0] or nb.rng[1] <= ab.rng[0]):
                        continue
                    if ab.w is not None:
                        s_, v = ab.w
                        if ev.get(s_, 0) < v:
                            ev[s_] = v
                    for s_, v in ab.r.items():
                        if ev.get(s_, 0) < v:
                            ev[s_] = v
                nb.w = None
                nb.r = ev

        u_tm_f, u_tm_b = aview(0, 4096, BF16, "u_tm"); u_tm = u_tm_f.rearrange("p (g s i) -> p g s i", g=16, s=8)
        yg_f, yg_b = aview(4096, 4096, BF16, "yg"); yg = yg_f.rearrange("p (t c) -> p t c", t=8)
        Ut_f, Ut_b = aview(8192, 4096, BF16, "Ut"); Ut = Ut_f.rearrange("p (g c) -> p g c", g=32)
        HT_f, HT_b = aview(12288, 4096, BF16, "HT"); HT = HT_f.rearrange("p (g c) -> p g c", g=32)
        wk1, wk1_b = aview(16384, 2048, F32, "wk1")
        wk2, wk2_b = aview(18432, 2048, F32, "wk2")
        hs, hs_b = aview(20480, 1024, BF16, "hs")
        hprev, hprev_b = aview(21504, 1024, BF16, "hprev")
        hrow, hrow_b = aview(22528, 2048, F32, "hrow")
        hrow2, hrow2_b = aview(24576, 2048, F32, "hrow2")
        ysT_f, ysT_b = aview(26624, 4096, BF16, "ysT"); ysT = ysT_f.rearrange("p (k n) -> p k n", k=4)
        qT_f, qT_b = aview(30720, 4096, BF16, "qT"); qT = qT_f.rearrange("p (k n) -> p k n", k=4)
        eT = []
        for i in range(2):
            v, b_ = aview(34816 + 2048 * i, 2048, BF16, "eT%d" % i)
            eT.append((v.rearrange("p (k n) -> p k n", k=2), b_))
        rec, rec_b = aview(38912, 2048, F32, "rec")
        tmpB = [aview(40960, 2048, F32, "tmpB0"), aview(36864, 2048, F32, "tmpB1")]
        h_tm_f, h_tm_b = aview(30720, 8192, BF16, "h_tm"); h_tm = h_tm_f.rearrange("p (t d) -> p t d", t=4)
        junk_f, junk_b = aview(22528, 8192, BF16, "junk"); junk = junk_f.rearrange("p (t d) -> p t d", t=4)
        merged_f, merged_b = aview(0, 16384, F32, "merged"); merged = merged_f.rearrange("p (k n) -> p k n", k=8)
        mergedT_f, mergedT_b = aview(16384, 8192, BF16, "mergedT"); mergedT = mergedT_f.rearrange("p (k n) -> p k n", k=8)
        actT_f, actT_b = aview(0, 22528, BF16, "actT"); actT = actT_f.rearrange("p (k n) -> p k n", k=22)
        S5_BUFS = [u_tm_b, yg_b, Ut_b, HT_b, wk1_b, wk2_b, hs_b, hprev_b, hrow_b, hrow2_b, ysT_b]
        P2_BUFS = [qT_b, eT[0][1], eT[1][1], rec_b, tmpB[0][1], tmpB[1][1]]

        xv = x.rearrange("(b tt p) d -> b p tt d", p=128, tt=4)
        ov = out.rearrange("(b tt p) d -> b p tt d", p=128, tt=4)

        def x_load(b):
            t, buf, ld, stsem = xbuf[b % 2]
            P.dma("pool", (t[:], buf), (xv[b], DRAM), ld)

        x_load(0)
        inv_sqrt_hd = 1.0 / math.sqrt(128.0)

        def cmul(np0, np1, src, src_b, tre, tim, t1, t1b, t2, t2b, ng):
            npn = np1 - np0
            treb = tre.unsqueeze(2).to_broadcast([npn, ng, 2, 64])
            P.tt("dve", (t1, t1b), (src, src_b), (treb, tab_b), ALU.mult)
            P.tt("dve", (t2[:, :, 0, :], t2b), (src[:, :, 1, :], src_b), (tim, tab_b), ALU.mult)
            P.tt("dve", (t2[:, :, 1, :], t2b), (src[:, :, 0, :], src_b), (tim, tab_b), ALU.mult)

        def final_norm_store(bb):
            xt_, xb_, _, xst_ = xbuf[bb % 2]
            jt, jtb = tmpA[0]
            jv = jt[:].bitcast(BF16)
            for tt in range(4):
                P.act((jv, jtb), (xt_[:, tt, :], xb_), AF.Square, accum=(ss[:, tt:tt + 1], ss_b[tt]))
                P.act((rstd[:, tt:tt + 1], rstd_b[tt]), (ss[:, tt:tt + 1], ss_b[tt]), AF.Sqrt, scale=1.0 / D, bias=(epscol[:, 0:1], cbuf))
                P.recip((rstd[:, tt:tt + 1], rstd_b[tt]), (rstd[:, tt:tt + 1], rstd_b[tt]))
                P.stt((xt_[:, tt, :], xb_), (xt_[:, tt, :], xb_), (rstd[:, tt:tt + 1], rstd_b[tt]), (gF[:], cbuf), ALU.mult, ALU.mult)
            P.dma("pool", (ov[bb], OUTB), (xt_[:], xb_), xst_)

        pending = [None]
        for b in range(NBLK):
            xt, xb, _, xst = xbuf[b % 2]
            if b == 0 and NBLK > 1:
                x_load(1)
            if b == 0:
                phase([h_tm_b])
                for tt in range(4):
                    norm_stats(xt, xb, g1b, h_tm, h_tm_b, tt)
                for tt in range(4):
                    norm_transposes(h_tm, h_tm_b, hT, hT_b, tt)
                dbg("hT", hT[:], hT_b, [128, 8, NB])

            chk('s1')
            phase(S5_BUFS)
            wt, wb = w_get("win3")
            wv_ = wt[:, 0:4096].rearrange("p (kt n) -> p kt n", kt=8)
            hTs = hT[:].rearrange("p kt (c s) -> p kt s c", s=8)
            for s_ in range(8):
                pt, pb = bank()
                for h in range(2):
                    for kt in range(8):
                        P.mm((pt[64 * h:64 * h + 64, 0:256], pb), (hTs[:, kt, s_, :], hT_b), (wv_[:, kt, 256 * h:256 * h + 256], wb),
                             start=(kt == 0), stop=(kt == 7))
                P.copy("act" if s_ % 2 == 0 else "dve", (u_tm[:, :, s_, :], u_tm_b), (pt[:, 0:256].rearrange("p (g i) -> p g i", g=16), pb))
            w_rel("win3")
            if b == 0:
                dbg("u_tm", u_tm_f, u_tm_b, [128, 2048])

            chk('u')
            Ut4 = Ut_f.rearrange("p (h gl c) -> p h gl c", h=2, gl=16)
            for jb in range(2):
                pt, pb = bank()
                pv = pt[:].bitcast(BF16)
                for g8 in range(8):
                    gl = jb * 8 + g8
                    P.tr((pv[:, g8 * 128:(g8 + 1) * 128], pb), (u_tm_f[:, gl * 128:(gl + 1) * 128], u_tm_b), (ident[:], cbuf))
                P.copy("act" if jb == 0 else "dve", (Ut4[:, :, jb * 8:(jb + 1) * 8, :].rearrange("p h gl c -> p gl h c"), Ut_b),
                       (pv[:, 0:1024].rearrange("p (gl h c) -> p gl h c", gl=8, h=2), pb))
            if pending[0] is not None:
                final_norm_store(pending[0])
                pending[0] = None
                if b + 1 < NBLK:
                    x_load(b + 1)
            chk('s5a')
            def s5_steps():
                for q in range(4):
                    gof = lambda h, k: 16 * h + 4 * q + k
                    pA, pAb = bank()
                    for h in range(2):
                        for k in range(4):
                            g = gof(h, k)
                            P.mm((pA[64 * h:64 * h + 64, k * 128:(k + 1) * 128], pAb), (Ut[:, g, :], Ut_b), (LTT[:, g, 0:128], LTT_b))
                    src = pA[:, :].rearrange("c (g r p) -> c g r p", g=4, r=2)
                    tre = S2re[:, q * 256:(q + 1) * 256].rearrange("c (g p) -> c g p", g=4)
                    tim = S2im[:, q * 256:(q + 1) * 256].rearrange("c (g p) -> c g p", g=4)
                    t1 = wk1.rearrange("c (g r p) -> c g r p", g=4, r=2)
                    t2 = wk2.rearrange("c (g r p) -> c g r p", g=4, r=2)
                    d = hs.rearrange("c (g r p) -> c g r p", g=4, r=2)
                    cmul(0, 128, src, pAb, tre, tim, t1, wk1_b, t2, wk2_b, 4)
                    P.tt("pool", (d[:, :, 0, :], hs_b), (t1[:, :, 0, :], wk1_b), (t2[:, :, 0, :], wk2_b), ALU.subtract)
                    P.tt("pool", (d[:, :, 1, :], hs_b), (t1[:, :, 1, :], wk1_b), (t2[:, :, 1, :], wk2_b), ALU.add)
                    if q == 0: chk('s5c')
                    yield
                    hrows, u64, selr, i2r, hbase, hbf = hst_rows(q)
                    pC, pCb = bank()
                    pT, pTb = bank()
                    P.mm((pC[:, :], pCb), (trix2_bf[:, :], cbuf), (hs, hs_b), start=True, stop=False)
                    P.mm((pC[:, :], pCb), (selr, cbuf), (hbf, Hst_b[q]), start=False, stop=True)
                    P.mm((pT[hbase:hbase + 32, :], pTb), (ones2_bf[:, :], cbuf), (hs, hs_b), start=True, stop=False)
                    P.mm((pT[hbase:hbase + 32, :], pTb), (i2r, cbuf), (hbf, Hst_b[q]), start=False, stop=True)
                    if q == 0: chk('s5d')
                    src = pC[:, :].rearrange("c (g r p) -> c g r p", g=4, r=2)
                    tre = U2re[:, q * 256:(q + 1) * 256].rearrange("c (g p) -> c g p", g=4)
                    tim = U2im[:, q * 256:(q + 1) * 256].rearrange("c (g p) -> c g p", g=4)
                    cmul(0, 128, src, pCb, tre, tim, t1, wk1_b, t2, wk2_b, 4)
                    dh = hprev.rearrange("c (g r p) -> c g r p", g=4, r=2)
                    P.tt("pool", (dh[:, :, 0, :], hprev_b), (t1[:, :, 0, :], wk1_b), (t2[:, :, 0, :], wk2_b), ALU.subtract)
                    P.tt("pool", (dh[:, :, 1, :], hprev_b), (t1[:, :, 1, :], wk1_b), (t2[:, :, 1, :], wk2_b), ALU.add)
                    srcT = pT[hbase:hbase + 2, :].rearrange("c (g r p) -> c g r p", g=4, r=2)
                    r1 = hrow[hbase:hbase + 2, :].rearrange("c (g r p) -> c g r p", g=4, r=2)
                    r2 = hrow2[hbase:hbase + 2, :].rearrange("c (g r p) -> c g r p", g=4, r=2)
                    ure = u64[:, 0, :].rearrange("c (g p) -> c g p", g=4)
                    uim = u64[:, 1, :].rearrange("c (g p) -> c g p", g=4)
                    cmul(hbase, hbase + 2, srcT, pTb, ure, uim, r1, hrow_b, r2, hrow2_b, 4)
                    hd = hrows.rearrange("c (g r p) -> c g r p", g=4, r=2)
                    P.tt("dve", (hd[:, :, 0, :], Hst_b[q]), (r1[:, :, 0, :], hrow_b), (r2[:, :, 0, :], hrow2_b), ALU.subtract)
                    P.tt("dve", (hd[:, :, 1, :], Hst_b[q]), (r1[:, :, 1, :], hrow_b), (r2[:, :, 1, :], hrow2_b), ALU.add)
                    P.copy("dve", (hbf, Hst_b[q]), (hrows, Hst_b[q]))
                    if q == 0: chk('s5e')
                    yield
                    pt, pb = bank()
                    pv = pt[:].bitcast(BF16)
                    for k in range(4):
                        P.tr((pv[:, k * 128:(k + 1) * 128], pb), (hprev[:, k * 128:(k + 1) * 128], hprev_b), (ident[:], cbuf))
                    HT4 = HT_f.rearrange("p (h gl c) -> p h gl c", h=2, gl=16)
                    P.copy("act", (HT4[:, :, 4 * q:4 * q + 4, :].rearrange("p h k c -> p k h c"), HT_b),
                           (pv[:, 0:512].rearrange("p (k h c) -> p k h c", k=4, h=2), pb))
                    if q == 0: chk('s5f')
                    pE, pEb = bank()
                    for h in range(2):
                        for k in range(4):
                            g = gof(h, k)
                            o_ = (pE[64 * h:64 * h + 64, k * 128:(k + 1) * 128], pEb)
                            P.mm(o_, (Ut[:, g, :], Ut_b), (LTT[:, g, 128:256], LTT_b), start=True, stop=False)
                            P.mm(o_, (HT[:, g, :], HT_b), (VV[:, g, :], VV_b), start=False, stop=True)
                    src = pE[:, :].rearrange("c (g t o) -> c t g o", g=4, t=8)
                    dstv = yg[:, :, q * 64:(q + 1) * 64].rearrange("c t (g o) -> c t g o", g=4)
                    P.act((dstv, yg_b), (src, pEb), AF.Gelu_apprx_tanh)
                    yield
            def bulk_units():
                wt, wb = w_get("win4")
                wv_ = wt[:, 0:4096].rearrange("p (kt n) -> p kt n", kt=8)
                for hh in range(4):
                    pt, pb = bank()
                    for kt in range(8):
                        P.mm((pt[:, :], pb), (wv_[:, kt, hh * 128:(hh + 1) * 128], wb), (hT[:, kt, :], hT_b), start=(kt == 0), stop=(kt == 7))
                    P.copy("act" if hh % 2 == 0 else "dve", (qT[:, hh, :], qT_b), (pt[:, :], pb))
                    yield
                w_rel("win4")
                for hh in range(4):
                    et, eb = eT[hh % 2]
                    for mt in range(2):
                        pt, pb = bank()
                        P.mm((pt[:, :], pb), (kT[:, hh, mt * 128:(mt + 1) * 128], kT_b), (qT[:, hh, :], qT_b))
                        P.act((et[:, mt, :], eb), (pt[:, :], pb), AF.Exp, scale=inv_sqrt_hd)
                    yield
                    po, pob = bank()
                    pd, pdb = bank()
                    for mt in range(2):
                        P.mm((po[:, :], pob), (Vm[:, mt, hh * 128:(hh + 1) * 128], Vm_b), (et[:, mt, :], eb), start=(mt == 0), stop=(mt == 1))
                    for mt in range(2):
                        P.mm((pd[:, :], pdb), (ones_bf[:, :], cbuf), (et[:, mt, :], eb), start=(mt == 0), stop=(mt == 1))
                    P.recip((rec, rec_b), (pd[:, :], pdb))
                    P.tt("dve", (oT[:, hh, :], oT_b), (po[:, :], pob), (rec, rec_b), ALU.mult)
                    yield
                if b == 0:
                    dbg("oT", oT[:], oT_b, [128, 4, NB])

                wc_t, wc_b = w_get("win1")
                wv_t, wv_b2 = w_get("win2")
                wcv = wc_t[:, 0:4096].rearrange("p (kt n) -> p kt n", kt=8)
                wvv2 = wv_t[:, 0:4096].rearrange("p (kt n) -> p kt n", kt=8)
                for ft in range(4):
                    pc, pcb = bank()
                    pvv, pvb = bank()
                    for kt in range(8):
                        P.mm((pc[:, :], pcb), (wcv[:, kt, ft * 128:(ft + 1) * 128], wc_b), (hT[:, kt, :], hT_b), start=(kt == 0), stop=(kt == 7))
                    for kt in range(8):
                        P.mm((pvv[:, :], pvb), (wvv2[:, kt, ft * 128:(ft + 1) * 128], wv_b2), (hT[:, kt, :], hT_b), start=(kt == 0), stop=(kt == 7))
                    ta_, tab_ = tmpA[ft % 2]
                    P.copy("act", (ta_[:], tab_), (pc[:, :], pcb))
                    P.tt("dve", (cv[:, ft, 2:NB + 2], cv_b[ft]), (ta_[:], tab_), (pvv[:, :], pvb), ALU.mult)
                    yield
                w_rel("win1")
                w_rel("win2")
                phase([tmpB[1][1]])
                wb_t, wb_b = w_get("win0")
                wbv = wb_t[:, 0:4096].rearrange("p (kt n) -> p kt n", kt=8)
                for ft in range(4):
                    pbk, pbb = bank()
                    for kt in range(8):
                        P.mm((pbk[:, :], pbb), (wbv[:, kt, ft * 128:(ft + 1) * 128], wb_b), (hT[:, kt, :], hT_b), start=(kt == 0), stop=(kt == 7))
                    tb_, tbb_ = tmpB[ft % 2]
                    P.ts("dve", (tb_, tbb_), (cv[:, ft, 2:NB + 2], cv_b[ft]), (convw[:, 2, ft:ft + 1], cbuf), ALU.mult)
                    P.stt((tb_, tbb_), (cv[:, ft, 1:NB + 1], cv_b[ft]), (convw[:, 1, ft:ft + 1], cbuf), (tb_, tbb_), ALU.mult, ALU.add)
                    P.stt((tb_, tbb_), (cv[:, ft, 0:NB], cv_b[ft]), (convw[:, 0, ft:ft + 1], cbuf), (tb_, tbb_), ALU.mult, ALU.add)
                    P.tt("dve", (conv_preT[:, ft, :], conv_pre_b), (tb_, tbb_), (pbk[:, :], pbb), ALU.mult)
                    P.copy("pool", (cv[:, ft, 0:2], cv_b[ft]), (cv[:, ft, NB:NB + 2], cv_b[ft]))
                    yield
                w_rel("win0")
                if b == 0:
                    dbg("conv_preT", conv_preT[:], conv_pre_b, [128, 4, NB])

                yield
            phase([bb for bb in P2_BUFS if bb is not tmpB[1][1]])
            gs_, gb_ = s5_steps(), bulk_units()
            alive_s, alive_b = True, True
            import os
            ILV = "a"
            nb_units = 0
            if ILV == "0":
                for _ in gs_:
                    pass
                alive_s = False
            if ILV == "c":
                for _ in range(12):
                    next(gb_)
            while alive_s or alive_b:
                if alive_s:
                    try:
                        next(gs_)
                    except StopIteration:
                        alive_s = False
                for _rep in range(1):
                    if alive_b and not (ILV == "a" and nb_units >= 12 and alive_s):
                        try:
                            next(gb_)
                            nb_units += 1
                        except StopIteration:
                            alive_b = False
            if b == 0:
                dbg("yg", yg_f, yg_b, [128, 2048])
            chk('s5h')
            ys5 = ysT_f.rearrange("p (h j c t) -> p h j c t", h=2, j=2, t=8)
            for j in range(2):
                pt, pb = bank()
                pv = pt[:].bitcast(BF16)
                for t_ in range(8):
                    P.tr((pv[:, t_ * 128:(t_ + 1) * 128], pb), (yg[:, t_, j * 128:(j + 1) * 128], yg_b), (ident[:], cbuf))
                P.copy("act", (ys5[:, :, j, :, :], ysT_b),
                       (pv[:, 0:1024].rearrange("p (t h c) -> p h c t", t=8, h=2), pb))
            if b == 0:
                dbg("ysT", ysT_f, ysT_b, [128, 2048])

            chk('s5')
            chk('conv')
            wt, wb = w_get("glu")
            wgl = wt[:, 0:2048].rearrange("p (kt n) -> p kt n", kt=4)
            for ft in range(4):
                pt, pb = bank()
                for kt in range(4):
                    P.mm((pt[:, :], pb), (wgl[:, kt, ft * 128:(ft + 1) * 128], wb), (ysT[:, kt, :], ysT_b), start=(kt == 0), stop=(kt == 3))
                ta_, tab_ = tmpA[ft % 2]
                P.act((ta_[:], tab_), (pt[:, :], pb), AF.Sigmoid, bias=(glub[:, ft:ft + 1], cbuf))
                P.tt("pool", (ys2T[:, ft, :], ys2_b), (ysT[:, ft, :], ysT_b), (ta_[:], tab_), ALU.mult)
            w_rel("glu")
            if b == 0:
                dbg("ys2T", ys2T[:], ys2_b, [128, 4, NB])

            chk('glu')
            phase([merged_b, mergedT_b])
            for bi, (pname, gch, srcT, srcb) in enumerate((("cp", (5, 6), conv_preT, conv_pre_b),
                                                            ("sp", (7, 8), ys2T, ys2_b),
                                                            ("ap", (9, 10), oT, oT_b))):
                wp_t, wp_b = w_get(pname)
                wpv = wp_t[:, 0:4096].rearrange("p (kt n) -> p kt n", kt=4)
                for gi, gc in enumerate(gch):
                    wg_t, wg_b = w_get("win%d" % gc)
                    wgv = wg_t[:, 0:4096].rearrange("p (kt n) -> p kt n", kt=8)
                    for jl in range(4):
                        j = gi * 4 + jl
                        py, pyb = bank()
                        pg, pgb = bank()
                        for kt in range(4):
                            P.mm((py[:, :], pyb), (wpv[:, kt, j * 128:(j + 1) * 128], wp_b), (srcT[:, kt, :], srcb), start=(kt == 0), stop=(kt == 3))
                        for kt in range(8):
                            P.mm((pg[:, :], pgb), (wgv[:, kt, jl * 128:(jl + 1) * 128], wg_b), (hT[:, kt, :], hT_b), start=(kt == 0), stop=(kt == 7))
                        ta_, tab_ = tmpA[j % 2]
                        P.act((ta_[:], tab_), (pg[:, :], pgb), AF.Sigmoid)
                        if bi == 0:
                            P.tt("dve", (merged[:, j, :], merged_b), (ta_[:], tab_), (py[:, :], pyb), ALU.mult)
                        elif bi == 1:
                            tb_, tbb_ = tmpB[j % 2]
                            P.tt("dve", (tb_, tbb_), (ta_[:], tab_), (py[:, :], pyb), ALU.mult)
                            P.tt("pool", (merged[:, j, :], merged_b), (merged[:, j, :], merged_b), (tb_, tbb_), ALU.add)
                        else:
                            tb_, tbb_ = tmpB[j % 2]
                            P.tt("dve", (tb_, tbb_), (ta_[:], tab_), (py[:, :], pyb), ALU.mult)
                            P.tt("pool", (mergedT[:, j, :], mergedT_b), (merged[:, j, :], merged_b), (tb_, tbb_), ALU.add)
                    w_rel("win%d" % gc)
                w_rel(pname)
            if b == 0:
                dbg("mergedT", mergedT_f, mergedT_b, [128, 4096])

            chk('merge')
            wo = [w_get("wo0"), w_get("wo1")]
            phase([h_tm_b])
            for tt in range(4):
                for nh in range(2):
                    wt, wb = wo[nh]
                    wov = wt[:, 0:4096].rearrange("p (kt n) -> p kt n", kt=8)
                    pt, pb = bank()
                    for j in range(8):
                        P.mm((pt[:, :], pb), (mergedT[:, j, tt * 128:(tt + 1) * 128], mergedT_b), (wov[:, j, :], wb), start=(j == 0), stop=(j == 7))
                    P.tt("dve", (xt[:, tt, nh * 512:(nh + 1) * 512], xb), (xt[:, tt, nh * 512:(nh + 1) * 512], xb), (pt[:, :], pb), ALU.add)
                norm_act(xt, xb, h_tm, h_tm_b, tt)
                if tt >= 1:
                    norm_dve(xt, xb, g2b, h_tm, h_tm_b, tt - 1)
            norm_dve(xt, xb, g2b, h_tm, h_tm_b, 3)
            w_rel("wo0")
            w_rel("wo1")
            if b == 0:
                dbg("x1", xt[:], xb, [128, 4, D])
            chk('wo')
            for tt in range(2):
                norm_transposes(h_tm, h_tm_b, hT, hT_b, tt)

            phase([actT_b])
            for j in range(11):
                wt, wb = w_get("ffn%d" % j)
                wgv = wt[:, 0:2048].rearrange("p (kt n) -> p kt n", kt=8)
                wuv = wt[:, 2048:4096].rearrange("p (kt n) -> p kt n", kt=8)
                fb = [(bank(), bank()) for _sub in range(2)]
                if j == 0:
                    for half in range(2):
                        c0, c1 = half * 256, (half + 1) * 256
                        hb2 = (hT_tt[2 * half], hT_tt[2 * half + 1])
                        for sub in range(2):
                            (pg, pgb), (pu, pub) = fb[sub]
                            for kt in range(8):
                                P.mm((pg[:, c0:c1], pgb), (wgv[:, kt, sub * 128:(sub + 1) * 128], wb), (hT[:, kt, c0:c1], hb2), start=(kt == 0), stop=(kt == 7))
                            for kt in range(8):
                                P.mm((pu[:, c0:c1], pub), (wuv[:, kt, sub * 128:(sub + 1) * 128], wb), (hT[:, kt, c0:c1], hb2), start=(kt == 0), stop=(kt == 7))
                        if half == 0:
                            for tt in range(2, 4):
                                norm_transposes(h_tm, h_tm_b, hT, hT_b, tt)
                else:
                    for sub in range(2):
                        (pg, pgb), (pu, pub) = fb[sub]
                        for kt in range(8):
                            P.mm((pg[:, :], pgb), (wgv[:, kt, sub * 128:(sub + 1) * 128], wb), (hT[:, kt, :], hT_b), start=(kt == 0), stop=(kt == 7))
                        for kt in range(8):
                            P.mm((pu[:, :], pub), (wuv[:, kt, sub * 128:(sub + 1) * 128], wb), (hT[:, kt, :], hT_b), start=(kt == 0), stop=(kt == 7))
                for sub in range(2):
                    f = 2 * j + sub
                    (pg, pgb), (pu, pub) = fb[sub]
                    ta_, tab_ = tmpA[f % 2]
                    P.act((ta_[:], tab_), (pg[:, :], pgb), AF.Silu)
                    P.tt("dve", (actT[:, f, :], actT_b), (ta_[:], tab_), (pu[:, :], pub), ALU.mult)
                w_rel("ffn%d" % j)
            chk('ffn1')
            if b + 1 < NBLK:
                xn, xnb, _, _ = xbuf[(b + 1) % 2]
                phase([h_tm_b])
                for tt in range(4):
                    norm_stats(xn, xnb, g1b, h_tm, h_tm_b, tt)
            accs = [pbank[i] for i in range(8)]
            for r in range(6):
                nk = 4 if r < 5 else 2
                wt, wb = w_get("dn%d" % r)
                wdv = wt[:, 0:nk * 1024].rearrange("p (kt n) -> p kt n", kt=nk)
                for tt in range(4):
                    for nh in range(2):
                        pt, pb = accs[tt * 2 + nh]
                        for k in range(nk):
                            f = r * 4 + k
                            P.mm((pt[:, :], pb), (actT[:, f, tt * 128:(tt + 1) * 128], actT_b), (wdv[:, k, nh * 512:(nh + 1) * 512], wb),
                                 start=(f == 0), stop=(f == 21))
                w_rel("dn%d" % r)
            for tt in range(4):
                for nh in range(2):
                    pt, pb = accs[tt * 2 + nh]
                    P.tt("dve", (xt[:, tt, nh * 512:(nh + 1) * 512], xb), (xt[:, tt, nh * 512:(nh + 1) * 512], xb), (pt[:, :], pb), ALU.add)
            bank_rr[0] = 0
            if b + 1 < NBLK:
                for tt in range(4):
                    norm_transposes(h_tm, h_tm_b, hT, hT_b, tt, eng="act")
            if b + 1 < NBLK:
                pending[0] = b
            else:
                final_norm_store(b)

      except _Stop:
        pass
      P.final_wait()
      P.emit()
    return nc, dbg_outs


_CACHE = {}

PARAM_KEYS = ["norm1_g", "w_in", "conv_w", "conv_proj", "ssm_A_re", "ssm_A_im", "ssm_log_dt", "ssm_B_re", "ssm_B_im",
              "ssm_C_re", "ssm_C_im", "ssm_D", "ssm_glu_w", "ssm_glu_b", "ssm_proj", "mem_norm_g", "attn_wk", "attn_wv",
              "attn_proj", "w_o", "norm2_g", "ffn_w_gate", "ffn_w_up", "ffn_w_down", "final_norm_g"]


def _prep_params(inp):
    f = lambda a: np.ascontiguousarray(np.asarray(a, dtype=np.float32))
    p = {}
    p["norm1_g"] = f(inp["norm1_g"]).reshape(1024)
    p["w_in"] = f(inp["w_in"]).reshape(1024, 5632)
    p["conv_w"] = f(inp["conv_w"]).reshape(3, 512)
    p["conv_proj"] = f(inp["conv_proj"]).reshape(512, 1024)
    p["ssm_A_re"] = f(inp["ssm_A_re"]).reshape(32, 64)
    p["ssm_A_im"] = f(inp["ssm_A_im"]).reshape(32, 64)
    p["ssm_log_dt"] = f(inp["ssm_log_dt"]).reshape(1, 32)
    p["ssm_B_re"] = f(inp["ssm_B_re"]).reshape(32, 64, 16)
    p["ssm_B_im"] = f(inp["ssm_B_im"]).reshape(32, 64, 16)
    p["ssm_C_re"] = f(inp["ssm_C_re"]).reshape(512, 64)
    p["ssm_C_im"] = f(inp["ssm_C_im"]).reshape(512, 64)
    p["ssm_D"] = f(inp["ssm_D"]).reshape(512)
    p["ssm_glu_w"] = f(inp["ssm_glu_w"]).reshape(512, 512)
    p["ssm_glu_b"] = f(inp["ssm_glu_b"]).reshape(512)
    p["ssm_proj"] = f(inp["ssm_proj"]).reshape(512, 1024)
    p["mem_norm_g"] = f(inp["mem_norm_g"]).reshape(1024)
    p["attn_wk"] = f(inp["attn_wk"]).reshape(1024, 512)
    p["attn_wv"] = f(inp["attn_wv"]).reshape(1024, 512)
    p["attn_proj"] = f(inp["attn_proj"]).reshape(512, 1024)
    p["w_o"] = f(inp["w_o"]).reshape(1024, 1024)
    p["norm2_g"] = f(inp["norm2_g"]).reshape(1024)
    p["ffn_w_gate"] = f(inp["ffn_w_gate"]).reshape(1024, 2816)
    p["ffn_w_up"] = f(inp["ffn_w_up"]).reshape(1024, 2816)
    p["ffn_w_down"] = f(inp["ffn_w_down"]).reshape(2816, 1024)
    p["final_norm_g"] = f(inp["final_norm_g"]).reshape(1, 1024)
    return p


def kernel(**inputs):
    x = np.asarray(inputs["x"], dtype=np.float32)
    mem = np.asarray(inputs["mem"], dtype=np.float32)
    Bn, T, _ = x.shape
    if T not in _CACHE:
        _CACHE[T] = build(T)[0]
    nc = _CACHE[T]
    p = _prep_params(inputs)
    in_maps = []
    for i in range(Bn):
        m = dict(p)
        m["x"] = np.ascontiguousarray(x[i])
        m["mem"] = np.ascontiguousarray(mem[i])
        in_maps.append(m)
    res = run_bass_kernel_spmd(nc, in_maps, core_ids=list(range(Bn)))
    return np.stack([np.asarray(r["out"], dtype=np.float32).reshape(T, 1024) for r in res.results], axis=0)
```

```python
import math
from contextlib import ExitStack

import numpy as np
import concourse.bass as bass
import concourse.mybir as mybir
from concourse.bass_utils import run_bass_kernel_spmd

F32 = mybir.dt.float32
BF16 = mybir.dt.bfloat16
I32 = mybir.dt.int32
AF = mybir.ActivationFunctionType
ALU = mybir.AluOpType
AX = mybir.AxisListType

D = 1024
NB = 512
NCH = NB // 8
NSLOT = 4
SLOT_E = 4096
EPS = 1e-6
TWO_PI = 2.0 * math.pi


class Sem:
    def __init__(self, h, name):
        self.h = h
        self.name = name
        self.count = 0


class Buf:
    def __init__(self, name, excl=False):
        self.name = name
        self.w = None
        self.r = {}
        self.excl = excl


def _flat(xs):
    out = []
    for x in xs:
        if isinstance(x, (list, tuple)):
            out.extend(x)
        else:
            out.append(x)
    return out


class Eng:
    def __init__(self, name, sem, self_sync):
        self.name = name
        self.sem = sem
        self.ops = []
        self.seen = {}
        self.self_sync = self_sync


class Prog:
    def __init__(self, nc, stack):
        self.nc = nc
        self.stack = stack
        self.engs = {}
        self.sems = []
        self.stopped = False
        for name, ss in (("pe", False), ("act", True), ("dve", True), ("pool", True), ("sp", True)):
            self.engs[name] = Eng(name, self.new_sem("e_" + name), ss)

    def new_sem(self, name):
        h = self.stack.enter_context(self.nc.semaphore(name))
        s = Sem(h, name)
        self.sems.append(s)
        return s

    def barrier(self):
        if self.stopped:
            return
        for e in self.engs.values():
            for s in self.sems:
                if s is e.sem or s.count == 0:
                    continue
                if e.seen.get(s, 0) >= s.count:
                    continue
                e.seen[s] = s.count
                e.ops.append((lambda h, s=s, v=s.count: h.wait_ge(s.h, v)))

    def final_wait(self):
        self.stopped = False
        self.barrier()

    def op(self, eng, fn, reads=(), writes=(), dma=None):
        if self.stopped:
            return None
        e = self.engs[eng]
        reads = _flat(reads)
        writes = _flat(writes)
        deps = {}

        def add(ev):
            s, v = ev
            if deps.get(s, 0) < v:
                deps[s] = v
        for b in reads:
            if b.w is not None:
                add(b.w)
            if b.excl:
                for s, v in b.r.items():
                    if s is not e.sem:
                        add((s, v))
        for b in writes:
            if b.w is not None:
                add(b.w)
            for s, v in b.r.items():
                add((s, v))
        for s, v in deps.items():
            if s is e.sem and not e.self_sync:
                continue
            if e.seen.get(s, 0) >= v:
                continue
            e.seen[s] = v
            e.ops.append((lambda h, s=s, v=v: h.wait_ge(s.h, v)))
        if dma is not None:
            dma.count += 16
            ev = (dma, dma.count)
            e.ops.append((lambda h, fn=fn, s=dma: fn(h).then_inc(s.h, 16)))
        else:
            e.sem.count += 1
            ev = (e.sem, e.sem.count)
            e.ops.append((lambda h, fn=fn, s=e.sem: fn(h).then_inc(s.h, 1)))
        for b in writes:
            b.w = ev
            b.r = {}
        for b in reads:
            if b in writes:
                continue
            s, v = ev
            if b.r.get(s, 0) < v:
                b.r[s] = v
        return ev

    def wait_all(self, eng, bufs):
        e = self.engs[eng]
        for b in bufs:
            evs = []
            if b.w is not None:
                evs.append(b.w)
            for s, v in evs:
                if e.seen.get(s, 0) < v:
                    e.seen[s] = v
                    e.ops.append((lambda h, s=s, v=v: h.wait_ge(s.h, v)))

    def mm(self, out, lhsT, rhs, start=True, stop=True, extra_r=()):
        o, l, r = out[0], lhsT[0], rhs[0]
        self.op("pe", lambda h: h.matmul(o, lhsT=l, rhs=r, start=start, stop=stop),
                reads=[lhsT[1], rhs[1]] + list(extra_r), writes=[out[1]])

    def tr(self, out, in_, ident):
        o, i, d = out[0], in_[0], ident[0]
        self.op("pe", lambda h: h.transpose(o, i, d), reads=[in_[1], ident[1]], writes=[out[1]])

    def act(self, out, in_, func, scale=None, bias=None, accum=None):
        o, i = out[0], in_[0]
        kw = {}
        reads = [in_[1]]
        writes = [out[1]]
        if scale is not None:
            if isinstance(scale, tuple):
                kw["scale"] = scale[0]
                reads.append(scale[1])
            else:
                kw["scale"] = float(scale)
        if bias is not None:
            if isinstance(bias, tuple):
                kw["bias"] = bias[0]
                reads.append(bias[1])
            else:
                kw["bias"] = float(bias)
        if accum is not None:
            kw["accum_out"] = accum[0]
            writes.append(accum[1])
        self.op("act", lambda h: h.activation(out=o, in_=i, func=func, **kw), reads=reads, writes=writes)

    def tt(self, eng, out, in0, in1, op):
        o, a, b = out[0], in0[0], in1[0]
        self.op(eng, lambda h: h.tensor_tensor(out=o, in0=a, in1=b, op=op),
                reads=[in0[1], in1[1]], writes=[out[1]])

    def ts(self, eng, out, in0, s1, op0, s2=None, op1=None):
        o, a = out[0], in0[0]
        reads = [in0[1]]
        if isinstance(s1, tuple):
            reads.append(s1[1])
            s1 = s1[0]
        else:
            s1 = float(s1)
        if isinstance(s2, tuple):
            reads.append(s2[1])
            s2 = s2[0]
        elif s2 is not None:
            s2 = float(s2)
        if op1 is None:
            self.op(eng, lambda h: h.tensor_scalar(out=o, in0=a, scalar1=s1, scalar2=None, op0=op0),
                    reads=reads, writes=[out[1]])
        else:
            self.op(eng, lambda h: h.tensor_scalar(out=o, in0=a, scalar1=s1, scalar2=s2, op0=op0, op1=op1),
                    reads=reads, writes=[out[1]])

    def stt(self, out, in0, scalar, in1, op0, op1):
        o, a, b = out[0], in0[0], in1[0]
        reads = [in0[1], in1[1]]
        if isinstance(scalar, tuple):
            reads.append(scalar[1])
            sc = scalar[0]
        else:
            sc = float(scalar)
        self.op("dve", lambda h: h.scalar_tensor_tensor(out=o, in0=a, scalar=sc, in1=b, op0=op0, op1=op1),
                reads=reads, writes=[out[1]])

    def copy(self, eng, out, in_):
        o, i = out[0], in_[0]
        if eng == "act":
            self.op("act", lambda h: h.activation(out=o, in_=i, func=AF.Copy), reads=[in_[1]], writes=[out[1]])
        else:
            self.op(eng, lambda h: h.tensor_copy(out=o, in_=i), reads=[in_[1]], writes=[out[1]])

    def memset(self, eng, out, val):
        o = out[0]
        self.op(eng, lambda h: h.memset(o, val), writes=[out[1]])

    def recip(self, out, in_):
        o, i = out[0], in_[0]
        self.op("dve", lambda h: h.reciprocal(out=o, in_=i), reads=[in_[1]], writes=[out[1]])

    def dma(self, eng, out, in_, sem, slow=False):
        o, i = out[0], in_[0]
        if slow:
            fn = lambda h: h.dma_start(out=o, in_=i, allow_slow_non_contiguous=True)
        else:
            fn = lambda h: h.dma_start(out=o, in_=i)
        self.op(eng, fn, reads=[in_[1]], writes=[out[1]], dma=sem)

    def emit(self):
        nc = self.nc
        with nc.Block() as block:
            @block.sync
            def _(h):
                for f in self.engs["sp"].ops:
                    f(h)

            @block.tensor
            def _(h):
                for f in self.engs["pe"].ops:
                    f(h)

            @block.scalar
            def _(h):
                for f in self.engs["act"].ops:
                    f(h)

            @block.vector
            def _(h):
                for f in self.engs["dve"].ops:
                    f(h)

            @block.gpsimd
            def _(h):
                for f in self.engs["pool"].ops:
                    f(h)


class _Stop(Exception):
    pass


def build(T, debug=(), stop=None):
    NBLK = T // NB
    nc = bass.Bass("TRN2", target_bir_lowering=False)

    def din(name, shape):
        return nc.dram_tensor(name, shape, F32, kind="ExternalInput").ap()

    x = din("x", [T, D])
    mem = din("mem", [256, D])
    norm1_g = din("norm1_g", [D])
    w_in = din("w_in", [D, 5632])
    conv_w = din("conv_w", [3, 512])
    conv_proj = din("conv_proj", [512, D])
    A_re = din("ssm_A_re", [32, 64])
    A_im = din("ssm_A_im", [32, 64])
    log_dt = din("ssm_log_dt", [1, 32])
    B_re = din("ssm_B_re", [32, 64, 16])
    B_im = din("ssm_B_im", [32, 64, 16])
    C_re = din("ssm_C_re", [512, 64])
    C_im = din("ssm_C_im", [512, 64])
    ssm_D = din("ssm_D", [512])
    glu_w = din("ssm_glu_w", [512, 512])
    glu_b = din("ssm_glu_b", [512])
    ssm_proj = din("ssm_proj", [512, D])
    mem_norm_g = din("mem_norm_g", [D])
    attn_wk = din("attn_wk", [D, 512])
    attn_wv = din("attn_wv", [D, 512])
    attn_proj = din("attn_proj", [512, D])
    w_o = din("w_o", [D, D])
    norm2_g = din("norm2_g", [D])
    ffn_g = din("ffn_w_gate", [D, 2816])
    ffn_u = din("ffn_w_up", [D, 2816])
    ffn_d = din("ffn_w_down", [2816, D])
    final_g = din("final_norm_g", [1, D])
    out = nc.dram_tensor("out", [T, D], F32, kind="ExternalOutput").ap()

    NCHUNK = 34
    wscr = nc.dram_tensor("wscr", [NCHUNK, 128, SLOT_E], BF16, kind="Internal").ap()
    dbg_outs = {}

    with ExitStack() as st:
      P = Prog(nc, st)
      try:
        def chk(name):
            if stop == name:
                P.stopped = True

        def sb(name, shape, dt, stack=None):
            return (stack or st).enter_context(nc.sbuf_tensor(name, shape, dt))

        DRAM = Buf("dram_in")
        OUTB = Buf("dram_out")

        def dbg(name, ap, buf, shape):
            if name not in debug:
                return
            o = nc.dram_tensor("dbg_" + name, list(shape), ap.dtype, kind="ExternalOutput").ap()
            dbg_outs[name] = o
            s = P.new_sem("dbg_" + name)
            P.dma("sp", (o, OUTB), (ap, buf), s)

        pbank = []
        for i in range(8):
            t = st.enter_context(nc.psum_tensor("pb%d" % i, [128, 512], F32))
            pbank.append((t, Buf("pb%d" % i, excl=True)))
        bank_rr = [0]

        def bank():
            i = bank_rr[0]
            bank_rr[0] = (i + 1) % 8
            return pbank[i]

        cbuf = Buf("consts")
        csem = P.new_sem("csem")
        identf = sb("identf", [128, 128], F32)
        ident = sb("ident", [128, 128], BF16)
        ones_bf = sb("ones_bf", [128, 128], BF16)
        trix2 = sb("trix2", [128, 128], F32)
        sel2 = sb("sel2", [66, 128], F32)
        ones2 = sb("ones2", [128, 32], F32)
        tmask = sb("tmask", [128, 128], F32)
        g1b = sb("g1b", [128, D], F32)
        g2b = sb("g2b", [128, D], F32)
        convw = sb("convw", [128, 3, 4], F32)
        glub = sb("glub", [128, 4], F32)
        gF = sb("gF", [128, D], F32)
        epscol = sb("epscol", [128, 1], F32)
        ss = sb("ss", [128, 4], F32); ss_b = [Buf("ss%d" % i) for i in range(4)]
        rstd = sb("rstd", [128, 4], F32); rstd_b = [Buf("rstd%d" % i) for i in range(4)]
        CB = lambda ap: (ap, cbuf)

        def asel(ap, pattern, op, base, cm):
            P.op("pool", lambda h: h.affine_select(out=ap, in_=ap, pattern=pattern, compare_op=op, fill=0.0,
                                                   base=base, channel_multiplier=cm), reads=[cbuf], writes=[cbuf])

        P.memset("pool", CB(identf[:]), 0.0)
        P.op("pool", lambda h: h.affine_select(out=identf[:], in_=identf[:], pattern=[[-1, 128]],
                                               compare_op=ALU.not_equal, fill=1.0, base=0, channel_multiplier=1),
             reads=[cbuf], writes=[cbuf])
        P.copy("dve", CB(ident[:]), CB(identf[:]))
        P.memset("dve", CB(ones_bf[:]), 1.0)
        P.memset("dve", CB(epscol[:]), EPS)
        P.memset("pool", CB(trix2[:]), 1.0)
        asel(trix2[:], [[1, 128]], ALU.is_ge, -1, -1)
        P.memset("pool", CB(trix2[0:64, 64:128]), 0.0)
        P.memset("pool", CB(sel2[:]), 1.0)
        for base in (0, 32, 64):
            asel(sel2[base:base + 2, :], [[1, 128]], ALU.is_ge, 0, -64)
            asel(sel2[base:base + 2, :], [[-1, 128]], ALU.is_ge, 63, 64)
        P.memset("pool", CB(ones2[:]), 0.0)
        P.memset("pool", CB(ones2[:, 0:2]), 1.0)
        asel(ones2[:, 0:2], [[-64, 2]], ALU.is_ge, 0, 1)
        asel(ones2[:, 0:2], [[64, 2]], ALU.is_ge, 63, -1)
        trix2_bf = sb("trix2_bf", [128, 128], BF16)
        sel2_bf = sb("sel2_bf", [66, 128], BF16)
        ones2_bf = sb("ones2_bf", [128, 32], BF16)
        P.copy("dve", CB(trix2_bf[:]), CB(trix2[:]))
        P.copy("dve", CB(sel2_bf[:]), CB(sel2[:]))
        P.copy("dve", CB(ones2_bf[:]), CB(ones2[:]))
        P.memset("pool", CB(tmask[:]), 1.0)
        asel(tmask[:].rearrange("p (t o) -> p t o", t=8), [[16, 8], [0, 16]], ALU.is_ge, 15, -1)
        P.dma("sp", CB(g1b[:]), (norm1_g.unsqueeze(0).partition_broadcast(128), DRAM), csem)
        P.dma("sp", CB(g2b[:]), (norm2_g.unsqueeze(0).partition_broadcast(128), DRAM), csem)
        for k in range(3):
            P.dma("sp", CB(convw[:, k, :]), (conv_w[k].rearrange("(ft p) -> p ft", p=128), DRAM), csem, slow=True)
        P.dma("sp", CB(glub[:]), (glu_b.rearrange("(ft p) -> p ft", p=128), DRAM), csem, slow=True)
        P.dma("sp", CB(gF[:]), (final_g.partition_broadcast(128), DRAM), csem)

        chk('consts')
        slots = []
        for i in range(NSLOT):
            t = sb("wslot%d" % i, [128, SLOT_E], BF16)
            slots.append((t, Buf("wslot%d" % i), P.new_sem("wld%d" % i), P.new_sem("wst%d" % i)))
        scr_buf = [Buf("scr%d" % i) for i in range(NCHUNK)]

        def wsrc(w, p=128):
            return w.rearrange("(kt p) n -> p kt n", p=p)

        chunk_parts = {}
        for j in range(11):
            chunk_parts["win%d" % j] = [(lambda s: s[:, 0:4096].rearrange("p (kt n) -> p kt n", kt=8),
                                         wsrc(w_in)[:, :, j * 512:(j + 1) * 512])]
        chunk_parts["cp"] = [(lambda s: s[:, 0:4096].rearrange("p (kt n) -> p kt n", kt=4), wsrc(conv_proj))]
        chunk_parts["sp"] = [(lambda s: s[:, 0:4096].rearrange("p (kt n) -> p kt n", kt=4), wsrc(ssm_proj))]
        chunk_parts["ap"] = [(lambda s: s[:, 0:4096].rearrange("p (kt n) -> p kt n", kt=4), wsrc(attn_proj))]
        chunk_parts["glu"] = [(lambda s: s[:, 0:2048].rearrange("p (kt n) -> p kt n", kt=4), wsrc(glu_w))]
        for hh in range(2):
            chunk_parts["wo%d" % hh] = [(lambda s: s[:, 0:4096].rearrange("p (kt n) -> p kt n", kt=8),
                                         wsrc(w_o)[:, :, hh * 512:(hh + 1) * 512])]
        for j in range(11):
            chunk_parts["ffn%d" % j] = [
                (lambda s: s[:, 0:2048].rearrange("p (kt n) -> p kt n", kt=8), wsrc(ffn_g)[:, :, j * 256:(j + 1) * 256]),
                (lambda s: s[:, 2048:4096].rearrange("p (kt n) -> p kt n", kt=8), wsrc(ffn_u)[:, :, j * 256:(j + 1) * 256]),
            ]
        for r in range(6):
            nk = 4 if r < 5 else 2
            chunk_parts["dn%d" % r] = [(lambda s, nk=nk: s[:, 0:nk * 1024].rearrange("p (kt n) -> p kt n", kt=nk),
                                        wsrc(ffn_d)[:, 4 * r:4 * r + nk, :])]
        chunk_ids = list(chunk_parts.keys())
        chunk_idx = {k: i for i, k in enumerate(chunk_ids)}
        assert len(chunk_ids) == NCHUNK

        block_sched = (["win3", "win4", "win1", "win2", "win0", "glu", "cp", "win5", "win6",
                        "sp", "win7", "win8", "ap", "win9", "win10", "wo0", "wo1"]
                       + ["ffn%d" % j for j in range(11)] + ["dn%d" % r for r in range(6)])
        assert len(block_sched) == NCHUNK
        sched = [(b, c) for b in range(NBLK) for c in block_sched]
        wstate = {"issued": 0, "use": 0}

        def w_issue(i):
            b, c = sched[i]
            t, buf, ld, stsem = slots[i % NSLOT]
            ci = chunk_idx[c]
            if b == 0:
                for dstf, src in chunk_parts[c]:
                    P.dma("pool", (dstf(t), buf), (src, DRAM), ld)
                if NBLK > 1:
                    P.dma("sp", (wscr[ci], scr_buf[ci]), (t[:, :], buf), stsem)
            else:
                P.dma("sp", (t[:, :], buf), (wscr[ci], scr_buf[ci]), ld)

        wdone = [False] * len(sched)
        wcur = {}

        def w_pump(limit):
            while wstate["issued"] < min(len(sched), limit):
                n = wstate["issued"]
                if n >= NSLOT and not wdone[n - NSLOT]:
                    break
                w_issue(n)
                wstate["issued"] += 1

        def w_get(c):
            i = wstate["use"]
            assert sched[i][1] == c, (sched[i], c)
            w_pump(len(sched))
            assert wstate["issued"] > i, ("weight slot plan deadlock", i, c)
            wstate["use"] = i + 1
            wcur[c] = i
            t, buf, _, _ = slots[i % NSLOT]
            return t, buf

        def w_rel(c):
            i = wcur.pop(c)
            wdone[i] = True
            w_pump(len(sched))

        xbuf = []
        for i in range(2):
            t = sb("x_tm%d" % i, [128, 4, D], F32)
            xbuf.append((t, Buf("x_tm%d" % i), P.new_sem("xld%d" % i), P.new_sem("xst%d" % i)))
        hT = sb("hT", [128, 8, NB], BF16); hT_tt = [Buf("hT%d" % i) for i in range(4)]; hT_b = tuple(hT_tt)
        conv_preT = sb("conv_preT", [128, 4, NB], BF16); conv_pre_b = Buf("conv_preT")
        ys2T = sb("ys2T", [128, 4, NB], BF16); ys2_b = Buf("ys2T")
        oT = sb("oT", [128, 4, NB], BF16); oT_b = Buf("oT")
        cv = sb("cv", [128, 4, NB + 2], F32); cv_b = [Buf("cv%d" % i) for i in range(4)]
        tmpA = [(sb("tmpA%d" % i, [128, NB], F32), Buf("tmpA%d" % i)) for i in range(2)]
        LTT = sb("LTT", [128, 32, 256], BF16); LTT_b = Buf("LTT")
        VV = sb("VV", [128, 32, 128], BF16); VV_b = Buf("VV")
        S2re = sb("S2re", [128, 1024], F32); S2im = sb("S2im", [128, 1024], F32)
        U2re = sb("U2re", [128, 1024], F32); U2im = sb("U2im", [128, 1024], F32)
        tab_b = Buf("tables")
        HstA = sb("HstA", [66, 512], F32); HstB = sb("HstB", [2, 512], F32); Hst_b = [Buf("Hst%d" % q) for q in range(4)]
        HbfA = sb("HbfA", [66, 512], BF16); HbfB = sb("HbfB", [2, 512], BF16)
        U64A = sb("U64A", [66, 2, 256], F32); U64B = sb("U64B", [2, 2, 256], F32)
        kT = sb("kT", [128, 4, 256], BF16); kT_b = Buf("kT")
        Vm = sb("Vm", [128, 2, 512], BF16); Vm_b = Buf("Vm")

        def hst_rows(q):
            if q < 3:
                return (HstA[32 * q:32 * q + 2, :], U64A[32 * q:32 * q + 2, :, :], sel2_bf[32 * q:32 * q + 2, :],
                        ident[32 * q:32 * q + 2, 32 * q:32 * q + 32], 32 * q, HbfA[32 * q:32 * q + 2, :])
            return HstB[0:2, :], U64B[0:2, :, :], sel2_bf[0:2, :], ident[0:2, 0:32], 0, HbfB[0:2, :]

        def sin_turns(out, turns, tmp_i, tmp_f):
            P.copy("dve", tmp_i, turns)
            P.copy("dve", tmp_f, tmp_i)
            P.tt("dve", turns, turns, tmp_f, ALU.subtract)
            P.ts("dve", tmp_f, turns, 0.5, ALU.is_gt)
            P.tt("dve", turns, turns, tmp_f, ALU.subtract)
            P.ts("dve", tmp_f, turns, -0.5, ALU.is_lt)
            P.tt("dve", turns, turns, tmp_f, ALU.add)
            P.act(out, turns, AF.Sin, scale=TWO_PI)

        def norm_act(xt, xb, htm, htm_b, tt):
            P.act((htm[:, tt, :], htm_b), (xt[:, tt, :], xb), AF.Square, accum=(ss[:, tt:tt + 1], ss_b[tt]))
            P.act((rstd[:, tt:tt + 1], rstd_b[tt]), (ss[:, tt:tt + 1], ss_b[tt]), AF.Sqrt, scale=1.0 / D, bias=(epscol[:, 0:1], cbuf))

        def norm_dve(xt, xb, gb, htm, htm_b, tt):
            P.recip((rstd[:, tt:tt + 1], rstd_b[tt]), (rstd[:, tt:tt + 1], rstd_b[tt]))
            P.stt((htm[:, tt, :], htm_b), (xt[:, tt, :], xb), (rstd[:, tt:tt + 1], rstd_b[tt]), (gb[:], cbuf), ALU.mult, ALU.mult)

        def norm_stats(xt, xb, gb, htm, htm_b, tt):
            norm_act(xt, xb, htm, htm_b, tt)
            norm_dve(xt, xb, gb, htm, htm_b, tt)

        def norm_transposes(htm, htm_b, dstT, dstT_b, tt, eng=None):
            pt, pb = bank()
            pv = pt[:].bitcast(BF16)
            for kt in range(8):
                P.tr((pv[:, kt * 128:(kt + 1) * 128], pb), (htm[:, tt, kt * 128:(kt + 1) * 128], htm_b), (ident[:], cbuf))
            P.copy(eng or ("act" if tt % 2 == 0 else "dve"), (dstT[:, :, tt * 128:(tt + 1) * 128], dstT_b[tt] if isinstance(dstT_b, (list, tuple)) else dstT_b),
                   (pv[:, 0:1024].rearrange("p (k n) -> p k n", k=8), pb))

        with ExitStack() as pst:
            psem = P.new_sem("psem")
            memt = sb("memt", [128, 2, D], F32, pst); memt_b = Buf("memt")
            memh = sb("memh", [128, 2, D], BF16, pst); memh_b = Buf("memh")
            memT = sb("memT", [128, 8, 256], BF16, pst); memT_b = Buf("memT")
            P.dma("sp", (memt[:], memt_b), (mem.rearrange("(mt p) d -> p mt d", p=128), DRAM), psem)
            gmb = sb("gmb", [128, D], F32, pst)
            P.dma("sp", CB(gmb[:]), (mem_norm_g.unsqueeze(0).partition_broadcast(128), DRAM), psem)
            for tt in range(2):
                norm_stats(memt, memt_b, gmb, memh, memh_b, tt)
            for tt in range(2):
                norm_transposes(memh, memh_b, memT, memT_b, tt)
            wk_t, wk_b = slots[0][0], slots[0][1]
            wv_t, wv_b = slots[1][0], slots[1][1]
            wkv = wk_t[:, 0:4096].rearrange("p (kt n) -> p kt n", kt=8)
            wvv = wv_t[:, 0:4096].rearrange("p (kt n) -> p kt n", kt=8)
            P.dma("pool", (wkv, wk_b), (wsrc(attn_wk), DRAM), slots[0][2])
            P.dma("pool", (wvv, wv_b), (wsrc(attn_wv), DRAM), slots[1][2])
            for hh in range(4):
                pt, pb = bank()
                for kt in range(8):
                    P.mm((pt[:, 0:256], pb), (wkv[:, kt, hh * 128:(hh + 1) * 128], wk_b), (memT[:, kt, :], memT_b),
                         start=(kt == 0), stop=(kt == 7))
                P.copy("act", (kT[:, hh, :], kT_b), (pt[:, 0:256], pb))
            for mt in range(2):
                pt, pb = bank()
                for kt in range(8):
                    P.mm((pt[:, :], pb), (memT[:, kt, mt * 128:(mt + 1) * 128], memT_b), (wvv[:, kt, :], wv_b),
                         start=(kt == 0), stop=(kt == 7))
                P.copy("dve", (Vm[:, mt, :], Vm_b), (pt[:, :], pb))
            P.barrier()

        chk('proA')
        with ExitStack() as pst:
            s5sem = P.new_sem("s5sem")
            pb_ = Buf("s5pro")
            PB = lambda ap: (ap, pb_)
            a_nat = sb("a_nat", [32, 2, 128], F32, pst)
            for half in range(2):
                P.dma("sp", PB(a_nat[:, 0, half * 64:(half + 1) * 64]), (A_re, DRAM), s5sem)
                P.dma("sp", PB(a_nat[:, 1, half * 64:(half + 1) * 64]), (A_im, DRAM), s5sem)
            aT = sb("aT", [128, 2, 32], F32, pst)
            for k in range(2):
                pt, pb = bank()
                P.tr((pt[:, 0:32], pb), PB(a_nat[:, k, :]), (identf[0:32, 0:32], cbuf))
                P.copy("dve", PB(aT[:, k, :]), (pt[:, 0:32], pb))
            dtb = sb("dtb", [128, 32], F32, pst)
            P.dma("sp", PB(dtb[:]), (log_dt.partition_broadcast(128), DRAM), s5sem)
            P.act(PB(dtb[:]), PB(dtb[:]), AF.Exp)
            chk('pb0')
            rho = sb("rho", [128, 32], F32, pst)
            thetat = sb("thetat", [128, 32], F32, pst)
            P.tt("dve", PB(rho[:]), PB(aT[:, 0, :]), PB(dtb[:]), ALU.mult)
            P.tt("dve", PB(thetat[:]), PB(aT[:, 1, :]), PB(dtb[:]), ALU.mult)
            P.ts("dve", PB(thetat[:]), PB(thetat[:]), 1.0 / TWO_PI, ALU.mult)
            chk('pb1')
            sm = [sb("sm%d" % i, [128, 32, 8], F32, pst) for i in range(8)]
            smi = sb("smi", [128, 32, 8], I32, pst)
            e1 = sm[0][:, :, 0]; c1 = sm[1][:, :, 0]; s1_ = sm[2][:, :, 0]; sh = sm[3][:, :, 0]
            t0_ = sm[4][:, :, 0]; t1_ = sm[5][:, :, 0]; t2_ = sm[6][:, :, 0]; t3_ = sm[7][:, :, 0]
            ti0 = smi[:, :, 0]
            P.act(PB(e1), PB(rho[:]), AF.Exp)
            P.copy("dve", PB(t0_), PB(thetat[:]))
            sin_turns(PB(s1_), PB(t0_), PB(ti0), PB(t1_))
            P.ts("dve", PB(t0_), PB(thetat[:]), 0.5, ALU.mult)
            sin_turns(PB(sh), PB(t0_), PB(ti0), PB(t1_))
            P.tt("dve", PB(t0_), PB(sh), PB(sh), ALU.mult)
            P.ts("dve", PB(t0_), PB(t0_), -2.0, ALU.mult)
            P.ts("dve", PB(c1), PB(t0_), 1.0, ALU.add)
            P.ts("dve", PB(t1_), PB(e1), -1.0, ALU.add)
            P.tt("dve", PB(t1_), PB(t1_), PB(c1), ALU.mult)
            P.tt("dve", PB(t1_), PB(t1_), PB(t0_), ALU.add)
            P.tt("dve", PB(t2_), PB(e1), PB(s1_), ALU.mult)
            P.tt("dve", PB(t0_), PB(aT[:, 0, :]), PB(aT[:, 0, :]), ALU.mult)
            P.tt("dve", PB(t3_), PB(aT[:, 1, :]), PB(aT[:, 1, :]), ALU.mult)
            P.tt("dve", PB(t0_), PB(t0_), PB(t3_), ALU.add)
            P.recip(PB(t0_), PB(t0_))
            kap = sb("kap", [128, 2, 32], F32, pst)
            P.tt("dve", PB(t3_), PB(t1_), PB(aT[:, 0, :]), ALU.mult)
            P.tt("dve", PB(c1), PB(t2_), PB(aT[:, 1, :]), ALU.mult)
            P.tt("dve", PB(t3_), PB(t3_), PB(c1), ALU.add)
            P.tt("dve", PB(kap[:, 0, :]), PB(t3_), PB(t0_), ALU.mult)
            P.tt("dve", PB(t3_), PB(t2_), PB(aT[:, 0, :]), ALU.mult)
            P.tt("dve", PB(c1), PB(t1_), PB(aT[:, 1, :]), ALU.mult)
            P.tt("dve", PB(t3_), PB(t3_), PB(c1), ALU.subtract)
            P.tt("dve", PB(kap[:, 1, :]), PB(t3_), PB(t0_), ALU.mult)

            chk('pb2')
            Bb = sb("Bb", [128, 2, 32, 16], F32, pst)
            Cz = sb("Cz", [128, 2, 512], F32, pst)
            pst2 = ExitStack()
            Bz = sb("Bz", [128, 2, 32, 16], F32, pst2)
            for half in range(2):
                for q in range(4):
                    P.dma("sp", PB(Bz[half * 64:(half + 1) * 64, 0, q * 8:(q + 1) * 8, :]),
                          (B_re[q * 8:(q + 1) * 8].rearrange("g p i -> p g i"), DRAM), s5sem, slow=True)
                    P.dma("sp", PB(Bz[half * 64:(half + 1) * 64, 1, q * 8:(q + 1) * 8, :]),
                          (B_im[q * 8:(q + 1) * 8].rearrange("g p i -> p g i"), DRAM), s5sem, slow=True)
            bt0 = sb("bt0", [128, 32, 16], F32, pst2)
            bt1 = sb("bt1", [128, 32, 16], F32, pst2)
            kre_b = kap[:, 0, :].unsqueeze(2).to_broadcast([128, 32, 16])
            kim_b = kap[:, 1, :].unsqueeze(2).to_broadcast([128, 32, 16])
            P.tt("dve", PB(bt0[:]), PB(Bz[:, 0]), PB(kre_b), ALU.mult)
            P.tt("dve", PB(bt1[:]), PB(Bz[:, 1]), PB(kim_b), ALU.mult)
            P.tt("dve", PB(Bb[:, 0]), PB(bt0[:]), PB(bt1[:]), ALU.subtract)
            P.tt("dve", PB(bt0[:]), PB(Bz[:, 1]), PB(kre_b), ALU.mult)
            P.tt("dve", PB(bt1[:]), PB(Bz[:, 0]), PB(kim_b), ALU.mult)
            P.tt("dve", PB(Bb[:, 1]), PB(bt0[:]), PB(bt1[:]), ALU.add)

            chk('pb3')
            c_nat = sb("c_nat", [128, 2, 4, 128], F32, pst2)
            for half in range(2):
                P.dma("sp", PB(c_nat[:, 0, :, half * 64:(half + 1) * 64]), (C_re.rearrange("(t r) p -> r t p", r=128), DRAM), s5sem)
                P.dma("sp", PB(c_nat[:, 1, :, half * 64:(half + 1) * 64]), (C_im.rearrange("(t r) p -> r t p", r=128), DRAM), s5sem)
            for k in range(2):
                pt, pb = bank()
                for tl in range(4):
                    P.tr((pt[:, tl * 128:(tl + 1) * 128], pb), PB(c_nat[:, k, tl, :]), (identf[:], cbuf))
                P.copy("dve", PB(Cz[:, k, :]), (pt[:, :], pb))

            chk('pb4')
            pst2.close()
            phcol = sb("phcol", [128, 1], F32, pst)
            sgcol = sb("sgcol", [128, 1], F32, pst)
            P.memset("dve", PB(phcol[:]), 0.0)
            P.memset("dve", PB(phcol[64:128, :]), -0.25)
            P.memset("dve", PB(sgcol[:]), 1.0)
            P.memset("dve", PB(sgcol[64:128, :]), -1.0)
            evec = sb("evec", [128, 8], F32, pst)
            P.op("pool", lambda h: h.iota(evec[:], pattern=[[1, 8]], base=0, channel_multiplier=0,
                                          allow_small_or_imprecise_dtypes=True), writes=[pb_])
            GQ = 8
            big0 = sb("big0", [128, GQ, 8, 16], F32, pst)
            big1 = sb("big1", [128, GQ, 8, 16], F32, pst)
            Lst = sb("Lst", [128, GQ, 128], F32, pst)
            Rst = sb("Rst", [128, GQ, 128], F32, pst)
            Dcol = sb("Dcol", [128, 32], F32, pst)
            Dcol_b = Buf("Dcol")
            for s_ in range(8):
                P.dma("sp", (Dcol[s_ * 16:(s_ + 1) * 16, :], Dcol_b), (ssm_D.rearrange("(g i) -> i g", i=16), DRAM), s5sem, slow=True)
            evs = sb("evs", [128, 8], F32, pst)

            def build_stack(g0, Zre, Zim, eoff, esign, use_sg):
                P.ts("dve", PB(evs[:]), PB(evec[:]), float(esign), ALU.mult, float(eoff), ALU.add)
                evb = evs[:].unsqueeze(1).to_broadcast([128, GQ, 8])
                thb = thetat[:, g0:g0 + GQ].unsqueeze(2).to_broadcast([128, GQ, 8])
                rhb = rho[:, g0:g0 + GQ].unsqueeze(2).to_broadcast([128, GQ, 8])
                ang = sm[1][:, 0:GQ, :]; mag = sm[2][:, 0:GQ, :]; cosA = sm[3][:, 0:GQ, :]; sinA = sm[4][:, 0:GQ, :]
                tf = sm[5][:, 0:GQ, :]; ang2 = sm[6][:, 0:GQ, :]; ti = smi[:, 0:GQ, :]
                P.tt("dve", PB(ang), PB(thb), PB(evb), ALU.mult)
                P.ts("dve", PB(ang), PB(ang), (phcol[:, 0:1], pb_), ALU.add)
                P.ts("dve", PB(ang2), PB(ang), 0.25, ALU.add)
                sin_turns(PB(sinA), PB(ang), PB(ti), PB(tf))
                sin_turns(PB(cosA), PB(ang2), PB(ti), PB(tf))
                P.tt("dve", PB(mag), PB(rhb), PB(evb), ALU.mult)
                P.act(PB(mag), PB(mag), AF.Exp)
                if use_sg:
                    P.ts("dve", PB(mag), PB(mag), (sgcol[:, 0:1], pb_), ALU.mult)
                P.tt("dve", PB(cosA), PB(cosA), PB(mag), ALU.mult)
                P.tt("dve", PB(sinA), PB(sinA), PB(mag), ALU.mult)
                cb = cosA.unsqueeze(3).to_broadcast([128, GQ, 8, 16])
                sbb = sinA.unsqueeze(3).to_broadcast([128, GQ, 8, 16])
                zr = Zre[:, g0:g0 + GQ, :].unsqueeze(2).to_broadcast([128, GQ, 8, 16])
                zi = Zim[:, g0:g0 + GQ, :].unsqueeze(2).to_broadcast([128, GQ, 8, 16])
                P.tt("dve", PB(big0[:]), PB(zr), PB(cb), ALU.mult)
                P.tt("dve", PB(big1[:]), PB(zi), PB(sbb), ALU.mult)
                P.tt("dve", PB(big0[:]), PB(big0[:]), PB(big1[:]), ALU.subtract)

            chk('pb5')
            Czre = Cz[:, 0, :].rearrange("p (g o) -> p g o", g=32)
            Czim = Cz[:, 1, :].rearrange("p (g o) -> p g o", g=32)
            for g0 in range(0, 32, GQ):
                build_stack(g0, Bb[:, 0], Bb[:, 1], 7.0, -1.0, False)
                chk('pb6')
                P.copy("dve", PB(Lst[:].rearrange("p g (s i) -> p g s i", s=8)), PB(big0[:]))
                build_stack(g0, Czre, Czim, -7.0, 1.0, True)
                P.copy("dve", PB(Rst[:].rearrange("p g (s i) -> p g s i", s=8)), PB(big0[:]))
                build_stack(g0, Czre, Czim, 1.0, 1.0, True)
                P.copy("dve", (VV[:, g0:g0 + GQ, :].rearrange("p g (s i) -> p g s i", s=8), VV_b), PB(big0[:]))
                chk('pb7')
                for gl in range(GQ):
                    g = g0 + gl
                    if gl == 1:
                        chk('pb8')
                    pt, pb = bank()
                    import os
                    DIS = os.environ.get("DIS", "")
                    if "tr" not in DIS:
                        P.tr((pt[:, 0:128], pb), PB(Lst[:, gl, :]), (identf[:], cbuf))
                    if "mm" not in DIS:
                        P.mm((pt[:, 128:256], pb), PB(Lst[:, gl, :]), PB(Rst[:, gl, :]))
                    if "cp" not in DIS:
                        P.copy("act", (LTT[:, g, 0:128], LTT_b), (pt[:, 0:128], pb))
                    tm, tmb = tmpA[g % 2]
                    if "tt" not in DIS:
                        _o, _a, _b = tm[:, 0:128], pt[:, 128:256], tmask[:]
                        P.op("dve", lambda h, _o=_o, _a=_a, _b=_b: h.tensor_tensor(out=_o, in0=_a, in1=_b, op=ALU.mult),
                             reads=[pb, cbuf, LTT_b], writes=[tmb])
                    if "st" not in DIS:
                        P.ts("dve", (tm[:, 128:256], tmb), (identf[:], cbuf), (Dcol[:, g:g + 1], Dcol_b), ALU.mult)
                        P.tt("dve", (LTT[:, g, 128:256], LTT_b), (tm[:, 0:128], tmb), (tm[:, 128:256], tmb), ALU.add)
            P.barrier()

        chk('proB')
        with ExitStack() as pst:
            s5sem2 = P.new_sem("s5sem2")
            pb_ = Buf("s5pro2")
            PB = lambda ap: (ap, pb_)

            def make_tables(npart, arow, dtrow, ccols, outs, ri, rf, rrow, trow, ta, ta2, tmag, ngl):
                n = ngl * 64
                P.act(PB(dtrow), PB(dtrow), AF.Exp)
                dtb2 = dtrow.unsqueeze(2).to_broadcast([npart, ngl, 64])
                P.tt("dve", PB(rrow.rearrange("c (g p) -> c g p", g=ngl)), PB(arow[:, 0, :].rearrange("c (g p) -> c g p", g=ngl)), PB(dtb2), ALU.mult)
                P.tt("dve", PB(trow.rearrange("c (g p) -> c g p", g=ngl)), PB(arow[:, 1, :].rearrange("c (g p) -> c g p", g=ngl)), PB(dtb2), ALU.mult)
                P.ts("dve", PB(trow), PB(trow), 1.0 / TWO_PI, ALU.mult)
                P.copy("dve", PB(ri), PB(trow))
                P.copy("dve", PB(rf), PB(ri))
                P.tt("dve", PB(trow), PB(trow), PB(rf), ALU.subtract)
                for (col, dre, dim_) in outs:
                    P.ts("dve", PB(ta), PB(trow), (col, pb_), ALU.mult)
                    P.ts("dve", PB(ta2), PB(ta), 0.25, ALU.add)
                    P.act(PB(tmag), PB(rrow), AF.Exp, scale=(col, pb_))
                    sin_turns(dim_, PB(ta), PB(ri), PB(rf))
                    sin_turns(dre, PB(ta2), PB(ri), PB(rf))
                    P.tt("dve", dim_, dim_, PB(tmag), ALU.mult)
                    P.tt("dve", dre, dre, PB(tmag), ALU.mult)

            arow2 = sb("arow2", [128, 2, 1024], F32, pst)
            dtrow2 = sb("dtrow2", [128, 16], F32, pst)
            A_re_h = A_re.rearrange("(h g) p -> h (g p)", h=2)
            A_im_h = A_im.rearrange("(h g) p -> h (g p)", h=2)
            ldt_h = log_dt.rearrange("o (h g) -> (o h) g", h=2)
            for h in range(2):
                P.dma("sp", PB(arow2[64 * h:64 * h + 64, 0, :]), (A_re_h[h:h + 1, :].partition_broadcast(64), DRAM), s5sem2)
                P.dma("sp", PB(arow2[64 * h:64 * h + 64, 1, :]), (A_im_h[h:h + 1, :].partition_broadcast(64), DRAM), s5sem2)
                P.dma("sp", PB(dtrow2[64 * h:64 * h + 64, :]), (ldt_h[h:h + 1, :].partition_broadcast(64), DRAM), s5sem2)
            ccol = sb("ccol", [128, 2], F32, pst)
            for h in range(2):
                P.op("pool", lambda hh, h=h: hh.iota(ccol[64 * h:64 * h + 64, 0:1], pattern=[[0, 1]], base=0, channel_multiplier=8,
                                                      allow_small_or_imprecise_dtypes=True), writes=[pb_])
            P.ts("dve", PB(ccol[:, 1:2]), PB(ccol[:, 0:1]), -1.0, ALU.mult, -8.0, ALU.add)
            ri = sb("ri", [128, 1024], I32, pst); rf = sb("rf", [128, 1024], F32, pst)
            rrow = sb("rrow", [128, 1024], F32, pst); trow = sb("trow", [128, 1024], F32, pst)
            ta = sb("ta", [128, 1024], F32, pst); ta2 = sb("ta2", [128, 1024], F32, pst); tmag = sb("tmag", [128, 1024], F32, pst)
            make_tables(128, arow2[:], dtrow2[:], None,
                        [(ccol[:, 0:1], (U2re[:], tab_b), (U2im[:], tab_b)), (ccol[:, 1:2], (S2re[:], tab_b), (S2im[:], tab_b))],
                        ri[:], rf[:], rrow[:], trow[:], ta[:], ta2[:], tmag[:], 16)
            arow64 = sb("arow64", [66, 2, 256], F32, pst)
            dtrow64 = sb("dtrow64", [66, 4], F32, pst)
            arowB = sb("arowB", [2, 2, 256], F32, pst)
            dtrowB = sb("dtrowB", [2, 4], F32, pst)
            P.memset("dve", PB(arow64[:]), 0.0)
            P.memset("dve", PB(dtrow64[:]), 0.0)
            A_re_hq = A_re.rearrange("(h q g) p -> h q (g p)", h=2, q=4)
            A_im_hq = A_im.rearrange("(h q g) p -> h q (g p)", h=2, q=4)
            ldt_hq = log_dt.rearrange("o (h q g) -> (o h) q g", h=2, q=4)
            for q in range(4):
                if q < 3:
                    dst_a, dst_d, base = arow64, dtrow64, 32 * q
                else:
                    dst_a, dst_d, base = arowB, dtrowB, 0
                P.dma("sp", PB(dst_a[base:base + 2, 0, :]), (A_re_hq[:, q, :], DRAM), s5sem2)
                P.dma("sp", PB(dst_a[base:base + 2, 1, :]), (A_im_hq[:, q, :], DRAM), s5sem2)
                P.dma("sp", PB(dst_d[base:base + 2, :]), (ldt_hq[:, q, :], DRAM), s5sem2)
            c64 = sb("c64", [66, 1], F32, pst)
            P.memset("dve", PB(c64[:]), 512.0)
            make_tables(66, arow64[:], dtrow64[:], None,
                        [(c64[:, 0:1], (U64A[:, 0, :], tab_b), (U64A[:, 1, :], tab_b))],
                        ri[0:66, 0:256], rf[0:66, 0:256], rrow[0:66, 0:256], trow[0:66, 0:256], ta[0:66, 0:256], ta2[0:66, 0:256], tmag[0:66, 0:256], 4)
            make_tables(2, arowB[:], dtrowB[:], None,
                        [(c64[0:2, 0:1], (U64B[:, 0, :], tab_b), (U64B[:, 1, :], tab_b))],
                        ri[0:2, 256:512], rf[0:2, 256:512], rrow[0:2, 256:512], trow[0:2, 256:512], ta[0:2, 256:512], ta2[0:2, 256:512], tmag[0:2, 256:512], 4)
            P.memset("dve", (HstA[:], Hst_b[0]), 0.0)
            P.memset("dve", (HstB[:], Hst_b[3]), 0.0)
            P.memset("dve", (HbfA[:], Hst_b[0]), 0.0)
            P.memset("dve", (HbfB[:], Hst_b[3]), 0.0)
            for ft in range(4):
                P.memset("pool", (cv[:, ft, 0:2], cv_b[ft]), 0.0)
            P.barrier()

        chk('proC')
        dbg("LTT", LTT[:], LTT_b, [128, 32, 256])
        dbg("VV", VV[:], VV_b, [128, 32, 128])
        dbg("U2re", U2re[:], tab_b, [128, 1024])
        dbg("U2im", U2im[:], tab_b, [128, 1024])
        dbg("S2re", S2re[:], tab_b, [128, 1024])
        dbg("S2im", S2im[:], tab_b, [128, 1024])
        dbg("U64A", U64A[:], tab_b, [66, 2, 256])
        dbg("U64B", U64B[:], tab_b, [2, 2, 256])
        dbg("kT", kT[:], kT_b, [128, 4, 256])
        dbg("Vm", Vm[:], Vm_b, [128, 2, 512])

        ARENA_B = 43008
        arena = sb("arena", [128, ARENA_B // 2], BF16)
        arena_bufs = []

        def aview(off, nbytes, dt, name):
            assert off + nbytes <= ARENA_B
            v = arena[:, off // 2:(off + nbytes) // 2]
            if dt == F32:
                v = v.bitcast(F32)
            b = Buf(name)
            b.rng = (off, off + nbytes)
            arena_bufs.append(b)
            return v, b

        def phase(bufs):
            for nb in bufs:
                ev = {}
                for ab in arena_bufs:
                    if ab is not nb and (ab.rng[1] <= nb.rng[0] or nb.rng[1] <= ab.rng[0]):
                        continue
                    if ab.w is not None:
                        s_, v = ab.w
                        if ev.get(s_, 0) < v:
                            ev[s_] = v
                    for s_, v in ab.r.items():
                        if ev.get(s_, 0) < v:
                            ev[s_] = v
                nb.w = None
                nb.r = ev

        u_tm_f, u_tm_b = aview(0, 4096, BF16, "u_tm"); u_tm = u_tm_f.rearrange("p (g s i) -> p g s i", g=16, s=8)
        yg_f, yg_b = aview(4096, 4096, BF16, "yg"); yg = yg_f.rearrange("p (t c) -> p t c", t=8)
        Ut_f, Ut_b = aview(8192, 4096, BF16, "Ut"); Ut = Ut_f.rearrange("p (g c) -> p g c", g=32)
        HT_f, HT_b = aview(12288, 4096, BF16, "HT"); HT = HT_f.rearrange("p (g c) -> p g c", g=32)
        wk1, wk1_b = aview(16384, 2048, F32, "wk1")
        wk2, wk2_b = aview(18432, 2048, F32, "wk2")
        hs, hs_b = aview(20480, 1024, BF16, "hs")
        hprev, hprev_b = aview(21504, 1024, BF16, "hprev")
        hrow, hrow_b = aview(22528, 2048, F32, "hrow")
        hrow2, hrow2_b = aview(24576, 2048, F32, "hrow2")
        ysT_f, ysT_b = aview(26624, 4096, BF16, "ysT"); ysT = ysT_f.rearrange("p (k n) -> p k n", k=4)
        qT_f, qT_b = aview(30720, 4096, BF16, "qT"); qT = qT_f.rearrange("p (k n) -> p k n", k=4)
        eT = []
        for i in range(2):
            v, b_ = aview(34816 + 2048 * i, 2048, BF16, "eT%d" % i)
            eT.append((v.rearrange("p (k n) -> p k n", k=2), b_))
        rec, rec_b = aview(38912, 2048, F32, "rec")
        tmpB = [aview(40960, 2048, F32, "tmpB0"), aview(36864, 2048, F32, "tmpB1")]
        h_tm_f, h_tm_b = aview(30720, 8192, BF16, "h_tm"); h_tm = h_tm_f.rearrange("p (t d) -> p t d", t=4)
        junk_f, junk_b = aview(22528, 8192, BF16, "junk"); junk = junk_f.rearrange("p (t d) -> p t d", t=4)
        merged_f, merged_b = aview(0, 16384, F32, "merged"); merged = merged_f.rearrange("p (k n) -> p k n", k=8)
        mergedT_f, mergedT_b = aview(16384, 8192, BF16, "mergedT"); mergedT = mergedT_f.rearrange("p (k n) -> p k n", k=8)
        actT_f, actT_b = aview(0, 22528, BF16, "actT"); actT = actT_f.rearrange("p (k n) -> p k n", k=22)
        S5_BUFS = [u_tm_b, yg_b, Ut_b, HT_b, wk1_b, wk2_b, hs_b, hprev_b, hrow_b, hrow2_b, ysT_b]
        P2_BUFS = [qT_b, eT[0][1], eT[1][1], rec_b, tmpB[0][1], tmpB[1][1]]

        xv = x.rearrange("(b tt p) d -> b p tt d", p=128, tt=4)
        ov = out.rearrange("(b tt p) d -> b p tt d", p=128, tt=4)

        def x_load(b):
            t, buf, ld, stsem = xbuf[b % 2]
            P.dma("pool", (t[:], buf), (xv[b], DRAM), ld)

        x_load(0)
        inv_sqrt_hd = 1.0 / math.sqrt(128.0)

        def cmul(np0, np1, src, src_b, tre, tim, t1, t1b, t2, t2b, ng):
            npn = np1 - np0
            treb = tre.unsqueeze(2).to_broadcast([npn, ng, 2, 64])
            P.tt("dve", (t1, t1b), (src, src_b), (treb, tab_b), ALU.mult)
            P.tt("dve", (t2[:, :, 0, :], t2b), (src[:, :, 1, :], src_b), (tim, tab_b), ALU.mult)
            P.tt("dve", (t2[:, :, 1, :], t2b), (src[:, :, 0, :], src_b), (tim, tab_b), ALU.mult)

        def final_norm_store(bb):
            xt_, xb_, _, xst_ = xbuf[bb % 2]
            jt, jtb = tmpA[0]
            jv = jt[:].bitcast(BF16)
            for tt in range(4):
                P.act((jv, jtb), (xt_[:, tt, :], xb_), AF.Square, accum=(ss[:, tt:tt + 1], ss_b[tt]))
                P.act((rstd[:, tt:tt + 1], rstd_b[tt]), (ss[:, tt:tt + 1], ss_b[tt]), AF.Sqrt, scale=1.0 / D, bias=(epscol[:, 0:1], cbuf))
                P.recip((rstd[:, tt:tt + 1], rstd_b[tt]), (rstd[:, tt:tt + 1], rstd_b[tt]))
                P.stt((xt_[:, tt, :], xb_), (xt_[:, tt, :], xb_), (rstd[:, tt:tt + 1], rstd_b[tt]), (gF[:], cbuf), ALU.mult, ALU.mult)
            P.dma("pool", (ov[bb], OUTB), (xt_[:], xb_), xst_)

        pending = [None]
        for b in range(NBLK):
            xt, xb, _, xst = xbuf[b % 2]
            if b == 0 and NBLK > 1:
                x_load(1)
            if b == 0:
                phase([h_tm_b])
                for tt in range(4):
                    norm_stats(xt, xb, g1b, h_tm, h_tm_b, tt)
                for tt in range(4):
                    norm_transposes(h_tm, h_tm_b, hT, hT_b, tt)
                dbg("hT", hT[:], hT_b, [128, 8, NB])

            chk('s1')
            phase(S5_BUFS)
            wt, wb = w_get("win3")
            wv_ = wt[:, 0:4096].rearrange("p (kt n) -> p kt n", kt=8)
            hTs = hT[:].rearrange("p kt (c s) -> p kt s c", s=8)
            for s_ in range(8):
                pt, pb = bank()
                for h in range(2):
                    for kt in range(8):
                        P.mm((pt[64 * h:64 * h + 64, 0:256], pb), (hTs[:, kt, s_, :], hT_b), (wv_[:, kt, 256 * h:256 * h + 256], wb),
                             start=(kt == 0), stop=(kt == 7))
                P.copy("act" if s_ % 2 == 0 else "dve", (u_tm[:, :, s_, :], u_tm_b), (pt[:, 0:256].rearrange("p (g i) -> p g i", g=16), pb))
            w_rel("win3")
            if b == 0:
                dbg("u_tm", u_tm_f, u_tm_b, [128, 2048])

            chk('u')
            Ut4 = Ut_f.rearrange("p (h gl c) -> p h gl c", h=2, gl=16)
            for jb in range(2):
                pt, pb = bank()
                pv = pt[:].bitcast(BF16)
                for g8 in range(8):
                    gl = jb * 8 + g8
                    P.tr((pv[:, g8 * 128:(g8 + 1) * 128], pb), (u_tm_f[:, gl * 128:(gl + 1) * 128], u_tm_b), (ident[:], cbuf))
                P.copy("act" if jb == 0 else "dve", (Ut4[:, :, jb * 8:(jb + 1) * 8, :].rearrange("p h gl c -> p gl h c"), Ut_b),
                       (pv[:, 0:1024].rearrange("p (gl h c) -> p gl h c", gl=8, h=2), pb))
            if pending[0] is not None:
                final_norm_store(pending[0])
                pending[0] = None
                if b + 1 < NBLK:
                    x_load(b + 1)
            chk('s5a')
            def s5_steps():
                for q in range(4):
                    gof = lambda h, k: 16 * h + 4 * q + k
                    pA, pAb = bank()
                    for h in range(2):
                        for k in range(4):
                            g = gof(h, k)
                            P.mm((pA[64 * h:64 * h + 64, k * 128:(k + 1) * 128], pAb), (Ut[:, g, :], Ut_b), (LTT[:, g, 0:128], LTT_b))
                    src = pA[:, :].rearrange("c (g r p) -> c g r p", g=4, r=2)
                    tre = S2re[:, q * 256:(q + 1) * 256].rearrange("c (g p) -> c g p", g=4)
                    tim = S2im[:, q * 256:(q + 1) * 256].rearrange("c (g p) -> c g p", g=4)
                    t1 = wk1.rearrange("c (g r p) -> c g r p", g=4, r=2)
                    t2 = wk2.rearrange("c (g r p) -> c g r p", g=4, r=2)
                    d = hs.rearrange("c (g r p) -> c g r p", g=4, r=2)
                    cmul(0, 128, src, pAb, tre, tim, t1, wk1_b, t2, wk2_b, 4)
                    P.tt("dve", (d[:, :, 0, :], hs_b), (t1[:, :, 0, :], wk1_b), (t2[:, :, 0, :], wk2_b), ALU.subtract)
                    P.tt("dve", (d[:, :, 1, :], hs_b), (t1[:, :, 1, :], wk1_b), (t2[:, :, 1, :], wk2_b), ALU.add)
                    if q == 0: chk('s5c')
                    yield
                    hrows, u64, selr, i2r, hbase, hbf = hst_rows(q)
                    pC, pCb = bank()
                    pT, pTb = bank()
                    P.mm((pC[:, :], pCb), (trix2_bf[:, :], cbuf), (hs, hs_b), start=True, stop=False)
                    P.mm((pC[:, :], pCb), (selr, cbuf), (hbf, Hst_b[q]), start=False, stop=True)
                    P.mm((pT[hbase:hbase + 32, :], pTb), (ones2_bf[:, :], cbuf), (hs, hs_b), start=True, stop=False)
                    P.mm((pT[hbase:hbase + 32, :], pTb), (i2r, cbuf), (hbf, Hst_b[q]), start=False, stop=True)
                    if q == 0: chk('s5d')
                    src = pC[:, :].rearrange("c (g r p) -> c g r p", g=4, r=2)
                    tre = U2re[:, q * 256:(q + 1) * 256].rearrange("c (g p) -> c g p", g=4)
                    tim = U2im[:, q * 256:(q + 1) * 256].rearrange("c (g p) -> c g p", g=4)
                    cmul(0, 128, src, pCb, tre, tim, t1, wk1_b, t2, wk2_b, 4)
                    dh = hprev.rearrange("c (g r p) -> c g r p", g=4, r=2)
                    P.tt("dve", (dh[:, :, 0, :], hprev_b), (t1[:, :, 0, :], wk1_b), (t2[:, :, 0, :], wk2_b), ALU.subtract)
                    P.tt("dve", (dh[:, :, 1, :], hprev_b), (t1[:, :, 1, :], wk1_b), (t2[:, :, 1, :], wk2_b), ALU.add)
                    srcT = pT[hbase:hbase + 2, :].rearrange("c (g r p) -> c g r p", g=4, r=2)
                    r1 = hrow[hbase:hbase + 2, :].rearrange("c (g r p) -> c g r p", g=4, r=2)
                    r2 = hrow2[hbase:hbase + 2, :].rearrange("c (g r p) -> c g r p", g=4, r=2)
                    ure = u64[:, 0, :].rearrange("c (g p) -> c g p", g=4)
                    uim = u64[:, 1, :].rearrange("c (g p) -> c g p", g=4)
                    cmul(hbase, hbase + 2, srcT, pTb, ure, uim, r1, hrow_b, r2, hrow2_b, 4)
                    hd = hrows.rearrange("c (g r p) -> c g r p", g=4, r=2)
                    P.tt("dve", (hd[:, :, 0, :], Hst_b[q]), (r1[:, :, 0, :], hrow_b), (r2[:, :, 0, :], hrow2_b), ALU.subtract)
                    P.tt("dve", (hd[:, :, 1, :], Hst_b[q]), (r1[:, :, 1, :], hrow_b), (r2[:, :, 1, :], hrow2_b), ALU.add)
                    P.copy("dve", (hbf, Hst_b[q]), (hrows, Hst_b[q]))
                    if q == 0: chk('s5e')
                    yield
                    pt, pb = bank()
                    pv = pt[:].bitcast(BF16)
                    for k in range(4):
                        P.tr((pv[:, k * 128:(k + 1) * 128], pb), (hprev[:, k * 128:(k + 1) * 128], hprev_b), (ident[:], cbuf))
                    HT4 = HT_f.rearrange("p (h gl c) -> p h gl c", h=2, gl=16)
                    P.copy("act", (HT4[:, :, 4 * q:4 * q + 4, :].rearrange("p h k c -> p k h c"), HT_b),
                           (pv[:, 0:512].rearrange("p (k h c) -> p k h c", k=4, h=2), pb))
                    if q == 0: chk('s5f')
                    pE, pEb = bank()
                    for h in range(2):
                        for k in range(4):
                            g = gof(h, k)
                            o_ = (pE[64 * h:64 * h + 64, k * 128:(k + 1) * 128], pEb)
                            P.mm(o_, (Ut[:, g, :], Ut_b), (LTT[:, g, 128:256], LTT_b), start=True, stop=False)
                            P.mm(o_, (HT[:, g, :], HT_b), (VV[:, g, :], VV_b), start=False, stop=True)
                    src = pE[:, :].rearrange("c (g t o) -> c t g o", g=4, t=8)
                    dstv = yg[:, :, q * 64:(q + 1) * 64].rearrange("c t (g o) -> c t g o", g=4)
                    P.act((dstv, yg_b), (src, pEb), AF.Gelu_apprx_tanh)
                    yield
            def bulk_units():
                wt, wb = w_get("win4")
                wv_ = wt[:, 0:4096].rearrange("p (kt n) -> p kt n", kt=8)
                for hh in range(4):
                    pt, pb = bank()
                    for kt in range(8):
                        P.mm((pt[:, :], pb), (wv_[:, kt, hh * 128:(hh + 1) * 128], wb), (hT[:, kt, :], hT_b), start=(kt == 0), stop=(kt == 7))
                    P.copy("act" if hh % 2 == 0 else "dve", (qT[:, hh, :], qT_b), (pt[:, :], pb))
                    yield
                w_rel("win4")
                for hh in range(4):
                    et, eb = eT[hh % 2]
                    for mt in range(2):
                        pt, pb = bank()
                        P.mm((pt[:, :], pb), (kT[:, hh, mt * 128:(mt + 1) * 128], kT_b), (qT[:, hh, :], qT_b))
                        P.act((et[:, mt, :], eb), (pt[:, :], pb), AF.Exp, scale=inv_sqrt_hd)
                    yield
                    po, pob = bank()
                    pd, pdb = bank()
                    for mt in range(2):
                        P.mm((po[:, :], pob), (Vm[:, mt, hh * 128:(hh + 1) * 128], Vm_b), (et[:, mt, :], eb), start=(mt == 0), stop=(mt == 1))
                    for mt in range(2):
                        P.mm((pd[:, :], pdb), (ones_bf[:, :], cbuf), (et[:, mt, :], eb), start=(mt == 0), stop=(mt == 1))
                    P.recip((rec, rec_b), (pd[:, :], pdb))
                    P.tt("dve", (oT[:, hh, :], oT_b), (po[:, :], pob), (rec, rec_b), ALU.mult)
                    yield
                if b == 0:
                    dbg("oT", oT[:], oT_b, [128, 4, NB])

                wc_t, wc_b = w_get("win1")
                wv_t, wv_b2 = w_get("win2")
                wcv = wc_t[:, 0:4096].rearrange("p (kt n) -> p kt n", kt=8)
                wvv2 = wv_t[:, 0:4096].rearrange("p (kt n) -> p kt n", kt=8)
                for ft in range(4):
                    pc, pcb = bank()
                    pvv, pvb = bank()
                    for kt in range(8):
                        P.mm((pc[:, :], pcb), (wcv[:, kt, ft * 128:(ft + 1) * 128], wc_b), (hT[:, kt, :], hT_b), start=(kt == 0), stop=(kt == 7))
                    for kt in range(8):
                        P.mm((pvv[:, :], pvb), (wvv2[:, kt, ft * 128:(ft + 1) * 128], wv_b2), (hT[:, kt, :], hT_b), start=(kt == 0), stop=(kt == 7))
                    ta_, tab_ = tmpA[ft % 2]
                    P.copy("act", (ta_[:], tab_), (pc[:, :], pcb))
                    P.tt("dve", (cv[:, ft, 2:NB + 2], cv_b[ft]), (ta_[:], tab_), (pvv[:, :], pvb), ALU.mult)
                    yield
                w_rel("win1")
                w_rel("win2")
                phase([tmpB[1][1]])
                wb_t, wb_b = w_get("win0")
                wbv = wb_t[:, 0:4096].rearrange("p (kt n) -> p kt n", kt=8)
                for ft in range(4):
                    pbk, pbb = bank()
                    for kt in range(8):
                        P.mm((pbk[:, :], pbb), (wbv[:, kt, ft * 128:(ft + 1) * 128], wb_b), (hT[:, kt, :], hT_b), start=(kt == 0), stop=(kt == 7))
                    tb_, tbb_ = tmpB[ft % 2]
                    P.ts("dve", (tb_, tbb_), (cv[:, ft, 2:NB + 2], cv_b[ft]), (convw[:, 2, ft:ft + 1], cbuf), ALU.mult)
                    P.stt((tb_, tbb_), (cv[:, ft, 1:NB + 1], cv_b[ft]), (convw[:, 1, ft:ft + 1], cbuf), (tb_, tbb_), ALU.mult, ALU.add)
                    P.stt((tb_, tbb_), (cv[:, ft, 0:NB], cv_b[ft]), (convw[:, 0, ft:ft + 1], cbuf), (tb_, tbb_), ALU.mult, ALU.add)
                    P.tt("dve", (conv_preT[:, ft, :], conv_pre_b), (tb_, tbb_), (pbk[:, :], pbb), ALU.mult)
                    P.copy("pool", (cv[:, ft, 0:2], cv_b[ft]), (cv[:, ft, NB:NB + 2], cv_b[ft]))
                    yield
                w_rel("win0")
                if b == 0:
                    dbg("conv_preT", conv_preT[:], conv_pre_b, [128, 4, NB])

                yield
            phase([bb for bb in P2_BUFS if bb is not tmpB[1][1]])
            gs_, gb_ = s5_steps(), bulk_units()
            alive_s, alive_b = True, True
            import os
            ILV = "a"
            nb_units = 0
            if ILV == "0":
                for _ in gs_:
                    pass
                alive_s = False
            if ILV == "c":
                for _ in range(12):
                    next(gb_)
            while alive_s or alive_b:
                if alive_s:
                    try:
                        next(gs_)
                    except StopIteration:
                        alive_s = False
                for _rep in range(1):
                    if alive_b and not (ILV == "a" and nb_units >= 12 and alive_s):
                        try:
                            next(gb_)
                            nb_units += 1
                        except StopIteration:
                            alive_b = False
            if b == 0:
                dbg("yg", yg_f, yg_b, [128, 2048])
            chk('s5h')
            ys5 = ysT_f.rearrange("p (h j c t) -> p h j c t", h=2, j=2, t=8)
            for j in range(2):
                pt, pb = bank()
                pv = pt[:].bitcast(BF16)
                for t_ in range(8):
                    P.tr((pv[:, t_ * 128:(t_ + 1) * 128], pb), (yg[:, t_, j * 128:(j + 1) * 128], yg_b), (ident[:], cbuf))
                P.copy("act", (ys5[:, :, j, :, :], ysT_b),
                       (pv[:, 0:1024].rearrange("p (t h c) -> p h c t", t=8, h=2), pb))
            if b == 0:
                dbg("ysT", ysT_f, ysT_b, [128, 2048])

            chk('s5')
            chk('conv')
            wt, wb = w_get("glu")
            wgl = wt[:, 0:2048].rearrange("p (kt n) -> p kt n", kt=4)
            for ft in range(4):
                pt, pb = bank()
                for kt in range(4):
                    P.mm((pt[:, :], pb), (wgl[:, kt, ft * 128:(ft + 1) * 128], wb), (ysT[:, kt, :], ysT_b), start=(kt == 0), stop=(kt == 3))
                ta_, tab_ = tmpA[ft % 2]
                P.act((ta_[:], tab_), (pt[:, :], pb), AF.Sigmoid, bias=(glub[:, ft:ft + 1], cbuf))
                P.tt("pool", (ys2T[:, ft, :], ys2_b), (ysT[:, ft, :], ysT_b), (ta_[:], tab_), ALU.mult)
            w_rel("glu")
            if b == 0:
                dbg("ys2T", ys2T[:], ys2_b, [128, 4, NB])

            chk('glu')
            phase([merged_b, mergedT_b])
            for bi, (pname, gch, srcT, srcb) in enumerate((("cp", (5, 6), conv_preT, conv_pre_b),
                                                            ("sp", (7, 8), ys2T, ys2_b),
                                                            ("ap", (9, 10), oT, oT_b))):
                wp_t, wp_b = w_get(pname)
                wpv = wp_t[:, 0:4096].rearrange("p (kt n) -> p kt n", kt=4)
                for gi, gc in enumerate(gch):
                    wg_t, wg_b = w_get("win%d" % gc)
                    wgv = wg_t[:, 0:4096].rearrange("p (kt n) -> p kt n", kt=8)
                    for jl in range(4):
                        j = gi * 4 + jl
                        py, pyb = bank()
                        pg, pgb = bank()
                        for kt in range(4):
                            P.mm((py[:, :], pyb), (wpv[:, kt, j * 128:(j + 1) * 128], wp_b), (srcT[:, kt, :], srcb), start=(kt == 0), stop=(kt == 3))
                        for kt in range(8):
                            P.mm((pg[:, :], pgb), (wgv[:, kt, jl * 128:(jl + 1) * 128], wg_b), (hT[:, kt, :], hT_b), start=(kt == 0), stop=(kt == 7))
                        ta_, tab_ = tmpA[j % 2]
                        P.act((ta_[:], tab_), (pg[:, :], pgb), AF.Sigmoid)
                        if bi == 0:
                            P.tt("dve", (merged[:, j, :], merged_b), (ta_[:], tab_), (py[:, :], pyb), ALU.mult)
                        elif bi == 1:
                            tb_, tbb_ = tmpB[j % 2]
                            P.tt("dve", (tb_, tbb_), (ta_[:], tab_), (py[:, :], pyb), ALU.mult)
                            P.tt("pool", (merged[:, j, :], merged_b), (merged[:, j, :], merged_b), (tb_, tbb_), ALU.add)
                        else:
                            tb_, tbb_ = tmpB[j % 2]
                            P.tt("dve", (tb_, tbb_), (ta_[:], tab_), (py[:, :], pyb), ALU.mult)
                            P.tt("pool", (mergedT[:, j, :], mergedT_b), (merged[:, j, :], merged_b), (tb_, tbb_), ALU.add)
                    w_rel("win%d" % gc)
                w_rel(pname)
            if b == 0:
                dbg("mergedT", mergedT_f, mergedT_b, [128, 4096])

            chk('merge')
            wo = [w_get("wo0"), w_get("wo1")]
            phase([h_tm_b])
            for tt in range(4):
                for nh in range(2):
                    wt, wb = wo[nh]
                    wov = wt[:, 0:4096].rearrange("p (kt n) -> p kt n", kt=8)
                    pt, pb = bank()
                    for j in range(8):
                        P.mm((pt[:, :], pb), (mergedT[:, j, tt * 128:(tt + 1) * 128], mergedT_b), (wov[:, j, :], wb), start=(j == 0), stop=(j == 7))
                    P.tt("dve", (xt[:, tt, nh * 512:(nh + 1) * 512], xb), (xt[:, tt, nh * 512:(nh + 1) * 512], xb), (pt[:, :], pb), ALU.add)
                norm_act(xt, xb, h_tm, h_tm_b, tt)
                if tt >= 1:
                    norm_dve(xt, xb, g2b, h_tm, h_tm_b, tt - 1)
            norm_dve(xt, xb, g2b, h_tm, h_tm_b, 3)
            w_rel("wo0")
            w_rel("wo1")
            if b == 0:
                dbg("x1", xt[:], xb, [128, 4, D])
            chk('wo')
            for tt in range(2):
                norm_transposes(h_tm, h_tm_b, hT, hT_b, tt)

            phase([actT_b])
            for j in range(11):
                wt, wb = w_get("ffn%d" % j)
                wgv = wt[:, 0:2048].rearrange("p (kt n) -> p kt n", kt=8)
                wuv = wt[:, 2048:4096].rearrange("p (kt n) -> p kt n", kt=8)
                fb = [(bank(), bank()) for _sub in range(2)]
                if j == 0:
                    for half in range(2):
                        c0, c1 = half * 256, (half + 1) * 256
                        hb2 = (hT_tt[2 * half], hT_tt[2 * half + 1])
                        for sub in range(2):
                            (pg, pgb), (pu, pub) = fb[sub]
                            for kt in range(8):
                                P.mm((pg[:, c0:c1], pgb), (wgv[:, kt, sub * 128:(sub + 1) * 128], wb), (hT[:, kt, c0:c1], hb2), start=(kt == 0), stop=(kt == 7))
                            for kt in range(8):
                                P.mm((pu[:, c0:c1], pub), (wuv[:, kt, sub * 128:(sub + 1) * 128], wb), (hT[:, kt, c0:c1], hb2), start=(kt == 0), stop=(kt == 7))
                        if half == 0:
                            for tt in range(2, 4):
                                norm_transposes(h_tm, h_tm_b, hT, hT_b, tt)
                else:
                    for sub in range(2):
                        (pg, pgb), (pu, pub) = fb[sub]
                        for kt in range(8):
                            P.mm((pg[:, :], pgb), (wgv[:, kt, sub * 128:(sub + 1) * 128], wb), (hT[:, kt, :], hT_b), start=(kt == 0), stop=(kt == 7))
                        for kt in range(8):
                            P.mm((pu[:, :], pub), (wuv[:, kt, sub * 128:(sub + 1) * 128], wb), (hT[:, kt, :], hT_b), start=(kt == 0), stop=(kt == 7))
                for sub in range(2):
                    f = 2 * j + sub
                    (pg, pgb), (pu, pub) = fb[sub]
                    ta_, tab_ = tmpA[f % 2]
                    P.act((ta_[:], tab_), (pg[:, :], pgb), AF.Silu)
                    P.tt("dve", (actT[:, f, :], actT_b), (ta_[:], tab_), (pu[:, :], pub), ALU.mult)
                w_rel("ffn%d" % j)
            chk('ffn1')
            if b + 1 < NBLK:
                xn, xnb, _, _ = xbuf[(b + 1) % 2]
                phase([h_tm_b])
                for tt in range(4):
                    norm_stats(xn, xnb, g1b, h_tm, h_tm_b, tt)
            accs = [pbank[i] for i in range(8)]
            for r in range(6):
                nk = 4 if r < 5 else 2
                wt, wb = w_get("dn%d" % r)
                wdv = wt[:, 0:nk * 1024].rearrange("p (kt n) -> p kt n", kt=nk)
                for tt in range(4):
                    for nh in range(2):
                        pt, pb = accs[tt * 2 + nh]
                        for k in range(nk):
                            f = r * 4 + k
                            P.mm((pt[:, :], pb), (actT[:, f, tt * 128:(tt + 1) * 128], actT_b), (wdv[:, k, nh * 512:(nh + 1) * 512], wb),
                                 start=(f == 0), stop=(f == 21))
                w_rel("dn%d" % r)
            for tt in range(4):
                for nh in range(2):
                    pt, pb = accs[tt * 2 + nh]
                    P.tt("dve", (xt[:, tt, nh * 512:(nh + 1) * 512], xb), (xt[:, tt, nh * 512:(nh + 1) * 512], xb), (pt[:, :], pb), ALU.add)
            bank_rr[0] = 0
            if b + 1 < NBLK:
                for tt in range(4):
                    norm_transposes(h_tm, h_tm_b, hT, hT_b, tt, eng="act")
            if b + 1 < NBLK:
                pending[0] = b
            else:
                final_norm_store(b)

      except _Stop:
        pass
      P.final_wait()
      P.emit()
    return nc, dbg_outs


_CACHE = {}

PARAM_KEYS = ["norm1_g", "w_in", "conv_w", "conv_proj", "ssm_A_re", "ssm_A_im", "ssm_log_dt", "ssm_B_re", "ssm_B_im",
              "ssm_C_re", "ssm_C_im", "ssm_D", "ssm_glu_w", "ssm_glu_b", "ssm_proj", "mem_norm_g", "attn_wk", "attn_wv",
              "attn_proj", "w_o", "norm2_g", "ffn_w_gate", "ffn_w_up", "ffn_w_down", "final_norm_g"]


def _prep_params(inp):
    f = lambda a: np.ascontiguousarray(np.asarray(a, dtype=np.float32))
    p = {}
    p["norm1_g"] = f(inp["norm1_g"]).reshape(1024)
    p["w_in"] = f(inp["w_in"]).reshape(1024, 5632)
    p["conv_w"] = f(inp["conv_w"]).reshape(3, 512)
    p["conv_proj"] = f(inp["conv_proj"]).reshape(512, 1024)
    p["ssm_A_re"] = f(inp["ssm_A_re"]).reshape(32, 64)
    p["ssm_A_im"] = f(inp["ssm_A_im"]).reshape(32, 64)
    p["ssm_log_dt"] = f(inp["ssm_log_dt"]).reshape(1, 32)
    p["ssm_B_re"] = f(inp["ssm_B_re"]).reshape(32, 64, 16)
    p["ssm_B_im"] = f(inp["ssm_B_im"]).reshape(32, 64, 16)
    p["ssm_C_re"] = f(inp["ssm_C_re"]).reshape(512, 64)
    p["ssm_C_im"] = f(inp["ssm_C_im"]).reshape(512, 64)
    p["ssm_D"] = f(inp["ssm_D"]).reshape(512)
    p["ssm_glu_w"] = f(inp["ssm_glu_w"]).reshape(512, 512)
    p["ssm_glu_b"] = f(inp["ssm_glu_b"]).reshape(512)
    p["ssm_proj"] = f(inp["ssm_proj"]).reshape(512, 1024)
    p["mem_norm_g"] = f(inp["mem_norm_g"]).reshape(1024)
    p["attn_wk"] = f(inp["attn_wk"]).reshape(1024, 512)
    p["attn_wv"] = f(inp["attn_wv"]).reshape(1024, 512)
    p["attn_proj"] = f(inp["attn_proj"]).reshape(512, 1024)
    p["w_o"] = f(inp["w_o"]).reshape(1024, 1024)
    p["norm2_g"] = f(inp["norm2_g"]).reshape(1024)
    p["ffn_w_gate"] = f(inp["ffn_w_gate"]).reshape(1024, 2816)
    p["ffn_w_up"] = f(inp["ffn_w_up"]).reshape(1024, 2816)
    p["ffn_w_down"] = f(inp["ffn_w_down"]).reshape(2816, 1024)
    p["final_norm_g"] = f(inp["final_norm_g"]).reshape(1, 1024)
    return p


def kernel(**inputs):
    x = np.asarray(inputs["x"], dtype=np.float32)
    mem = np.asarray(inputs["mem"], dtype=np.float32)
    Bn, T, _ = x.shape
    if T not in _CACHE:
        _CACHE[T] = build(T)[0]
    nc = _CACHE[T]
    p = _prep_params(inputs)
    in_maps = []
    for i in range(Bn):
        m = dict(p)
        m["x"] = np.ascontiguousarray(x[i])
        m["mem"] = np.ascontiguousarray(mem[i])
        in_maps.append(m)
    res = run_bass_kernel_spmd(nc, in_maps, core_ids=list(range(Bn)))
    return np.stack([np.asarray(r["out"], dtype=np.float32).reshape(T, 1024) for r in res.results], axis=0)
```

```python
import math
from contextlib import ExitStack

import numpy as np
import concourse.bass as bass
import concourse.mybir as mybir
from concourse.bass_utils import run_bass_kernel_spmd

F32 = mybir.dt.float32
BF16 = mybir.dt.bfloat16
I32 = mybir.dt.int32
AF = mybir.ActivationFunctionType
ALU = mybir.AluOpType
AX = mybir.AxisListType

D = 1024
NB = 512
NCH = NB // 8
NSLOT = 4
SLOT_E = 4096
EPS = 1e-6
TWO_PI = 2.0 * math.pi


class Sem:
    def __init__(self, h, name):
        self.h = h
        self.name = name
        self.count = 0


class Buf:
    def __init__(self, name, excl=False):
        self.name = name
        self.w = None
        self.r = {}
        self.excl = excl


def _flat(xs):
    out = []
    for x in xs:
        if isinstance(x, (list, tuple)):
            out.extend(x)
        else:
            out.append(x)
    return out


class Eng:
    def __init__(self, name, sem, self_sync):
        self.name = name
        self.sem = sem
        self.ops = []
        self.seen = {}
        self.self_sync = self_sync


class Prog:
    def __init__(self, nc, stack):
        self.nc = nc
        self.stack = stack
        self.engs = {}
        self.sems = []
        self.stopped = False
        for name, ss in (("pe", False), ("act", True), ("dve", True), ("pool", True), ("sp", True)):
            self.engs[name] = Eng(name, self.new_sem("e_" + name), ss)

    def new_sem(self, name):
        h = self.stack.enter_context(self.nc.semaphore(name))
        s = Sem(h, name)
        self.sems.append(s)
        return s

    def barrier(self):
        if self.stopped:
            return
        for e in self.engs.values():
            for s in self.sems:
                if s is e.sem or s.count == 0:
                    continue
                if e.seen.get(s, 0) >= s.count:
                    continue
                e.seen[s] = s.count
                e.ops.append((lambda h, s=s, v=s.count: h.wait_ge(s.h, v)))

    def final_wait(self):
        self.stopped = False
        self.barrier()

    def op(self, eng, fn, reads=(), writes=(), dma=None):
        if self.stopped:
            return None
        e = self.engs[eng]
        reads = _flat(reads)
        writes = _flat(writes)
        deps = {}

        def add(ev):
            s, v = ev
            if deps.get(s, 0) < v:
                deps[s] = v
        for b in reads:
            if b.w is not None:
                add(b.w)
            if b.excl:
                for s, v in b.r.items():
                    if s is not e.sem:
                        add((s, v))
        for b in writes:
            if b.w is not None:
                add(b.w)
            for s, v in b.r.items():
                add((s, v))
        for s, v in deps.items():
            if s is e.sem and not e.self_sync:
                continue
            if e.seen.get(s, 0) >= v:
                continue
            e.seen[s] = v
            e.ops.append((lambda h, s=s, v=v: h.wait_ge(s.h, v)))
        if dma is not None:
            dma.count += 16
            ev = (dma, dma.count)
            e.ops.append((lambda h, fn=fn, s=dma: fn(h).then_inc(s.h, 16)))
        else:
            e.sem.count += 1
            ev = (e.sem, e.sem.count)
            e.ops.append((lambda h, fn=fn, s=e.sem: fn(h).then_inc(s.h, 1)))
        for b in writes:
            b.w = ev
            b.r = {}
        for b in reads:
            if b in writes:
                continue
            s, v = ev
            if b.r.get(s, 0) < v:
                b.r[s] = v
        return ev

    def wait_all(self, eng, bufs):
        e = self.engs[eng]
        for b in bufs:
            evs = []
            if b.w is not None:
                evs.append(b.w)
            for s, v in evs:
                if e.seen.get(s, 0) < v:
                    e.seen[s] = v
                    e.ops.append((lambda h, s=s, v=v: h.wait_ge(s.h, v)))

    def mm(self, out, lhsT, rhs, start=True, stop=True, extra_r=()):
        o, l, r = out[0], lhsT[0], rhs[0]
        self.op("pe", lambda h: h.matmul(o, lhsT=l, rhs=r, start=start, stop=stop),
                reads=[lhsT[1], rhs[1]] + list(extra_r), writes=[out[1]])

    def tr(self, out, in_, ident):
        o, i, d = out[0], in_[0], ident[0]
        self.op("pe", lambda h: h.transpose(o, i, d), reads=[in_[1], ident[1]], writes=[out[1]])

    def act(self, out, in_, func, scale=None, bias=None, accum=None):
        o, i = out[0], in_[0]
        kw = {}
        reads = [in_[1]]
        writes = [out[1]]
        if scale is not None:
            if isinstance(scale, tuple):
                kw["scale"] = scale[0]
                reads.append(scale[1])
            else:
                kw["scale"] = float(scale)
        if bias is not None:
            if isinstance(bias, tuple):
                kw["bias"] = bias[0]
                reads.append(bias[1])
            else:
                kw["bias"] = float(bias)
        if accum is not None:
            kw["accum_out"] = accum[0]
            writes.append(accum[1])
        self.op("act", lambda h: h.activation(out=o, in_=i, func=func, **kw), reads=reads, writes=writes)

    def tt(self, eng, out, in0, in1, op):
        o, a, b = out[0], in0[0], in1[0]
        self.op(eng, lambda h: h.tensor_tensor(out=o, in0=a, in1=b, op=op),
                reads=[in0[1], in1[1]], writes=[out[1]])

    def ts(self, eng, out, in0, s1, op0, s2=None, op1=None):
        o, a = out[0], in0[0]
        reads = [in0[1]]
        if isinstance(s1, tuple):
            reads.append(s1[1])
            s1 = s1[0]
        else:
            s1 = float(s1)
        if isinstance(s2, tuple):
            reads.append(s2[1])
            s2 = s2[0]
        elif s2 is not None:
            s2 = float(s2)
        if op1 is None:
            self.op(eng, lambda h: h.tensor_scalar(out=o, in0=a, scalar1=s1, scalar2=None, op0=op0),
                    reads=reads, writes=[out[1]])
        else:
            self.op(eng, lambda h: h.tensor_scalar(out=o, in0=a, scalar1=s1, scalar2=s2, op0=op0, op1=op1),
                    reads=reads, writes=[out[1]])

    def stt(self, out, in0, scalar, in1, op0, op1):
        o, a, b = out[0], in0[0], in1[0]
        reads = [in0[1], in1[1]]
        if isinstance(scalar, tuple):
            reads.append(scalar[1])
            sc = scalar[0]
        else:
            sc = float(scalar)
        self.op("dve", lambda h: h.scalar_tensor_tensor(out=o, in0=a, scalar=sc, in1=b, op0=op0, op1=op1),
                reads=reads, writes=[out[1]])

    def copy(self, eng, out, in_):
        o, i = out[0], in_[0]
        if eng == "act":
            self.op("act", lambda h: h.activation(out=o, in_=i, func=AF.Copy), reads=[in_[1]], writes=[out[1]])
        else:
            self.op(eng, lambda h: h.tensor_copy(out=o, in_=i), reads=[in_[1]], writes=[out[1]])

    def memset(self, eng, out, val):
        o = out[0]
        self.op(eng, lambda h: h.memset(o, val), writes=[out[1]])

    def recip(self, out, in_):
        o, i = out[0], in_[0]
        self.op("dve", lambda h: h.reciprocal(out=o, in_=i), reads=[in_[1]], writes=[out[1]])

    def dma(self, eng, out, in_, sem, slow=False):
        o, i = out[0], in_[0]
        if slow:
            fn = lambda h: h.dma_start(out=o, in_=i, allow_slow_non_contiguous=True)
        else:
            fn = lambda h: h.dma_start(out=o, in_=i)
        self.op(eng, fn, reads=[in_[1]], writes=[out[1]], dma=sem)

    def emit(self):
        nc = self.nc
        with nc.Block() as block:
            @block.sync
            def _(h):
                for f in self.engs["sp"].ops:
                    f(h)

            @block.tensor
            def _(h):
                for f in self.engs["pe"].ops:
                    f(h)

            @block.scalar
            def _(h):
                for f in self.engs["act"].ops:
                    f(h)

            @block.vector
            def _(h):
                for f in self.engs["dve"].ops:
                    f(h)

            @block.gpsimd
            def _(h):
                for f in self.engs["pool"].ops:
                    f(h)


class _Stop(Exception):
    pass


def build(T, debug=(), stop=None):
    NBLK = T // NB
    nc = bass.Bass("TRN2", target_bir_lowering=False)

    def din(name, shape):
        return nc.dram_tensor(name, shape, F32, kind="ExternalInput").ap()

    x = din("x", [T, D])
    mem = din("mem", [256, D])
    norm1_g = din("norm1_g", [D])
    w_in = din("w_in", [D, 5632])
    conv_w = din("conv_w", [3, 512])
    conv_proj = din("conv_proj", [512, D])
    A_re = din("ssm_A_re", [32, 64])
    A_im = din("ssm_A_im", [32, 64])
    log_dt = din("ssm_log_dt", [1, 32])
    B_re = din("ssm_B_re", [32, 64, 16])
    B_im = din("ssm_B_im", [32, 64, 16])
    C_re = din("ssm_C_re", [512, 64])
    C_im = din("ssm_C_im", [512, 64])
    ssm_D = din("ssm_D", [512])
    glu_w = din("ssm_glu_w", [512, 512])
    glu_b = din("ssm_glu_b", [512])
    ssm_proj = din("ssm_proj", [512, D])
    mem_norm_g = din("mem_norm_g", [D])
    attn_wk = din("attn_wk", [D, 512])
    attn_wv = din("attn_wv", [D, 512])
    attn_proj = din("attn_proj", [512, D])
    w_o = din("w_o", [D, D])
    norm2_g = din("norm2_g", [D])
    ffn_g = din("ffn_w_gate", [D, 2816])
    ffn_u = din("ffn_w_up", [D, 2816])
    ffn_d = din("ffn_w_down", [2816, D])
    final_g = din("final_norm_g", [1, D])
    out = nc.dram_tensor("out", [T, D], F32, kind="ExternalOutput").ap()

    NCHUNK = 34
    wscr = nc.dram_tensor("wscr", [NCHUNK, 128, SLOT_E], BF16, kind="Internal").ap()
    dbg_outs = {}

    with ExitStack() as st:
      P = Prog(nc, st)
      try:
        def chk(name):
            if stop == name:
                P.stopped = True

        def sb(name, shape, dt, stack=None):
            return (stack or st).enter_context(nc.sbuf_tensor(name, shape, dt))

        DRAM = Buf("dram_in")
        OUTB = Buf("dram_out")

        def dbg(name, ap, buf, shape):
            if name not in debug:
                return
            o = nc.dram_tensor("dbg_" + name, list(shape), ap.dtype, kind="ExternalOutput").ap()
            dbg_outs[name] = o
            s = P.new_sem("dbg_" + name)
            P.dma("sp", (o, OUTB), (ap, buf), s)

        pbank = []
        for i in range(8):
            t = st.enter_context(nc.psum_tensor("pb%d" % i, [128, 512], F32))
            pbank.append((t, Buf("pb%d" % i, excl=True)))
        bank_rr = [0]

        def bank():
            i = bank_rr[0]
            bank_rr[0] = (i + 1) % 8
            return pbank[i]

        cbuf = Buf("consts")
        csem = P.new_sem("csem")
        identf = sb("identf", [128, 128], F32)
        ident = sb("ident", [128, 128], BF16)
        ones_bf = sb("ones_bf", [128, 128], BF16)
        trix2 = sb("trix2", [128, 128], F32)
        sel2 = sb("sel2", [66, 128], F32)
        ones2 = sb("ones2", [128, 32], F32)
        tmask = sb("tmask", [128, 128], F32)
        g1b = sb("g1b", [128, D], F32)
        g2b = sb("g2b", [128, D], F32)
        convw = sb("convw", [128, 3, 4], F32)
        glub = sb("glub", [128, 4], F32)
        gF = sb("gF", [128, D], F32)
        epscol = sb("epscol", [128, 1], F32)
        ss = sb("ss", [128, 4], F32); ss_b = [Buf("ss%d" % i) for i in range(4)]
        rstd = sb("rstd", [128, 4], F32); rstd_b = [Buf("rstd%d" % i) for i in range(4)]
        CB = lambda ap: (ap, cbuf)

        def asel(ap, pattern, op, base, cm):
            P.op("pool", lambda h: h.affine_select(out=ap, in_=ap, pattern=pattern, compare_op=op, fill=0.0,
                                                   base=base, channel_multiplier=cm), reads=[cbuf], writes=[cbuf])

        P.memset("pool", CB(identf[:]), 0.0)
        P.op("pool", lambda h: h.affine_select(out=identf[:], in_=identf[:], pattern=[[-1, 128]],
                                               compare_op=ALU.not_equal, fill=1.0, base=0, channel_multiplier=1),
             reads=[cbuf], writes=[cbuf])
        P.copy("dve", CB(ident[:]), CB(identf[:]))
        P.memset("dve", CB(ones_bf[:]), 1.0)
        P.memset("dve", CB(epscol[:]), EPS)
        P.memset("pool", CB(trix2[:]), 1.0)
        asel(trix2[:], [[1, 128]], ALU.is_ge, -1, -1)
        P.memset("pool", CB(trix2[0:64, 64:128]), 0.0)
        P.memset("pool", CB(sel2[:]), 1.0)
        for base in (0, 32, 64):
            asel(sel2[base:base + 2, :], [[1, 128]], ALU.is_ge, 0, -64)
            asel(sel2[base:base + 2, :], [[-1, 128]], ALU.is_ge, 63, 64)
        P.memset("pool", CB(ones2[:]), 0.0)
        P.memset("pool", CB(ones2[:, 0:2]), 1.0)
        asel(ones2[:, 0:2], [[-64, 2]], ALU.is_ge, 0, 1)
        asel(ones2[:, 0:2], [[64, 2]], ALU.is_ge, 63, -1)
        trix2_bf = sb("trix2_bf", [128, 128], BF16)
        sel2_bf = sb("sel2_bf", [66, 128], BF16)
        ones2_bf = sb("ones2_bf", [128, 32], BF16)
        P.copy("dve", CB(trix2_bf[:]), CB(trix2[:]))
        P.copy("dve", CB(sel2_bf[:]), CB(sel2[:]))
        P.copy("dve", CB(ones2_bf[:]), CB(ones2[:]))
        P.memset("pool", CB(tmask[:]), 1.0)
        asel(tmask[:].rearrange("p (t o) -> p t o", t=8), [[16, 8], [0, 16]], ALU.is_ge, 15, -1)
        P.dma("sp", CB(g1b[:]), (norm1_g.unsqueeze(0).partition_broadcast(128), DRAM), csem)
        P.dma("sp", CB(g2b[:]), (norm2_g.unsqueeze(0).partition_broadcast(128), DRAM), csem)
        for k in range(3):
            P.dma("sp", CB(convw[:, k, :]), (conv_w[k].rearrange("(ft p) -> p ft", p=128), DRAM), csem, slow=True)
        P.dma("sp", CB(glub[:]), (glu_b.rearrange("(ft p) -> p ft", p=128), DRAM), csem, slow=True)
        P.dma("sp", CB(gF[:]), (final_g.partition_broadcast(128), DRAM), csem)

        chk('consts')
        slots = []
        for i in range(NSLOT):
            t = sb("wslot%d" % i, [128, SLOT_E], BF16)
            slots.append((t, Buf("wslot%d" % i), P.new_sem("wld%d" % i), P.new_sem("wst%d" % i)))
        scr_buf = [Buf("scr%d" % i) for i in range(NCHUNK)]

        def wsrc(w, p=128):
            return w.rearrange("(kt p) n -> p kt n", p=p)

        chunk_parts = {}
        for j in range(11):
            chunk_parts["win%d" % j] = [(lambda s: s[:, 0:4096].rearrange("p (kt n) -> p kt n", kt=8),
                                         wsrc(w_in)[:, :, j * 512:(j + 1) * 512])]
        chunk_parts["cp"] = [(lambda s: s[:, 0:4096].rearrange("p (kt n) -> p kt n", kt=4), wsrc(conv_proj))]
        chunk_parts["sp"] = [(lambda s: s[:, 0:4096].rearrange("p (kt n) -> p kt n", kt=4), wsrc(ssm_proj))]
        chunk_parts["ap"] = [(lambda s: s[:, 0:4096].rearrange("p (kt n) -> p kt n", kt=4), wsrc(attn_proj))]
        chunk_parts["glu"] = [(lambda s: s[:, 0:2048].rearrange("p (kt n) -> p kt n", kt=4), wsrc(glu_w))]
        for hh in range(2):
            chunk_parts["wo%d" % hh] = [(lambda s: s[:, 0:4096].rearrange("p (kt n) -> p kt n", kt=8),
                                         wsrc(w_o)[:, :, hh * 512:(hh + 1) * 512])]
        for j in range(11):
            chunk_parts["ffn%d" % j] = [
                (lambda s: s[:, 0:2048].rearrange("p (kt n) -> p kt n", kt=8), wsrc(ffn_g)[:, :, j * 256:(j + 1) * 256]),
                (lambda s: s[:, 2048:4096].rearrange("p (kt n) -> p kt n", kt=8), wsrc(ffn_u)[:, :, j * 256:(j + 1) * 256]),
            ]
        for r in range(6):
            nk = 4 if r < 5 else 2
            chunk_parts["dn%d" % r] = [(lambda s, nk=nk: s[:, 0:nk * 1024].rearrange("p (kt n) -> p kt n", kt=nk),
                                        wsrc(ffn_d)[:, 4 * r:4 * r + nk, :])]
        chunk_ids = list(chunk_parts.keys())
        chunk_idx = {k: i for i, k in enumerate(chunk_ids)}
        assert len(chunk_ids) == NCHUNK

        block_sched = (["win3", "win4", "win1", "win2", "win0", "glu", "cp", "win5", "win6",
                        "sp", "win7", "win8", "ap", "win9", "win10", "wo0", "wo1"]
                       + ["ffn%d" % j for j in range(11)] + ["dn%d" % r for r in range(6)])
        assert len(block_sched) == NCHUNK
        sched = [(b, c) for b in range(NBLK) for c in block_sched]
        wstate = {"issued": 0, "use": 0}

        def w_issue(i):
            b, c = sched[i]
            t, buf, ld, stsem = slots[i % NSLOT]
            ci = chunk_idx[c]
            if b == 0:
                for dstf, src in chunk_parts[c]:
                    P.dma("pool", (dstf(t), buf), (src, DRAM), ld)
                if NBLK > 1:
                    P.dma("sp", (wscr[ci], scr_buf[ci]), (t[:, :], buf), stsem)
            else:
                P.dma("sp", (t[:, :], buf), (wscr[ci], scr_buf[ci]), ld)

        wdone = [False] * len(sched)
        wcur = {}

        def w_pump(limit):
            while wstate["issued"] < min(len(sched), limit):
                n = wstate["issued"]
                if n >= NSLOT and not wdone[n - NSLOT]:
                    break
                w_issue(n)
                wstate["issued"] += 1

        def w_get(c):
            i = wstate["use"]
            assert sched[i][1] == c, (sched[i], c)
            w_pump(len(sched))
            assert wstate["issued"] > i, ("weight slot plan deadlock", i, c)
            wstate["use"] = i + 1
            wcur[c] = i
            t, buf, _, _ = slots[i % NSLOT]
            return t, buf

        def w_rel(c):
            i = wcur.pop(c)
            wdone[i] = True
            w_pump(len(sched))

        xbuf = []
        for i in range(2):
            t = sb("x_tm%d" % i, [128, 4, D], F32)
            xbuf.append((t, Buf("x_tm%d" % i), P.new_sem("xld%d" % i), P.new_sem("xst%d" % i)))
        hT = sb("hT", [128, 8, NB], BF16); hT_tt = [Buf("hT%d" % i) for i in range(4)]; hT_b = tuple(hT_tt)
        conv_preT = sb("conv_preT", [128, 4, NB], BF16); conv_pre_b = Buf("conv_preT")
        ys2T = sb("ys2T", [128, 4, NB], BF16); ys2_b = Buf("ys2T")
        oT = sb("oT", [128, 4, NB], BF16); oT_b = Buf("oT")
        cv = sb("cv", [128, 4, NB + 2], F32); cv_b = [Buf("cv%d" % i) for i in range(4)]
        tmpA = [(sb("tmpA%d" % i, [128, NB], F32), Buf("tmpA%d" % i)) for i in range(2)]
        LTT = sb("LTT", [128, 32, 256], BF16); LTT_b = Buf("LTT")
        VV = sb("VV", [128, 32, 128], BF16); VV_b = Buf("VV")
        S2re = sb("S2re", [128, 1024], F32); S2im = sb("S2im", [128, 1024], F32)
        U2re = sb("U2re", [128, 1024], F32); U2im = sb("U2im", [128, 1024], F32)
        tab_b = Buf("tables")
        HstA = sb("HstA", [66, 512], F32); HstB = sb("HstB", [2, 512], F32); Hst_b = [Buf("Hst%d" % q) for q in range(4)]
        HbfA = sb("HbfA", [66, 512], BF16); HbfB = sb("HbfB", [2, 512], BF16)
        U64A = sb("U64A", [66, 2, 256], F32); U64B = sb("U64B", [2, 2, 256], F32)
        kT = sb("kT", [128, 4, 256], BF16); kT_b = Buf("kT")
        Vm = sb("Vm", [128, 2, 512], BF16); Vm_b = Buf("Vm")

        def hst_rows(q):
            if q < 3:
                return (HstA[32 * q:32 * q + 2, :], U64A[32 * q:32 * q + 2, :, :], sel2_bf[32 * q:32 * q + 2, :],
                        ident[32 * q:32 * q + 2, 32 * q:32 * q + 32], 32 * q, HbfA[32 * q:32 * q + 2, :])
            return HstB[0:2, :], U64B[0:2, :, :], sel2_bf[0:2, :], ident[0:2, 0:32], 0, HbfB[0:2, :]

        def sin_turns(out, turns, tmp_i, tmp_f):
            P.copy("dve", tmp_i, turns)
            P.copy("dve", tmp_f, tmp_i)
            P.tt("dve", turns, turns, tmp_f, ALU.subtract)
            P.ts("dve", tmp_f, turns, 0.5, ALU.is_gt)
            P.tt("dve", turns, turns, tmp_f, ALU.subtract)
            P.ts("dve", tmp_f, turns, -0.5, ALU.is_lt)
            P.tt("dve", turns, turns, tmp_f, ALU.add)
            P.act(out, turns, AF.Sin, scale=TWO_PI)

        def norm_act(xt, xb, htm, htm_b, tt):
            P.act((htm[:, tt, :], htm_b), (xt[:, tt, :], xb), AF.Square, accum=(ss[:, tt:tt + 1], ss_b[tt]))
            P.act((rstd[:, tt:tt + 1], rstd_b[tt]), (ss[:, tt:tt + 1], ss_b[tt]), AF.Sqrt, scale=1.0 / D, bias=(epscol[:, 0:1], cbuf))

        def norm_dve(xt, xb, gb, htm, htm_b, tt):
            P.recip((rstd[:, tt:tt + 1], rstd_b[tt]), (rstd[:, tt:tt + 1], rstd_b[tt]))
            P.stt((htm[:, tt, :], htm_b), (xt[:, tt, :], xb), (rstd[:, tt:tt + 1], rstd_b[tt]), (gb[:], cbuf), ALU.mult, ALU.mult)

        def norm_stats(xt, xb, gb, htm, htm_b, tt):
            norm_act(xt, xb, htm, htm_b, tt)
            norm_dve(xt, xb, gb, htm, htm_b, tt)

        def norm_transposes(htm, htm_b, dstT, dstT_b, tt, eng=None):
            pt, pb = bank()
            pv = pt[:].bitcast(BF16)
            for kt in range(8):
                P.tr((pv[:, kt * 128:(kt + 1) * 128], pb), (htm[:, tt, kt * 128:(kt + 1) * 128], htm_b), (ident[:], cbuf))
            P.copy(eng or ("act" if tt % 2 == 0 else "dve"), (dstT[:, :, tt * 128:(tt + 1) * 128], dstT_b[tt] if isinstance(dstT_b, (list, tuple)) else dstT_b),
                   (pv[:, 0:1024].rearrange("p (k n) -> p k n", k=8), pb))

        with ExitStack() as pst:
            psem = P.new_sem("psem")
            memt = sb("memt", [128, 2, D], F32, pst); memt_b = Buf("memt")
            memh = sb("memh", [128, 2, D], BF16, pst); memh_b = Buf("memh")
            memT = sb("memT", [128, 8, 256], BF16, pst); memT_b = Buf("memT")
            P.dma("sp", (memt[:], memt_b), (mem.rearrange("(mt p) d -> p mt d", p=128), DRAM), psem)
            gmb = sb("gmb", [128, D], F32, pst)
            P.dma("sp", CB(gmb[:]), (mem_norm_g.unsqueeze(0).partition_broadcast(128), DRAM), psem)
            for tt in range(2):
                norm_stats(memt, memt_b, gmb, memh, memh_b, tt)
            for tt in range(2):
                norm_transposes(memh, memh_b, memT, memT_b, tt)
            wk_t, wk_b = slots[0][0], slots[0][1]
            wv_t, wv_b = slots[1][0], slots[1][1]
            wkv = wk_t[:, 0:4096].rearrange("p (kt n) -> p kt n", kt=8)
            wvv = wv_t[:, 0:4096].rearrange("p (kt n) -> p kt n", kt=8)
            P.dma("pool", (wkv, wk_b), (wsrc(attn_wk), DRAM), slots[0][2])
            P.dma("pool", (wvv, wv_b), (wsrc(attn_wv), DRAM), slots[1][2])
            for hh in range(4):
                pt, pb = bank()
                for kt in range(8):
                    P.mm((pt[:, 0:256], pb), (wkv[:, kt, hh * 128:(hh + 1) * 128], wk_b), (memT[:, kt, :], memT_b),
                         start=(kt == 0), stop=(kt == 7))
                P.copy("act", (kT[:, hh, :], kT_b), (pt[:, 0:256], pb))
            for mt in range(2):
                pt, pb = bank()
                for kt in range(8):
                    P.mm((pt[:, :], pb), (memT[:, kt, mt * 128:(mt + 1) * 128], memT_b), (wvv[:, kt, :], wv_b),
                         start=(kt == 0), stop=(kt == 7))
                P.copy("dve", (Vm[:, mt, :], Vm_b), (pt[:, :], pb))
            P.barrier()

        chk('proA')
        with ExitStack() as pst:
            s5sem = P.new_sem("s5sem")
            pb_ = Buf("s5pro")
            PB = lambda ap: (ap, pb_)
            a_nat = sb("a_nat", [32, 2, 128], F32, pst)
            for half in range(2):
                P.dma("sp", PB(a_nat[:, 0, half * 64:(half + 1) * 64]), (A_re, DRAM), s5sem)
                P.dma("sp", PB(a_nat[:, 1, half * 64:(half + 1) * 64]), (A_im, DRAM), s5sem)
            aT = sb("aT", [128, 2, 32], F32, pst)
            for k in range(2):
                pt, pb = bank()
                P.tr((pt[:, 0:32], pb), PB(a_nat[:, k, :]), (identf[0:32, 0:32], cbuf))
                P.copy("dve", PB(aT[:, k, :]), (pt[:, 0:32], pb))
            dtb = sb("dtb", [128, 32], F32, pst)
            P.dma("sp", PB(dtb[:]), (log_dt.partition_broadcast(128), DRAM), s5sem)
            P.act(PB(dtb[:]), PB(dtb[:]), AF.Exp)
            chk('pb0')
            rho = sb("rho", [128, 32], F32, pst)
            thetat = sb("thetat", [128, 32], F32, pst)
            P.tt("dve", PB(rho[:]), PB(aT[:, 0, :]), PB(dtb[:]), ALU.mult)
            P.tt("dve", PB(thetat[:]), PB(aT[:, 1, :]), PB(dtb[:]), ALU.mult)
            P.ts("dve", PB(thetat[:]), PB(thetat[:]), 1.0 / TWO_PI, ALU.mult)
            chk('pb1')
            sm = [sb("sm%d" % i, [128, 32, 8], F32, pst) for i in range(8)]
            smi = sb("smi", [128, 32, 8], I32, pst)
            e1 = sm[0][:, :, 0]; c1 = sm[1][:, :, 0]; s1_ = sm[2][:, :, 0]; sh = sm[3][:, :, 0]
            t0_ = sm[4][:, :, 0]; t1_ = sm[5][:, :, 0]; t2_ = sm[6][:, :, 0]; t3_ = sm[7][:, :, 0]
            ti0 = smi[:, :, 0]
            P.act(PB(e1), PB(rho[:]), AF.Exp)
            P.copy("dve", PB(t0_), PB(thetat[:]))
            sin_turns(PB(s1_), PB(t0_), PB(ti0), PB(t1_))
            P.ts("dve", PB(t0_), PB(thetat[:]), 0.5, ALU.mult)
            sin_turns(PB(sh), PB(t0_), PB(ti0), PB(t1_))
            P.tt("dve", PB(t0_), PB(sh), PB(sh), ALU.mult)
            P.ts("dve", PB(t0_), PB(t0_), -2.0, ALU.mult)
            P.ts("dve", PB(c1), PB(t0_), 1.0, ALU.add)
            P.ts("dve", PB(t1_), PB(e1), -1.0, ALU.add)
            P.tt("dve", PB(t1_), PB(t1_), PB(c1), ALU.mult)
            P.tt("dve", PB(t1_), PB(t1_), PB(t0_), ALU.add)
            P.tt("dve", PB(t2_), PB(e1), PB(s1_), ALU.mult)
            P.tt("dve", PB(t0_), PB(aT[:, 0, :]), PB(aT[:, 0, :]), ALU.mult)
            P.tt("dve", PB(t3_), PB(aT[:, 1, :]), PB(aT[:, 1, :]), ALU.mult)
            P.tt("dve", PB(t0_), PB(t0_), PB(t3_), ALU.add)
            P.recip(PB(t0_), PB(t0_))
            kap = sb("kap", [128, 2, 32], F32, pst)
            P.tt("dve", PB(t3_), PB(t1_), PB(aT[:, 0, :]), ALU.mult)
            P.tt("dve", PB(c1), PB(t2_), PB(aT[:, 1, :]), ALU.mult)
            P.tt("dve", PB(t3_), PB(t3_), PB(c1), ALU.add)
            P.tt("dve", PB(kap[:, 0, :]), PB(t3_), PB(t0_), ALU.mult)
            P.tt("dve", PB(t3_), PB(t2_), PB(aT[:, 0, :]), ALU.mult)
            P.tt("dve", PB(c1), PB(t1_), PB(aT[:, 1, :]), ALU.mult)
            P.tt("dve", PB(t3_), PB(t3_), PB(c1), ALU.subtract)
            P.tt("dve", PB(kap[:, 1, :]), PB(t3_), PB(t0_), ALU.mult)

            chk('pb2')
            Bb = sb("Bb", [128, 2, 32, 16], F32, pst)
            Cz = sb("Cz", [128, 2, 512], F32, pst)
            pst2 = ExitStack()
            Bz = sb("Bz", [128, 2, 32, 16], F32, pst2)
            for half in range(2):
                for q in range(4):
                    P.dma("sp", PB(Bz[half * 64:(half + 1) * 64, 0, q * 8:(q + 1) * 8, :]),
                          (B_re[q * 8:(q + 1) * 8].rearrange("g p i -> p g i"), DRAM), s5sem, slow=True)
                    P.dma("sp", PB(Bz[half * 64:(half + 1) * 64, 1, q * 8:(q + 1) * 8, :]),
                          (B_im[q * 8:(q + 1) * 8].rearrange("g p i -> p g i"), DRAM), s5sem, slow=True)
            bt0 = sb("bt0", [128, 32, 16], F32, pst2)
            bt1 = sb("bt1", [128, 32, 16], F32, pst2)
            kre_b = kap[:, 0, :].unsqueeze(2).to_broadcast([128, 32, 16])
            kim_b = kap[:, 1, :].unsqueeze(2).to_broadcast([128, 32, 16])
            P.tt("dve", PB(bt0[:]), PB(Bz[:, 0]), PB(kre_b), ALU.mult)
            P.tt("dve", PB(bt1[:]), PB(Bz[:, 1]), PB(kim_b), ALU.mult)
            P.tt("dve", PB(Bb[:, 0]), PB(bt0[:]), PB(bt1[:]), ALU.subtract)
            P.tt("dve", PB(bt0[:]), PB(Bz[:, 1]), PB(kre_b), ALU.mult)
            P.tt("dve", PB(bt1[:]), PB(Bz[:, 0]), PB(kim_b), ALU.mult)
            P.tt("dve", PB(Bb[:, 1]), PB(bt0[:]), PB(bt1[:]), ALU.add)

            chk('pb3')
            c_nat = sb("c_nat", [128, 2, 4, 128], F32, pst2)
            for half in range(2):
                P.dma("sp", PB(c_nat[:, 0, :, half * 64:(half + 1) * 64]), (C_re.rearrange("(t r) p -> r t p", r=128), DRAM), s5sem)
                P.dma("sp", PB(c_nat[:, 1, :, half * 64:(half + 1) * 64]), (C_im.rearrange("(t r) p -> r t p", r=128), DRAM), s5sem)
            for k in range(2):
                pt, pb = bank()
                for tl in range(4):
                    P.tr((pt[:, tl * 128:(tl + 1) * 128], pb), PB(c_nat[:, k, tl, :]), (identf[:], cbuf))
                P.copy("dve", PB(Cz[:, k, :]), (pt[:, :], pb))

            chk('pb4')
            pst2.close()
            phcol = sb("phcol", [128, 1], F32, pst)
            sgcol = sb("sgcol", [128, 1], F32, pst)
            P.memset("dve", PB(phcol[:]), 0.0)
            P.memset("dve", PB(phcol[64:128, :]), -0.25)
            P.memset("dve", PB(sgcol[:]), 1.0)
            P.memset("dve", PB(sgcol[64:128, :]), -1.0)
            evec = sb("evec", [128, 8], F32, pst)
            P.op("pool", lambda h: h.iota(evec[:], pattern=[[1, 8]], base=0, channel_multiplier=0,
                                          allow_small_or_imprecise_dtypes=True), writes=[pb_])
            GQ = 8
            big0 = sb("big0", [128, GQ, 8, 16], F32, pst)
            big1 = sb("big1", [128, GQ, 8, 16], F32, pst)
            Lst = sb("Lst", [128, GQ, 128], F32, pst)
            Rst = sb("Rst", [128, GQ, 128], F32, pst)
            Dcol = sb("Dcol", [128, 32], F32, pst)
            Dcol_b = Buf("Dcol")
            for s_ in range(8):
                P.dma("sp", (Dcol[s_ * 16:(s_ + 1) * 16, :], Dcol_b), (ssm_D.rearrange("(g i) -> i g", i=16), DRAM), s5sem, slow=True)
            evs = sb("evs", [128, 8], F32, pst)

            def build_stack(g0, Zre, Zim, eoff, esign, use_sg):
                P.ts("dve", PB(evs[:]), PB(evec[:]), float(esign), ALU.mult, float(eoff), ALU.add)
                evb = evs[:].unsqueeze(1).to_broadcast([128, GQ, 8])
                thb = thetat[:, g0:g0 + GQ].unsqueeze(2).to_broadcast([128, GQ, 8])
                rhb = rho[:, g0:g0 + GQ].unsqueeze(2).to_broadcast([128, GQ, 8])
                ang = sm[1][:, 0:GQ, :]; mag = sm[2][:, 0:GQ, :]; cosA = sm[3][:, 0:GQ, :]; sinA = sm[4][:, 0:GQ, :]
                tf = sm[5][:, 0:GQ, :]; ang2 = sm[6][:, 0:GQ, :]; ti = smi[:, 0:GQ, :]
                P.tt("dve", PB(ang), PB(thb), PB(evb), ALU.mult)
                P.ts("dve", PB(ang), PB(ang), (phcol[:, 0:1], pb_), ALU.add)
                P.ts("dve", PB(ang2), PB(ang), 0.25, ALU.add)
                sin_turns(PB(sinA), PB(ang), PB(ti), PB(tf))
                sin_turns(PB(cosA), PB(ang2), PB(ti), PB(tf))
                P.tt("dve", PB(mag), PB(rhb), PB(evb), ALU.mult)
                P.act(PB(mag), PB(mag), AF.Exp)
                if use_sg:
                    P.ts("dve", PB(mag), PB(mag), (sgcol[:, 0:1], pb_), ALU.mult)
                P.tt("dve", PB(cosA), PB(cosA), PB(mag), ALU.mult)
                P.tt("dve", PB(sinA), PB(sinA), PB(mag), ALU.mult)
                cb = cosA.unsqueeze(3).to_broadcast([128, GQ, 8, 16])
                sbb = sinA.unsqueeze(3).to_broadcast([128, GQ, 8, 16])
                zr = Zre[:, g0:g0 + GQ, :].unsqueeze(2).to_broadcast([128, GQ, 8, 16])
                zi = Zim[:, g0:g0 + GQ, :].unsqueeze(2).to_broadcast([128, GQ, 8, 16])
                P.tt("dve", PB(big0[:]), PB(zr), PB(cb), ALU.mult)
                P.tt("dve", PB(big1[:]), PB(zi), PB(sbb), ALU.mult)
                P.tt("dve", PB(big0[:]), PB(big0[:]), PB(big1[:]), ALU.subtract)

            chk('pb5')
            Czre = Cz[:, 0, :].rearrange("p (g o) -> p g o", g=32)
            Czim = Cz[:, 1, :].rearrange("p (g o) -> p g o", g=32)
            for g0 in range(0, 32, GQ):
                build_stack(g0, Bb[:, 0], Bb[:, 1], 7.0, -1.0, False)
                chk('pb6')
                P.copy("dve", PB(Lst[:].rearrange("p g (s i) -> p g s i", s=8)), PB(big0[:]))
                build_stack(g0, Czre, Czim, -7.0, 1.0, True)
                P.copy("dve", PB(Rst[:].rearrange("p g (s i) -> p g s i", s=8)), PB(big0[:]))
                build_stack(g0, Czre, Czim, 1.0, 1.0, True)
                P.copy("dve", (VV[:, g0:g0 + GQ, :].rearrange("p g (s i) -> p g s i", s=8), VV_b), PB(big0[:]))
                chk('pb7')
                for gl in range(GQ):
                    g = g0 + gl
                    if gl == 1:
                        chk('pb8')
                    pt, pb = bank()
                    import os
                    DIS = os.environ.get("DIS", "")
                    if "tr" not in DIS:
                        P.tr((pt[:, 0:128], pb), PB(Lst[:, gl, :]), (identf[:], cbuf))
                    if "mm" not in DIS:
                        P.mm((pt[:, 128:256], pb), PB(Lst[:, gl, :]), PB(Rst[:, gl, :]))
                    if "cp" not in DIS:
                        P.copy("act", (LTT[:, g, 0:128], LTT_b), (pt[:, 0:128], pb))
                    tm, tmb = tmpA[g % 2]
                    if "tt" not in DIS:
                        _o, _a, _b = tm[:, 0:128], pt[:, 128:256], tmask[:]
                        P.op("dve", lambda h, _o=_o, _a=_a, _b=_b: h.tensor_tensor(out=_o, in0=_a, in1=_b, op=ALU.mult),
                             reads=[pb, cbuf, LTT_b], writes=[tmb])
                    if "st" not in DIS:
                        P.ts("dve", (tm[:, 128:256], tmb), (identf[:], cbuf), (Dcol[:, g:g + 1], Dcol_b), ALU.mult)
                        P.tt("dve", (LTT[:, g, 128:256], LTT_b), (tm[:, 0:128], tmb), (tm[:, 128:256], tmb), ALU.add)
            P.barrier()

        chk('proB')
        with ExitStack() as pst:
            s5sem2 = P.new_sem("s5sem2")
            pb_ = Buf("s5pro2")
            PB = lambda ap: (ap, pb_)

            def make_tables(npart, arow, dtrow, ccols, outs, ri, rf, rrow, trow, ta, ta2, tmag, ngl):
                n = ngl * 64
                P.act(PB(dtrow), PB(dtrow), AF.Exp)
                dtb2 = dtrow.unsqueeze(2).to_broadcast([npart, ngl, 64])
                P.tt("dve", PB(rrow.rearrange("c (g p) -> c g p", g=ngl)), PB(arow[:, 0, :].rearrange("c (g p) -> c g p", g=ngl)), PB(dtb2), ALU.mult)
                P.tt("dve", PB(trow.rearrange("c (g p) -> c g p", g=ngl)), PB(arow[:, 1, :].rearrange("c (g p) -> c g p", g=ngl)), PB(dtb2), ALU.mult)
                P.ts("dve", PB(trow), PB(trow), 1.0 / TWO_PI, ALU.mult)
                P.copy("dve", PB(ri), PB(trow))
                P.copy("dve", PB(rf), PB(ri))
                P.tt("dve", PB(trow), PB(trow), PB(rf), ALU.subtract)
                for (col, dre, dim_) in outs:
                    P.ts("dve", PB(ta), PB(trow), (col, pb_), ALU.mult)
                    P.ts("dve", PB(ta2), PB(ta), 0.25, ALU.add)
                    P.act(PB(tmag), PB(rrow), AF.Exp, scale=(col, pb_))
                    sin_turns(dim_, PB(ta), PB(ri), PB(rf))
                    sin_turns(dre, PB(ta2), PB(ri), PB(rf))
                    P.tt("dve", dim_, dim_, PB(tmag), ALU.mult)
                    P.tt("dve", dre, dre, PB(tmag), ALU.mult)

            arow2 = sb("arow2", [128, 2, 1024], F32, pst)
            dtrow2 = sb("dtrow2", [128, 16], F32, pst)
            A_re_h = A_re.rearrange("(h g) p -> h (g p)", h=2)
            A_im_h = A_im.rearrange("(h g) p -> h (g p)", h=2)
            ldt_h = log_dt.rearrange("o (h g) -> (o h) g", h=2)
            for h in range(2):
                P.dma("sp", PB(arow2[64 * h:64 * h + 64, 0, :]), (A_re_h[h:h + 1, :].partition_broadcast(64), DRAM), s5sem2)
                P.dma("sp", PB(arow2[64 * h:64 * h + 64, 1, :]), (A_im_h[h:h + 1, :].partition_broadcast(64), DRAM), s5sem2)
                P.dma("sp", PB(dtrow2[64 * h:64 * h + 64, :]), (ldt_h[h:h + 1, :].partition_broadcast(64), DRAM), s5sem2)
            ccol = sb("ccol", [128, 2], F32, pst)
            for h in range(2):
                P.op("pool", lambda hh, h=h: hh.iota(ccol[64 * h:64 * h + 64, 0:1], pattern=[[0, 1]], base=0, channel_multiplier=8,
                                                      allow_small_or_imprecise_dtypes=True), writes=[pb_])
            P.ts("dve", PB(ccol[:, 1:2]), PB(ccol[:, 0:1]), -1.0, ALU.mult, -8.0, ALU.add)
            ri = sb("ri", [128, 1024], I32, pst); rf = sb("rf", [128, 1024], F32, pst)
            rrow = sb("rrow", [128, 1024], F32, pst); trow = sb("trow", [128, 1024], F32, pst)
            ta = sb("ta", [128, 1024], F32, pst); ta2 = sb("ta2", [128, 1024], F32, pst); tmag = sb("tmag", [128, 1024], F32, pst)
            make_tables(128, arow2[:], dtrow2[:], None,
                        [(ccol[:, 0:1], (U2re[:], tab_b), (U2im[:], tab_b)), (ccol[:, 1:2], (S2re[:], tab_b), (S2im[:], tab_b))],
                        ri[:], rf[:], rrow[:], trow[:], ta[:], ta2[:], tmag[:], 16)
            arow64 = sb("arow64", [66, 2, 256], F32, pst)
            dtrow64 = sb("dtrow64", [66, 4], F32, pst)
            arowB = sb("arowB", [2, 2, 256], F32, pst)
            dtrowB = sb("dtrowB", [2, 4], F32, pst)
            P.memset("dve", PB(arow64[:]), 0.0)
            P.memset("dve", PB(dtrow64[:]), 0.0)
            A_re_hq = A_re.rearrange("(h q g) p -> h q (g p)", h=2, q=4)
            A_im_hq = A_im.rearrange("(h q g) p -> h q (g p)", h=2, q=4)
            ldt_hq = log_dt.rearrange("o (h q g) -> (o h) q g", h=2, q=4)
            for q in range(4):
                if q < 3:
                    dst_a, dst_d, base = arow64, dtrow64, 32 * q
                else:
                    dst_a, dst_d, base = arowB, dtrowB, 0
                P.dma("sp", PB(dst_a[base:base + 2, 0, :]), (A_re_hq[:, q, :], DRAM), s5sem2)
                P.dma("sp", PB(dst_a[base:base + 2, 1, :]), (A_im_hq[:, q, :], DRAM), s5sem2)
                P.dma("sp", PB(dst_d[base:base + 2, :]), (ldt_hq[:, q, :], DRAM), s5sem2)
            c64 = sb("c64", [66, 1], F32, pst)
            P.memset("dve", PB(c64[:]), 512.0)
            make_tables(66, arow64[:], dtrow64[:], None,
                        [(c64[:, 0:1], (U64A[:, 0, :], tab_b), (U64A[:, 1, :], tab_b))],
                        ri[0:66, 0:256], rf[0:66, 0:256], rrow[0:66, 0:256], trow[0:66, 0:256], ta[0:66, 0:256], ta2[0:66, 0:256], tmag[0:66, 0:256], 4)
            make_tables(2, arowB[:], dtrowB[:], None,
                        [(c64[0:2, 0:1], (U64B[:, 0, :], tab_b), (U64B[:, 1, :], tab_b))],
                        ri[0:2, 256:512], rf[0:2, 256:512], rrow[0:2, 256:512], trow[0:2, 256:512], ta[0:2, 256:512], ta2[0:2, 256:512], tmag[0:2, 256:512], 4)
            P.memset("dve", (HstA[:], Hst_b[0]), 0.0)
            P.memset("dve", (HstB[:], Hst_b[3]), 0.0)
            P.memset("dve", (HbfA[:], Hst_b[0]), 0.0)
            P.memset("dve", (HbfB[:], Hst_b[3]), 0.0)
            for ft in range(4):
                P.memset("pool", (cv[:, ft, 0:2], cv_b[ft]), 0.0)
            P.barrier()

        chk('proC')
        dbg("LTT", LTT[:], LTT_b, [128, 32, 256])
        dbg("VV", VV[:], VV_b, [128, 32, 128])
        dbg("U2re", U2re[:], tab_b, [128, 1024])
        dbg("U2im", U2im[:], tab_b, [128, 1024])
        dbg("S2re", S2re[:], tab_b, [128, 1024])
        dbg("S2im", S2im[:], tab_b, [128, 1024])
        dbg("U64A", U64A[:], tab_b, [66, 2, 256])
        dbg("U64B", U64B[:], tab_b, [2, 2, 256])
        dbg("kT", kT[:], kT_b, [128, 4, 256])
        dbg("Vm", Vm[:], Vm_b, [128, 2, 512])

        ARENA_B = 43008
        arena = sb("arena", [128, ARENA_B // 2], BF16)
        arena_bufs = []

        def aview(off, nbytes, dt, name):
            assert off + nbytes <= ARENA_B
            v = arena[:, off // 2:(off + nbytes) // 2]
            if dt == F32:
                v = v.bitcast(F32)
            b = Buf(name)
            b.rng = (off, off + nbytes)
            arena_bufs.append(b)
            return v, b

        def phase(bufs):
            for nb in bufs:
                ev = {}
                for ab in arena_bufs:
                    if ab is not nb and (ab.rng[1] <= nb.rng[0] or nb.rng[1] <= ab.rng[0]):
                        continue
                    if ab.w is not None:
                        s_, v = ab.w
                        if ev.get(s_, 0) < v:
                            ev[s_] = v
                    for s_, v in ab.r.items():
                        if ev.get(s_, 0) < v:
                            ev[s_] = v
                nb.w = None
                nb.r = ev

        u_tm_f, u_tm_b = aview(0, 4096, BF16, "u_tm"); u_tm = u_tm_f.rearrange("p (g s i) -> p g s i", g=16, s=8)
        yg_f, yg_b = aview(4096, 4096, BF16, "yg"); yg = yg_f.rearrange("p (t c) -> p t c", t=8)
        Ut_f, Ut_b = aview(8192, 4096, BF16, "Ut"); Ut = Ut_f.rearrange("p (g c) -> p g c", g=32)
        HT_f, HT_b = aview(12288, 4096, BF16, "HT"); HT = HT_f.rearrange("p (g c) -> p g c", g=32)
        wk1, wk1_b = aview(16384, 2048, F32, "wk1")
        wk2, wk2_b = aview(18432, 2048, F32, "wk2")
        hs, hs_b = aview(20480, 1024, BF16, "hs")
        hprev, hprev_b = aview(21504, 1024, BF16, "hprev")
        hrow, hrow_b = aview(22528, 2048, F32, "hrow")
        hrow2, hrow2_b = aview(24576, 2048, F32, "hrow2")
        ysT_f, ysT_b = aview(26624, 4096, BF16, "ysT"); ysT = ysT_f.rearrange("p (k n) -> p k n", k=4)
        qT_f, qT_b = aview(30720, 4096, BF16, "qT"); qT = qT_f.rearrange("p (k n) -> p k n", k=4)
        eT = []
        for i in range(2):
            v, b_ = aview(34816 + 2048 * i, 2048, BF16, "eT%d" % i)
            eT.append((v.rearrange("p (k n) -> p k n", k=2), b_))
        rec, rec_b = aview(38912, 2048, F32, "rec")
        tmpB = [aview(40960, 2048, F32, "tmpB0"), aview(36864, 2048, F32, "tmpB1")]
        h_tm_f, h_tm_b = aview(30720, 8192, BF16, "h_tm"); h_tm = h_tm_f.rearrange("p (t d) -> p t d", t=4)
        junk_f, junk_b = aview(22528, 8192, BF16, "junk"); junk = junk_f.rearrange("p (t d) -> p t d", t=4)
        merged_f, merged_b = aview(0, 16384, F32, "merged"); merged = merged_f.rearrange("p (k n) -> p k n", k=8)
        mergedT_f, mergedT_b = aview(16384, 8192, BF16, "mergedT"); mergedT = mergedT_f.rearrange("p (k n) -> p k n", k=8)
        actT_f, actT_b = aview(0, 22528, BF16, "actT"); actT = actT_f.rearrange("p (k n) -> p k n", k=22)
        S5_BUFS = [u_tm_b, yg_b, Ut_b, HT_b, wk1_b, wk2_b, hs_b, hprev_b, hrow_b, hrow2_b, ysT_b]
        P2_BUFS = [qT_b, eT[0][1], eT[1][1], rec_b, tmpB[0][1], tmpB[1][1]]

        xv = x.rearrange("(b tt p) d -> b p tt d", p=128, tt=4)
        ov = out.rearrange("(b tt p) d -> b p tt d", p=128, tt=4)

        def x_load(b):
            t, buf, ld, stsem = xbuf[b % 2]
            P.dma("pool", (t[:], buf), (xv[b], DRAM), ld)

        x_load(0)
        inv_sqrt_hd = 1.0 / math.sqrt(128.0)

        def cmul(np0, np1, src, src_b, tre, tim, t1, t1b, t2, t2b, ng):
            npn = np1 - np0
            treb = tre.unsqueeze(2).to_broadcast([npn, ng, 2, 64])
            P.tt("dve", (t1, t1b), (src, src_b), (treb, tab_b), ALU.mult)
            P.tt("dve", (t2[:, :, 0, :], t2b), (src[:, :, 1, :], src_b), (tim, tab_b), ALU.mult)
            P.tt("dve", (t2[:, :, 1, :], t2b), (src[:, :, 0, :], src_b), (tim, tab_b), ALU.mult)

        for b in range(NBLK):
            xt, xb, _, xst = xbuf[b % 2]
            if b + 1 < NBLK:
                x_load(b + 1)
            if b == 0:
                phase([h_tm_b])
                for tt in range(4):
                    norm_stats(xt, xb, g1b, h_tm, h_tm_b, tt)
                for tt in range(4):
                    norm_transposes(h_tm, h_tm_b, hT, hT_b, tt)
                dbg("hT", hT[:], hT_b, [128, 8, NB])

            chk('s1')
            phase(S5_BUFS)
            wt, wb = w_get("win3")
            wv_ = wt[:, 0:4096].rearrange("p (kt n) -> p kt n", kt=8)
            hTs = hT[:].rearrange("p kt (c s) -> p kt s c", s=8)
            for s_ in range(8):
                pt, pb = bank()
                for h in range(2):
                    for kt in range(8):
                        P.mm((pt[64 * h:64 * h + 64, 0:256], pb), (hTs[:, kt, s_, :], hT_b), (wv_[:, kt, 256 * h:256 * h + 256], wb),
                             start=(kt == 0), stop=(kt == 7))
                P.copy("act", (u_tm[:, :, s_, :], u_tm_b), (pt[:, 0:256].rearrange("p (g i) -> p g i", g=16), pb))
            w_rel("win3")
            if b == 0:
                dbg("u_tm", u_tm_f, u_tm_b, [128, 2048])

            chk('u')
            Ut4 = Ut_f.rearrange("p (h gl c) -> p h gl c", h=2, gl=16)
            for jb in range(2):
                pt, pb = bank()
                pv = pt[:].bitcast(BF16)
                for g8 in range(8):
                    gl = jb * 8 + g8
                    P.tr((pv[:, g8 * 128:(g8 + 1) * 128], pb), (u_tm_f[:, gl * 128:(gl + 1) * 128], u_tm_b), (ident[:], cbuf))
                P.copy("act", (Ut4[:, :, jb * 8:(jb + 1) * 8, :].rearrange("p h gl c -> p gl h c"), Ut_b),
                       (pv[:, 0:1024].rearrange("p (gl h c) -> p gl h c", gl=8, h=2), pb))
            chk('s5a')
            def s5_steps():
                for q in range(4):
                    gof = lambda h, k: 16 * h + 4 * q + k
                    pA, pAb = bank()
                    for h in range(2):
                        for k in range(4):
                            g = gof(h, k)
                            P.mm((pA[64 * h:64 * h + 64, k * 128:(k + 1) * 128], pAb), (Ut[:, g, :], Ut_b), (LTT[:, g, 0:128], LTT_b))
                    src = pA[:, :].rearrange("c (g r p) -> c g r p", g=4, r=2)
                    tre = S2re[:, q * 256:(q + 1) * 256].rearrange("c (g p) -> c g p", g=4)
                    tim = S2im[:, q * 256:(q + 1) * 256].rearrange("c (g p) -> c g p", g=4)
                    t1 = wk1.rearrange("c (g r p) -> c g r p", g=4, r=2)
                    t2 = wk2.rearrange("c (g r p) -> c g r p", g=4, r=2)
                    d = hs.rearrange("c (g r p) -> c g r p", g=4, r=2)
                    cmul(0, 128, src, pAb, tre, tim, t1, wk1_b, t2, wk2_b, 4)
                    P.tt("pool", (d[:, :, 0, :], hs_b), (t1[:, :, 0, :], wk1_b), (t2[:, :, 0, :], wk2_b), ALU.subtract)
                    P.tt("pool", (d[:, :, 1, :], hs_b), (t1[:, :, 1, :], wk1_b), (t2[:, :, 1, :], wk2_b), ALU.add)
                    if q == 0: chk('s5c')
                    yield
                    hrows, u64, selr, i2r, hbase, hbf = hst_rows(q)
                    pC, pCb = bank()
                    pT, pTb = bank()
                    P.mm((pC[:, :], pCb), (trix2_bf[:, :], cbuf), (hs, hs_b), start=True, stop=False)
                    P.mm((pC[:, :], pCb), (selr, cbuf), (hbf, Hst_b[q]), start=False, stop=True)
                    P.mm((pT[hbase:hbase + 32, :], pTb), (ones2_bf[:, :], cbuf), (hs, hs_b), start=True, stop=False)
                    P.mm((pT[hbase:hbase + 32, :], pTb), (i2r, cbuf), (hbf, Hst_b[q]), start=False, stop=True)
                    if q == 0: chk('s5d')
                    src = pC[:, :].rearrange("c (g r p) -> c g r p", g=4, r=2)
                    tre = U2re[:, q * 256:(q + 1) * 256].rearrange("c (g p) -> c g p", g=4)
                    tim = U2im[:, q * 256:(q + 1) * 256].rearrange("c (g p) -> c g p", g=4)
                    cmul(0, 128, src, pCb, tre, tim, t1, wk1_b, t2, wk2_b, 4)
                    dh = hprev.rearrange("c (g r p) -> c g r p", g=4, r=2)
                    P.tt("pool", (dh[:, :, 0, :], hprev_b), (t1[:, :, 0, :], wk1_b), (t2[:, :, 0, :], wk2_b), ALU.subtract)
                    P.tt("pool", (dh[:, :, 1, :], hprev_b), (t1[:, :, 1, :], wk1_b), (t2[:, :, 1, :], wk2_b), ALU.add)
                    srcT = pT[hbase:hbase + 2, :].rearrange("c (g r p) -> c g r p", g=4, r=2)
                    r1 = hrow[hbase:hbase + 2, :].rearrange("c (g r p) -> c g r p", g=4, r=2)
                    r2 = hrow2[hbase:hbase + 2, :].rearrange("c (g r p) -> c g r p", g=4, r=2)
                    ure = u64[:, 0, :].rearrange("c (g p) -> c g p", g=4)
                    uim = u64[:, 1, :].rearrange("c (g p) -> c g p", g=4)
                    cmul(hbase, hbase + 2, srcT, pTb, ure, uim, r1, hrow_b, r2, hrow2_b, 4)
                    hd = hrows.rearrange("c (g r p) -> c g r p", g=4, r=2)
                    P.tt("dve", (hd[:, :, 0, :], Hst_b[q]), (r1[:, :, 0, :], hrow_b), (r2[:, :, 0, :], hrow2_b), ALU.subtract)
                    P.tt("dve", (hd[:, :, 1, :], Hst_b[q]), (r1[:, :, 1, :], hrow_b), (r2[:, :, 1, :], hrow2_b), ALU.add)
                    P.copy("dve", (hbf, Hst_b[q]), (hrows, Hst_b[q]))
                    if q == 0: chk('s5e')
                    yield
                    pt, pb = bank()
                    pv = pt[:].bitcast(BF16)
                    for k in range(4):
                        P.tr((pv[:, k * 128:(k + 1) * 128], pb), (hprev[:, k * 128:(k + 1) * 128], hprev_b), (ident[:], cbuf))
                    HT4 = HT_f.rearrange("p (h gl c) -> p h gl c", h=2, gl=16)
                    P.copy("act", (HT4[:, :, 4 * q:4 * q + 4, :].rearrange("p h k c -> p k h c"), HT_b),
                           (pv[:, 0:512].rearrange("p (k h c) -> p k h c", k=4, h=2), pb))
                    if q == 0: chk('s5f')
                    pE, pEb = bank()
                    for h in range(2):
                        for k in range(4):
                            g = gof(h, k)
                            o_ = (pE[64 * h:64 * h + 64, k * 128:(k + 1) * 128], pEb)
                            P.mm(o_, (Ut[:, g, :], Ut_b), (LTT[:, g, 128:256], LTT_b), start=True, stop=False)
                            P.mm(o_, (HT[:, g, :], HT_b), (VV[:, g, :], VV_b), start=False, stop=True)
                    src = pE[:, :].rearrange("c (g t o) -> c t g o", g=4, t=8)
                    dstv = yg[:, :, q * 64:(q + 1) * 64].rearrange("c t (g o) -> c t g o", g=4)
                    P.act((dstv, yg_b), (src, pEb), AF.Gelu_apprx_tanh)
                    yield
            def bulk_units():
                wt, wb = w_get("win4")
                wv_ = wt[:, 0:4096].rearrange("p (kt n) -> p kt n", kt=8)
                for hh in range(4):
                    pt, pb = bank()
                    for kt in range(8):
                        P.mm((pt[:, :], pb), (wv_[:, kt, hh * 128:(hh + 1) * 128], wb), (hT[:, kt, :], hT_b), start=(kt == 0), stop=(kt == 7))
                    P.copy("act", (qT[:, hh, :], qT_b), (pt[:, :], pb))
                    yield
                w_rel("win4")
                for hh in range(4):
                    et, eb = eT[hh % 2]
                    for mt in range(2):
                        pt, pb = bank()
                        P.mm((pt[:, :], pb), (kT[:, hh, mt * 128:(mt + 1) * 128], kT_b), (qT[:, hh, :], qT_b))
                        P.act((et[:, mt, :], eb), (pt[:, :], pb), AF.Exp, scale=inv_sqrt_hd)
                    yield
                    po, pob = bank()
                    pd, pdb = bank()
                    for mt in range(2):
                        P.mm((po[:, :], pob), (Vm[:, mt, hh * 128:(hh + 1) * 128], Vm_b), (et[:, mt, :], eb), start=(mt == 0), stop=(mt == 1))
                    for mt in range(2):
                        P.mm((pd[:, :], pdb), (ones_bf[:, :], cbuf), (et[:, mt, :], eb), start=(mt == 0), stop=(mt == 1))
                    P.recip((rec, rec_b), (pd[:, :], pdb))
                    P.tt("dve", (oT[:, hh, :], oT_b), (po[:, :], pob), (rec, rec_b), ALU.mult)
                    yield
                if b == 0:
                    dbg("oT", oT[:], oT_b, [128, 4, NB])

                wc_t, wc_b = w_get("win1")
                wv_t, wv_b2 = w_get("win2")
                wcv = wc_t[:, 0:4096].rearrange("p (kt n) -> p kt n", kt=8)
                wvv2 = wv_t[:, 0:4096].rearrange("p (kt n) -> p kt n", kt=8)
                for ft in range(4):
                    pc, pcb = bank()
                    pvv, pvb = bank()
                    for kt in range(8):
                        P.mm((pc[:, :], pcb), (wcv[:, kt, ft * 128:(ft + 1) * 128], wc_b), (hT[:, kt, :], hT_b), start=(kt == 0), stop=(kt == 7))
                    for kt in range(8):
                        P.mm((pvv[:, :], pvb), (wvv2[:, kt, ft * 128:(ft + 1) * 128], wv_b2), (hT[:, kt, :], hT_b), start=(kt == 0), stop=(kt == 7))
                    ta_, tab_ = tmpA[ft % 2]
                    P.copy("act", (ta_[:], tab_), (pc[:, :], pcb))
                    P.tt("dve", (cv[:, ft, 2:NB + 2], cv_b[ft]), (ta_[:], tab_), (pvv[:, :], pvb), ALU.mult)
                    yield
                w_rel("win1")
                w_rel("win2")
                phase([tmpB[1][1]])
                wb_t, wb_b = w_get("win0")
                wbv = wb_t[:, 0:4096].rearrange("p (kt n) -> p kt n", kt=8)
                for ft in range(4):
                    pbk, pbb = bank()
                    for kt in range(8):
                        P.mm((pbk[:, :], pbb), (wbv[:, kt, ft * 128:(ft + 1) * 128], wb_b), (hT[:, kt, :], hT_b), start=(kt == 0), stop=(kt == 7))
                    tb_, tbb_ = tmpB[ft % 2]
                    P.ts("dve", (tb_, tbb_), (cv[:, ft, 2:NB + 2], cv_b[ft]), (convw[:, 2, ft:ft + 1], cbuf), ALU.mult)
                    P.stt((tb_, tbb_), (cv[:, ft, 1:NB + 1], cv_b[ft]), (convw[:, 1, ft:ft + 1], cbuf), (tb_, tbb_), ALU.mult, ALU.add)
                    P.stt((tb_, tbb_), (cv[:, ft, 0:NB], cv_b[ft]), (convw[:, 0, ft:ft + 1], cbuf), (tb_, tbb_), ALU.mult, ALU.add)
                    P.tt("dve", (conv_preT[:, ft, :], conv_pre_b), (tb_, tbb_), (pbk[:, :], pbb), ALU.mult)
                    P.copy("pool", (cv[:, ft, 0:2], cv_b[ft]), (cv[:, ft, NB:NB + 2], cv_b[ft]))
                    yield
                w_rel("win0")
                if b == 0:
                    dbg("conv_preT", conv_preT[:], conv_pre_b, [128, 4, NB])

                yield
            phase([bb for bb in P2_BUFS if bb is not tmpB[1][1]])
            gs_, gb_ = s5_steps(), bulk_units()
            alive_s, alive_b = True, True
            import os
            ILV = "a"
            nb_units = 0
            if ILV == "0":
                for _ in gs_:
                    pass
                alive_s = False
            if ILV == "c":
                for _ in range(12):
                    next(gb_)
            while alive_s or alive_b:
                if alive_s:
                    try:
                        next(gs_)
                    except StopIteration:
                        alive_s = False
                for _rep in range(1):
                    if alive_b and not (ILV == "a" and nb_units >= 12 and alive_s):
                        try:
                            next(gb_)
                            nb_units += 1
                        except StopIteration:
                            alive_b = False
            if b == 0:
                dbg("yg", yg_f, yg_b, [128, 2048])
            chk('s5h')
            ys5 = ysT_f.rearrange("p (h j c t) -> p h j c t", h=2, j=2, t=8)
            for j in range(2):
                pt, pb = bank()
                pv = pt[:].bitcast(BF16)
                for t_ in range(8):
                    P.tr((pv[:, t_ * 128:(t_ + 1) * 128], pb), (yg[:, t_, j * 128:(j + 1) * 128], yg_b), (ident[:], cbuf))
                P.copy("act", (ys5[:, :, j, :, :], ysT_b),
                       (pv[:, 0:1024].rearrange("p (t h c) -> p h c t", t=8, h=2), pb))
            if b == 0:
                dbg("ysT", ysT_f, ysT_b, [128, 2048])

            chk('s5')
            chk('conv')
            wt, wb = w_get("glu")
            wgl = wt[:, 0:2048].rearrange("p (kt n) -> p kt n", kt=4)
            for ft in range(4):
                pt, pb = bank()
                for kt in range(4):
                    P.mm((pt[:, :], pb), (wgl[:, kt, ft * 128:(ft + 1) * 128], wb), (ysT[:, kt, :], ysT_b), start=(kt == 0), stop=(kt == 3))
                ta_, tab_ = tmpA[ft % 2]
                P.act((ta_[:], tab_), (pt[:, :], pb), AF.Sigmoid, bias=(glub[:, ft:ft + 1], cbuf))
                P.tt("pool", (ys2T[:, ft, :], ys2_b), (ysT[:, ft, :], ysT_b), (ta_[:], tab_), ALU.mult)
            w_rel("glu")
            if b == 0:
                dbg("ys2T", ys2T[:], ys2_b, [128, 4, NB])

            chk('glu')
            phase([merged_b, mergedT_b])
            for bi, (pname, gch, srcT, srcb) in enumerate((("cp", (5, 6), conv_preT, conv_pre_b),
                                                            ("sp", (7, 8), ys2T, ys2_b),
                                                            ("ap", (9, 10), oT, oT_b))):
                wp_t, wp_b = w_get(pname)
                wpv = wp_t[:, 0:4096].rearrange("p (kt n) -> p kt n", kt=4)
                for gi, gc in enumerate(gch):
                    wg_t, wg_b = w_get("win%d" % gc)
                    wgv = wg_t[:, 0:4096].rearrange("p (kt n) -> p kt n", kt=8)
                    for jl in range(4):
                        j = gi * 4 + jl
                        py, pyb = bank()
                        pg, pgb = bank()
                        for kt in range(4):
                            P.mm((py[:, :], pyb), (wpv[:, kt, j * 128:(j + 1) * 128], wp_b), (srcT[:, kt, :], srcb), start=(kt == 0), stop=(kt == 3))
                        for kt in range(8):
                            P.mm((pg[:, :], pgb), (wgv[:, kt, jl * 128:(jl + 1) * 128], wg_b), (hT[:, kt, :], hT_b), start=(kt == 0), stop=(kt == 7))
                        ta_, tab_ = tmpA[j % 2]
                        P.act((ta_[:], tab_), (pg[:, :], pgb), AF.Sigmoid)
                        if bi == 0:
                            P.tt("dve", (merged[:, j, :], merged_b), (ta_[:], tab_), (py[:, :], pyb), ALU.mult)
                        elif bi == 1:
                            tb_, tbb_ = tmpB[j % 2]
                            P.tt("dve", (tb_, tbb_), (ta_[:], tab_), (py[:, :], pyb), ALU.mult)
                            P.tt("pool", (merged[:, j, :], merged_b), (merged[:, j, :], merged_b), (tb_, tbb_), ALU.add)
                        else:
                            tb_, tbb_ = tmpB[j % 2]
                            P.tt("dve", (tb_, tbb_), (ta_[:], tab_), (py[:, :], pyb), ALU.mult)
                            P.tt("pool", (mergedT[:, j, :], mergedT_b), (merged[:, j, :], merged_b), (tb_, tbb_), ALU.add)
                    w_rel("win%d" % gc)
                w_rel(pname)
            if b == 0:
                dbg("mergedT", mergedT_f, mergedT_b, [128, 4096])

            chk('merge')
            wo = [w_get("wo0"), w_get("wo1")]
            phase([h_tm_b])
            for tt in range(4):
                for nh in range(2):
                    wt, wb = wo[nh]
                    wov = wt[:, 0:4096].rearrange("p (kt n) -> p kt n", kt=8)
                    pt, pb = bank()
                    for j in range(8):
                        P.mm((pt[:, :], pb), (mergedT[:, j, tt * 128:(tt + 1) * 128], mergedT_b), (wov[:, j, :], wb), start=(j == 0), stop=(j == 7))
                    P.tt("dve", (xt[:, tt, nh * 512:(nh + 1) * 512], xb), (xt[:, tt, nh * 512:(nh + 1) * 512], xb), (pt[:, :], pb), ALU.add)
                norm_act(xt, xb, h_tm, h_tm_b, tt)
                if tt >= 1:
                    norm_dve(xt, xb, g2b, h_tm, h_tm_b, tt - 1)
            norm_dve(xt, xb, g2b, h_tm, h_tm_b, 3)
            w_rel("wo0")
            w_rel("wo1")
            if b == 0:
                dbg("x1", xt[:], xb, [128, 4, D])
            chk('wo')
            for tt in range(2):
                norm_transposes(h_tm, h_tm_b, hT, hT_b, tt)

            phase([actT_b])
            for j in range(11):
                wt, wb = w_get("ffn%d" % j)
                wgv = wt[:, 0:2048].rearrange("p (kt n) -> p kt n", kt=8)
                wuv = wt[:, 2048:4096].rearrange("p (kt n) -> p kt n", kt=8)
                fb = [(bank(), bank()) for _sub in range(2)]
                if j == 0:
                    for half in range(2):
                        c0, c1 = half * 256, (half + 1) * 256
                        hb2 = (hT_tt[2 * half], hT_tt[2 * half + 1])
                        for sub in range(2):
                            (pg, pgb), (pu, pub) = fb[sub]
                            for kt in range(8):
                                P.mm((pg[:, c0:c1], pgb), (wgv[:, kt, sub * 128:(sub + 1) * 128], wb), (hT[:, kt, c0:c1], hb2), start=(kt == 0), stop=(kt == 7))
                            for kt in range(8):
                                P.mm((pu[:, c0:c1], pub), (wuv[:, kt, sub * 128:(sub + 1) * 128], wb), (hT[:, kt, c0:c1], hb2), start=(kt == 0), stop=(kt == 7))
                        if half == 0:
                            for tt in range(2, 4):
                                norm_transposes(h_tm, h_tm_b, hT, hT_b, tt)
                else:
                    for sub in range(2):
                        (pg, pgb), (pu, pub) = fb[sub]
                        for kt in range(8):
                            P.mm((pg[:, :], pgb), (wgv[:, kt, sub * 128:(sub + 1) * 128], wb), (hT[:, kt, :], hT_b), start=(kt == 0), stop=(kt == 7))
                        for kt in range(8):
                            P.mm((pu[:, :], pub), (wuv[:, kt, sub * 128:(sub + 1) * 128], wb), (hT[:, kt, :], hT_b), start=(kt == 0), stop=(kt == 7))
                for sub in range(2):
                    f = 2 * j + sub
                    (pg, pgb), (pu, pub) = fb[sub]
                    ta_, tab_ = tmpA[f % 2]
                    P.act((ta_[:], tab_), (pg[:, :], pgb), AF.Silu)
                    P.tt("dve", (actT[:, f, :], actT_b), (ta_[:], tab_), (pu[:, :], pub), ALU.mult)
                w_rel("ffn%d" % j)
            chk('ffn1')
            if b + 1 < NBLK:
                xn, xnb, _, _ = xbuf[(b + 1) % 2]
                phase([h_tm_b])
                for tt in range(4):
                    norm_stats(xn, xnb, g1b, h_tm, h_tm_b, tt)
            accs = [pbank[i] for i in range(8)]
            for r in range(6):
                nk = 4 if r < 5 else 2
                wt, wb = w_get("dn%d" % r)
                wdv = wt[:, 0:nk * 1024].rearrange("p (kt n) -> p kt n", kt=nk)
                for tt in range(4):
                    for nh in range(2):
                        pt, pb = accs[tt * 2 + nh]
                        for k in range(nk):
                            f = r * 4 + k
                            P.mm((pt[:, :], pb), (actT[:, f, tt * 128:(tt + 1) * 128], actT_b), (wdv[:, k, nh * 512:(nh + 1) * 512], wb),
                                 start=(f == 0), stop=(f == 21))
                w_rel("dn%d" % r)
            for tt in range(4):
                for nh in range(2):
                    pt, pb = accs[tt * 2 + nh]
                    P.tt("dve", (xt[:, tt, nh * 512:(nh + 1) * 512], xb), (xt[:, tt, nh * 512:(nh + 1) * 512], xb), (pt[:, :], pb), ALU.add)
            bank_rr[0] = 0
            if b + 1 < NBLK:
                for tt in range(4):
                    norm_transposes(h_tm, h_tm_b, hT, hT_b, tt, eng="act")
            phase([junk_b])
            for tt in range(4):
                P.act((junk[:, tt, :], junk_b), (xt[:, tt, :], xb), AF.Square, accum=(ss[:, tt:tt + 1], ss_b[tt]))
                P.act((rstd[:, tt:tt + 1], rstd_b[tt]), (ss[:, tt:tt + 1], ss_b[tt]), AF.Sqrt, scale=1.0 / D, bias=(epscol[:, 0:1], cbuf))
                P.recip((rstd[:, tt:tt + 1], rstd_b[tt]), (rstd[:, tt:tt + 1], rstd_b[tt]))
                P.stt((xt[:, tt, :], xb), (xt[:, tt, :], xb), (rstd[:, tt:tt + 1], rstd_b[tt]), (gF[:], cbuf), ALU.mult, ALU.mult)
            P.dma("pool", (ov[b], OUTB), (xt[:], xb), xst)

      except _Stop:
        pass
      P.final_wait()
      P.emit()
    return nc, dbg_outs


_CACHE = {}

PARAM_KEYS = ["norm1_g", "w_in", "conv_w", "conv_proj", "ssm_A_re", "ssm_A_im", "ssm_log_dt", "ssm_B_re", "ssm_B_im",
              "ssm_C_re", "ssm_C_im", "ssm_D", "ssm_glu_w", "ssm_glu_b", "ssm_proj", "mem_norm_g", "attn_wk", "attn_wv",
              "attn_proj", "w_o", "norm2_g", "ffn_w_gate", "ffn_w_up", "ffn_w_down", "final_norm_g"]


def _prep_params(inp):
    f = lambda a: np.ascontiguousarray(np.asarray(a, dtype=np.float32))
    p = {}
    p["norm1_g"] = f(inp["norm1_g"]).reshape(1024)
    p["w_in"] = f(inp["w_in"]).reshape(1024, 5632)
    p["conv_w"] = f(inp["conv_w"]).reshape(3, 512)
    p["conv_proj"] = f(inp["conv_proj"]).reshape(512, 1024)
    p["ssm_A_re"] = f(inp["ssm_A_re"]).reshape(32, 64)
    p["ssm_A_im"] = f(inp["ssm_A_im"]).reshape(32, 64)
    p["ssm_log_dt"] = f(inp["ssm_log_dt"]).reshape(1, 32)
    p["ssm_B_re"] = f(inp["ssm_B_re"]).reshape(32, 64, 16)
    p["ssm_B_im"] = f(inp["ssm_B_im"]).reshape(32, 64, 16)
    p["ssm_C_re"] = f(inp["ssm_C_re"]).reshape(512, 64)
    p["ssm_C_im"] = f(inp["ssm_C_im"]).reshape(512, 64)
    p["ssm_D"] = f(inp["ssm_D"]).reshape(512)
    p["ssm_glu_w"] = f(inp["ssm_glu_w"]).reshape(512, 512)
    p["ssm_glu_b"] = f(inp["ssm_glu_b"]).reshape(512)
    p["ssm_proj"] = f(inp["ssm_proj"]).reshape(512, 1024)
    p["mem_norm_g"] = f(inp["mem_norm_g"]).reshape(1024)
    p["attn_wk"] = f(inp["attn_wk"]).reshape(1024, 512)
    p["attn_wv"] = f(inp["attn_wv"]).reshape(1024, 512)
    p["attn_proj"] = f(inp["attn_proj"]).reshape(512, 1024)
    p["w_o"] = f(inp["w_o"]).reshape(1024, 1024)
    p["norm2_g"] = f(inp["norm2_g"]).reshape(1024)
    p["ffn_w_gate"] = f(inp["ffn_w_gate"]).reshape(1024, 2816)
    p["ffn_w_up"] = f(inp["ffn_w_up"]).reshape(1024, 2816)
    p["ffn_w_down"] = f(inp["ffn_w_down"]).reshape(2816, 1024)
    p["final_norm_g"] = f(inp["final_norm_g"]).reshape(1, 1024)
    return p


def kernel(**inputs):
    x = np.asarray(inputs["x"], dtype=np.float32)
    mem = np.asarray(inputs["mem"], dtype=np.float32)
    Bn, T, _ = x.shape
    if T not in _CACHE:
        _CACHE[T] = build(T)[0]
    nc = _CACHE[T]
    p = _prep_params(inputs)
    in_maps = []
    for i in range(Bn):
        m = dict(p)
        m["x"] = np.ascontiguousarray(x[i])
        m["mem"] = np.ascontiguousarray(mem[i])
        in_maps.append(m)
    res = run_bass_kernel_spmd(nc, in_maps, core_ids=list(range(Bn)))
    return np.stack([np.asarray(r["out"], dtype=np.float32).reshape(T, 1024) for r in res.results], axis=0)
```
